# Optimizing a Trainium2 kernel written in Bass

```python
import jax, jax.numpy as jnp
from jax import lax
import numpy as np

D_MODEL = 2048
BATCH = 2
SEQ = 4096
DEPTH = 4

N_MIXERS = 3
D_FF = 4 * D_MODEL
NORM_EPS = 1e-6
ROPE_THETA = 500000.0
ROPE_FRACTION = 4
NEG_INF = -1e30

NSA_HEAD_DIM = 128
NSA_HEADS = D_MODEL // NSA_HEAD_DIM
NSA_KV_GROUPS = 4
NSA_CMP_BLOCK = 32
NSA_CMP_STRIDE = 16
NSA_SEL_BLOCK = 64
NSA_TOP_N = 16
NSA_WINDOW = 512
NSA_WIN_BLOCK = 128
NSA_QUERY_CHUNK = 64
NSA_FORCE_BONUS = 1e4
NSA_IN = (NSA_HEADS + 6 * NSA_KV_GROUPS) * NSA_HEAD_DIM + 3 * NSA_HEADS

GLA_HEADS = 4
GLA_KEY_DIM = D_MODEL // 2 // GLA_HEADS
GLA_VAL_DIM = D_MODEL // GLA_HEADS
GLA_GATE_RANK = 16
GLA_TAU = 16.0
GLA_CHUNK = 64
GLA_IN = GLA_HEADS * (2 * GLA_KEY_DIM + 2 * GLA_VAL_DIM) + GLA_GATE_RANK

SWA_HEAD_DIM = 64
SWA_HEADS = D_MODEL // SWA_HEAD_DIM
SWA_KV_HEADS = 4
SWA_WINDOW = 128
SWA_BLOCK = 128
SWA_IN = (SWA_HEADS + 2 * SWA_KV_HEADS) * SWA_HEAD_DIM

N_A = (DEPTH + N_MIXERS - 1) // N_MIXERS
N_B = (DEPTH + N_MIXERS - 2) // N_MIXERS
N_C = DEPTH // N_MIXERS

kernel_name = "hybrid_nsa_gla_swasink_trunk"


def rms_norm(x, gain):
    x32 = x.astype(jnp.float32)
    y = x32 * lax.rsqrt(jnp.mean(x32 * x32, axis=-1, keepdims=True) + NORM_EPS)
    return (y * gain.astype(jnp.float32)).astype(x.dtype)


def partial_rope(x, pos):
    d = x.shape[-1]
    rot = d // ROPE_FRACTION
    half = rot // 2
    inv_freq = jnp.power(jnp.float32(ROPE_THETA), -jnp.arange(half, dtype=jnp.float32) / half)
    ang = pos.astype(jnp.float32)[:, None] * inv_freq[None, :]
    ang = ang.reshape(ang.shape[:1] + (1,) * (x.ndim - 3) + (half,))
    cos, sin = jnp.cos(ang), jnp.sin(ang)
    x32 = x.astype(jnp.float32)
    x1, x2 = x32[..., :half], x32[..., half:rot]
    out = jnp.concatenate([x1 * cos - x2 * sin, x2 * cos + x1 * sin, x32[..., rot:]], axis=-1)
    return out.astype(x.dtype)


def masked_softmax(s, mask):
    return jax.nn.softmax(jnp.where(mask, s.astype(jnp.float32), NEG_INF), axis=-1)


def split_cols(t, sizes):
    offs = [int(o) for o in np.cumsum(sizes)[:-1]]
    return jnp.split(t, offs, axis=-1)


def banded_blocks(x, blk, n_prev):
    b, s = x.shape[:2]
    nb = s // blk
    xb = x.reshape((b, nb, blk) + x.shape[2:])
    parts = []
    for p in range(n_prev, 0, -1):
        pw = [(0, 0)] * xb.ndim
        pw[1] = (p, 0)
        parts.append(jnp.pad(xb, pw)[:, :nb])
    parts.append(xb)
    return jnp.concatenate(parts, axis=2)


def band_mask(s, blk, n_prev, window):
    nb = s // blk
    qpos = jnp.arange(s).reshape(nb, blk)
    kpos = (jnp.arange(nb)[:, None] - n_prev) * blk + jnp.arange((n_prev + 1) * blk)[None, :]
    qp, kp = qpos[:, :, None], kpos[:, None, :]
    return (kp <= qp) & (kp > qp - window) & (kp >= 0)


def selection_overlap(n_cmp, n_sel):
    c0 = np.arange(n_cmp) * NSA_CMP_STRIDE
    s0 = np.arange(n_sel) * NSA_SEL_BLOCK
    ov = np.minimum(c0[:, None] + NSA_CMP_BLOCK, s0[None, :] + NSA_SEL_BLOCK) - np.maximum(c0[:, None], s0[None, :])
    return jnp.asarray(np.clip(ov, 0, None) / NSA_CMP_BLOCK, dtype=jnp.float32)


def compress_blocks(x, idx, pe, w1, w2):
    blk = x[:, idx] + pe[:, None, :]
    b, n, l, g, d = blk.shape
    flat = blk.transpose(0, 1, 3, 2, 4).reshape(b, n, g, l * d)
    return jax.nn.gelu(flat @ w1) @ w2


def nsa_mixer(h, w_in, w_out, q_norm, k_norm, cmp_pe, cmp_w1, cmp_w2):
    b, s, _ = h.shape
    g, d = NSA_KV_GROUPS, NSA_HEAD_DIM
    hpg = NSA_HEADS // g
    scale = d ** -0.5
    kv = g * d
    q, kc, vc, ks, vs, kw, vw, gate = split_cols(h @ w_in, [NSA_HEADS * d] + [kv] * 6 + [3 * NSA_HEADS])
    pos = jnp.arange(s)
    q = partial_rope(rms_norm(q.reshape(b, s, g, hpg, d), q_norm), pos)
    kc, vc, ks, vs, kw, vw = [t.reshape(b, s, g, d) for t in (kc, vc, ks, vs, kw, vw)]

    n_cmp = (s - NSA_CMP_BLOCK) // NSA_CMP_STRIDE + 1
    cmp_idx = np.arange(n_cmp)[:, None] * NSA_CMP_STRIDE + np.arange(NSA_CMP_BLOCK)[None, :]
    cmp_end = jnp.asarray(cmp_idx[:, -1])
    k_cmp = compress_blocks(kc, cmp_idx, cmp_pe[0], cmp_w1[0], cmp_w2[0])
    k_cmp = partial_rope(rms_norm(k_cmp, k_norm[0]), cmp_end)
    v_cmp = compress_blocks(vc, cmp_idx, cmp_pe[1], cmp_w1[1], cmp_w2[1])
    s_cmp = jnp.einsum('bsghd,bcgd->bghsc', q, k_cmp).astype(jnp.float32) * scale
    has_cmp = (pos >= NSA_CMP_BLOCK - 1).astype(jnp.float32)
    p_cmp = masked_softmax(s_cmp, cmp_end[None, :] <= pos[:, None]) * has_cmp[:, None]
    o_cmp = jnp.einsum('bghsc,bcgd->bsghd', p_cmp.astype(v_cmp.dtype), v_cmp)

    n_sel = s // NSA_SEL_BLOCK
    n_top = min(NSA_TOP_N, n_sel)
    imp = jnp.einsum('bghsc,cn->bgsn', p_cmp, selection_overlap(n_cmp, n_sel))
    blk_q = (pos // NSA_SEL_BLOCK)[:, None]
    j = jnp.arange(n_sel)[None, :]
    forced = (j == 0) | (j == blk_q) | (j == blk_q - 1)
    score = jnp.where(j <= blk_q, imp + jnp.where(forced, NSA_FORCE_BONUS, 0.0), NEG_INF)
    _, sel_idx = lax.top_k(score, n_top)

    ks = partial_rope(rms_norm(ks, k_norm[1]), pos)
    kb = ks.reshape(b, n_sel, NSA_SEL_BLOCK, g, d).transpose(0, 3, 1, 2, 4)
    vb = vs.reshape(b, n_sel, NSA_SEL_BLOCK, g, d).transpose(0, 3, 1, 2, 4)
    qc_len = NSA_QUERY_CHUNK
    nq = s // qc_len
    q_chunks = q.transpose(0, 2, 3, 1, 4).reshape(b, g, hpg, nq, qc_len, d).transpose(3, 0, 1, 2, 4, 5)
    idx_chunks = sel_idx.reshape(b, g, nq, qc_len, n_top).transpose(2, 0, 1, 3, 4)
    pos_chunks = pos.reshape(nq, qc_len)
    gather = jax.vmap(jax.vmap(lambda blocks, ix: blocks[ix]))
    n_keys = n_top * NSA_SEL_BLOCK

    def sel_chunk(args):
        qc, ic, pc = args
        kg = gather(kb, ic).reshape(b, g, qc_len, n_keys, d)
        vg = gather(vb, ic).reshape(b, g, qc_len, n_keys, d)
        kpos = (ic[..., None] * NSA_SEL_BLOCK + jnp.arange(NSA_SEL_BLOCK)).reshape(b, g, qc_len, n_keys)
        mask = (kpos <= pc[None, None, :, None])[:, :, None]
        sc = jnp.einsum('bghqd,bgqkd->bghqk', qc, kg).astype(jnp.float32) * scale
        p = masked_softmax(sc, mask)
        return jnp.einsum('bghqk,bgqkd->bghqd', p.astype(vg.dtype), vg)

    o_sel = lax.map(sel_chunk, (q_chunks, idx_chunks, pos_chunks))
    o_sel = o_sel.transpose(1, 0, 4, 2, 3, 5).reshape(b, s, g, hpg, d)

    kw = partial_rope(rms_norm(kw, k_norm[2]), pos)
    n_prev = NSA_WINDOW // NSA_WIN_BLOCK
    nb = s // NSA_WIN_BLOCK
    kwb = banded_blocks(kw, NSA_WIN_BLOCK, n_prev)
    vwb = banded_blocks(vw, NSA_WIN_BLOCK, n_prev)
    qwb = q.reshape(b, nb, NSA_WIN_BLOCK, g, hpg, d)
    s_win = jnp.einsum('bnqghd,bnkgd->bghnqk', qwb, kwb).astype(jnp.float32) * scale
    p_win = masked_softmax(s_win, band_mask(s, NSA_WIN_BLOCK, n_prev, NSA_WINDOW))
    o_win = jnp.einsum('bghnqk,bnkgd->bnqghd', p_win.astype(vwb.dtype), vwb).reshape(b, s, g, hpg, d)

    gts = jax.nn.sigmoid(gate.astype(jnp.float32)).reshape(b, s, g, hpg, 3)
    o = gts[..., 0:1] * o_cmp + gts[..., 1:2] * o_sel + gts[..., 2:3] * o_win
    return o.astype(h.dtype).reshape(b, s, NSA_HEADS * d) @ w_out


def gla_mixer(h, w_in, w_gate_up, b_gate, o_norm, w_out):
    b, s, _ = h.shape
    nh, dk, dv, c = GLA_HEADS, GLA_KEY_DIM, GLA_VAL_DIM, GLA_CHUNK
    q, k, v, g_lr, r = split_cols(h @ w_in, [nh * dk, nh * dk, nh * dv, GLA_GATE_RANK, nh * dv])
    log_a = jax.nn.log_sigmoid((g_lr @ w_gate_up + b_gate).astype(jnp.float32)) / GLA_TAU
    nc = s // c

    def to_chunks(t, dd):
        return t.astype(jnp.float32).reshape(b, nc, c, nh, dd).transpose(1, 0, 3, 2, 4)

    qs = to_chunks(q, dk) * (dk ** -0.5)
    ks_, vs_, gs = to_chunks(k, dk), to_chunks(v, dv), to_chunks(log_a, dk)
    causal = jnp.tril(jnp.ones((c, c), dtype=bool))[:, :, None]

    def step(state, inp):
        qc, kc, vc, gc = inp
        cum = jnp.cumsum(gc, axis=2)
        o_inter = jnp.einsum('bhcd,bhde->bhce', qc * jnp.exp(cum), state)
        decay = jnp.exp(jnp.where(causal, cum[:, :, :, None, :] - cum[:, :, None, :, :], NEG_INF))
        att = jnp.einsum('bhid,bhjd,bhijd->bhij', qc, kc, decay)
        o_intra = jnp.einsum('bhij,bhje->bhie', att, vc)
        last = cum[:, :, -1:, :]
        state = jnp.exp(last[:, :, 0, :])[..., None] * state + jnp.einsum('bhcd,bhce->bhde', kc * jnp.exp(last - cum), vc)
        return state, o_inter + o_intra

    state0 = jnp.zeros((b, nh, dk, dv), jnp.float32)
    _, o = lax.scan(step, state0, (qs, ks_, vs_, gs))
    o = o.transpose(1, 0, 3, 2, 4).reshape(b, s, nh, dv).astype(h.dtype)
    o = rms_norm(o, o_norm) * jax.nn.silu(r.reshape(b, s, nh, dv))
    return o.reshape(b, s, nh * dv) @ w_out


def swa_sink_mixer(h, w_in, w_out, q_norm, k_norm, sinks):
    b, s, _ = h.shape
    g, d = SWA_KV_HEADS, SWA_HEAD_DIM
    hpg = SWA_HEADS // g
    q, k, v = split_cols(h @ w_in, [SWA_HEADS * d, g * d, g * d])
    pos = jnp.arange(s)
    q = partial_rope(rms_norm(q.reshape(b, s, g, hpg, d), q_norm), pos)
    k = partial_rope(rms_norm(k.reshape(b, s, g, d), k_norm), pos)
    v = v.reshape(b, s, g, d)
    n_prev = 1
    nb = s // SWA_BLOCK
    kb = banded_blocks(k, SWA_BLOCK, n_prev)
    vb = banded_blocks(v, SWA_BLOCK, n_prev)
    qb = q.reshape(b, nb, SWA_BLOCK, g, hpg, d)
    sc = jnp.einsum('bnqghd,bnkgd->bghnqk', qb, kb).astype(jnp.float32) * (d ** -0.5)
    sc = jnp.where(band_mask(s, SWA_BLOCK, n_prev, SWA_WINDOW), sc, NEG_INF)
    sink = jnp.broadcast_to(sinks.astype(jnp.float32).reshape(1, g, hpg, 1, 1, 1), sc.shape[:-1] + (1,))
    p = jax.nn.softmax(jnp.concatenate([sc, sink], axis=-1), axis=-1)[..., :-1]
    o = jnp.einsum('bghnqk,bnkgd->bnqghd', p.astype(vb.dtype), vb)
    return o.reshape(b, s, SWA_HEADS * d).astype(h.dtype) @ w_out


def squared_relu_mlp(h, w_up, w_down):
    return jnp.square(jax.nn.relu(h @ w_up)) @ w_down


def setup_inputs(seed: int = 0) -> dict:
    key = jax.random.key(seed)
    keys = iter(jax.random.split(key, 32))

    def normal(shape, scale):
        return scale * jax.random.normal(next(keys), shape, jnp.float32)

    def gain(shape):
        return jnp.ones(shape, jnp.float32) + normal(shape, 0.05)

    nd = NSA_HEAD_DIM
    return {
        "x": normal((BATCH, SEQ, D_MODEL), 1.0),
        "norm_mix": gain((DEPTH, D_MODEL)),
        "norm_mlp": gain((DEPTH, D_MODEL)),
        "mlp_w_up": normal((DEPTH, D_MODEL, D_FF), D_MODEL ** -0.5),
        "mlp_w_down": normal((DEPTH, D_FF, D_MODEL), D_FF ** -0.5),
        "nsa_w_in": normal((N_A, D_MODEL, NSA_IN), D_MODEL ** -0.5),
        "nsa_w_out": normal((N_A, NSA_HEADS * nd, D_MODEL), (NSA_HEADS * nd) ** -0.5),
        "nsa_q_norm": gain((N_A, nd)),
        "nsa_k_norm": gain((N_A, 3, nd)),
        "nsa_cmp_pe": normal((N_A, 2, NSA_CMP_BLOCK, nd), 0.1),
        "nsa_cmp_w1": normal((N_A, 2, NSA_CMP_BLOCK * nd, nd), (NSA_CMP_BLOCK * nd) ** -0.5),
        "nsa_cmp_w2": normal((N_A, 2, nd, nd), nd ** -0.5),
        "gla_w_in": normal((N_B, D_MODEL, GLA_IN), D_MODEL ** -0.5),
        "gla_w_gate_up": normal((N_B, GLA_GATE_RANK, GLA_HEADS * GLA_KEY_DIM), GLA_GATE_RANK ** -0.5),
        "gla_b_gate": normal((N_B, GLA_HEADS * GLA_KEY_DIM), 0.01),
        "gla_o_norm": gain((N_B, GLA_VAL_DIM)),
        "gla_w_out": normal((N_B, GLA_HEADS * GLA_VAL_DIM, D_MODEL), (GLA_HEADS * GLA_VAL_DIM) ** -0.5),
        "swa_w_in": normal((N_C, D_MODEL, SWA_IN), D_MODEL ** -0.5),
        "swa_w_out": normal((N_C, SWA_HEADS * SWA_HEAD_DIM, D_MODEL), (SWA_HEADS * SWA_HEAD_DIM) ** -0.5),
        "swa_q_norm": gain((N_C, SWA_HEAD_DIM)),
        "swa_k_norm": gain((N_C, SWA_HEAD_DIM)),
        "swa_sinks": normal((N_C, SWA_HEADS), 0.5),
    }


def reference(x, norm_mix, norm_mlp, mlp_w_up, mlp_w_down,
              nsa_w_in, nsa_w_out, nsa_q_norm, nsa_k_norm, nsa_cmp_pe, nsa_cmp_w1, nsa_cmp_w2,
              gla_w_in, gla_w_gate_up, gla_b_gate, gla_o_norm, gla_w_out,
              swa_w_in, swa_w_out, swa_q_norm, swa_k_norm, swa_sinks):
    ia = ib = ic = 0
    for i in range(DEPTH):
        h = rms_norm(x, norm_mix[i])
        kind = i % N_MIXERS
        if kind == 0:
            y = nsa_mixer(h, nsa_w_in[ia], nsa_w_out[ia], nsa_q_norm[ia], nsa_k_norm[ia],
                          nsa_cmp_pe[ia], nsa_cmp_w1[ia], nsa_cmp_w2[ia])
            ia += 1
        elif kind == 1:
            y = gla_mixer(h, gla_w_in[ib], gla_w_gate_up[ib], gla_b_gate[ib], gla_o_norm[ib], gla_w_out[ib])
            ib += 1
        else:
            y = swa_sink_mixer(h, swa_w_in[ic], swa_w_out[ic], swa_q_norm[ic], swa_k_norm[ic], swa_sinks[ic])
            ic += 1
        x = x + y.astype(x.dtype)
        h = rms_norm(x, norm_mlp[i])
        x = x + squared_relu_mlp(h, mlp_w_up[i], mlp_w_down[i]).astype(x.dtype)
    return x
```

```python
import numpy as np
from contextlib import ExitStack
import concourse.bass as bass
import concourse.mybir as mybir
from concourse.bass_utils import run_bass_kernel_spmd

F32 = mybir.dt.float32
BF16 = mybir.dt.bfloat16
ALU = mybir.AluOpType
AF = mybir.ActivationFunctionType
AX = mybir.AxisListType

D_MODEL = 2048
BATCH = 2
SEQ = 4096
DEPTH = 4
D_FF = 4 * D_MODEL
NORM_EPS = 1e-6
NCORES = 8
TOK = BATCH * SEQ // NCORES
KC = D_MODEL // 128

SEM_LIM = 16000
NDMASEM = 8


class Prog:
    ENGS = ("pe", "act", "dve", "pool", "sp")

    def __init__(self, nc, stack):
        self.nc = nc
        self.stack = stack
        self.ops = []
        self.lastw = {}
        self.readers = {}
        self._n = 0
        self.fence = None

    def sb(self, shape, dtype, name=None):
        self._n += 1
        return self.stack.enter_context(self.nc.sbuf_tensor((name + "_s") if name else f"sb{self._n}", list(shape), dtype))

    def ps(self, shape, dtype=F32, name=None):
        self._n += 1
        return self.stack.enter_context(self.nc.psum_tensor(name or f"ps{self._n}", list(shape), dtype))

    def add(self, eng, fn, r=(), w=(), dma=False):
        oid = len(self.ops)
        deps = set()
        for k in r:
            if k in self.lastw:
                deps.add(self.lastw[k])
        for k in w:
            if k in self.lastw:
                deps.add(self.lastw[k])
            deps.update(self.readers.get(k, ()))
        if self.fence is not None:
            deps.add(self.fence)
        for k in r:
            self.readers.setdefault(k, []).append(oid)
        for k in w:
            self.lastw[k] = oid
            self.readers[k] = []
        self.ops.append(dict(eng=eng, fn=fn, deps=deps, dma=dma))
        return oid

    def barrier(self):
        if not hasattr(self, "_fdummy"):
            self._fdummy = self.sb([1, 8], F32, "fence_dummy")
        keys = list(set(self.lastw.keys()) | set(self.readers.keys()))
        d = self._fdummy
        self.fence = None
        self.fence = self.add("pool", lambda e: e.memset(d[:], 0.0), r=(), w=keys)

    def dma(self, eng, out, in_, r=(), w=()):
        return self.add(eng, lambda e: e.dma_start(out=out, in_=in_), r=r, w=w, dma=True)

    def dump(self, name, ap, r):
        d = self.nc.dram_tensor(name, list(ap.shape), ap.dtype, kind="ExternalOutput").ap()
        self.dma("sp", d, ap, r=r, w=[("dump", name)])
        self.dumps = getattr(self, "dumps", []) + [("dump", name)]

    def finish(self, r):
        r = list(r) + getattr(self, "dumps", [])
        self.add("sp", None, r=r, w=())

    def emit(self):
        nc = self.nc
        ops = self.ops
        has_dep = [False] * len(ops)
        for op in ops:
            for d in op["deps"]:
                has_dep[d] = True
        cnt = {e: 0 for e in self.ENGS}
        dcnt = {e: 0 for e in self.ENGS}
        nsem = {}
        for i, op in enumerate(ops):
            e = op["eng"]
            if op["dma"]:
                j = dcnt[e]
                dcnt[e] += 1
                op["sig"] = (("d", e, j % NDMASEM), 16 * (j // NDMASEM + 1), 16)
                op["prev"] = (("d", e, j % NDMASEM), 16 * (j // NDMASEM)) if j >= NDMASEM else None
            elif has_dep[i] and op["fn"] is not None:
                c = cnt[e]
                cnt[e] += 1
                op["sig"] = (("c", e, c // SEM_LIM), c % SEM_LIM + 1, 1)
            else:
                op["sig"] = None
        keys = set()
        for op in ops:
            if op["sig"]:
                keys.add(op["sig"][0])
        sems = {}
        for k in sorted(keys):
            sems[k] = self.stack.enter_context(nc.semaphore("s_" + "_".join(str(x) for x in k)))
        by_eng = {e: [] for e in self.ENGS}
        for i, op in enumerate(ops):
            by_eng[op["eng"]].append(i)
        bname = {"pe": "tensor", "act": "scalar", "dve": "vector", "pool": "gpsimd", "sp": "sync"}
        self.n_wait = 0
        with nc.Block() as block:
            for E in self.ENGS:
                if not by_eng[E]:
                    continue

                def body(eng, E=E):
                    seen = {}
                    for i in by_eng[E]:
                        op = ops[i]
                        waits = {}
                        for d in op["deps"]:
                            dop = ops[d]
                            if dop["sig"] is None:
                                continue
                            if dop["eng"] == E and E == "pe" and not dop["dma"]:
                                continue
                            k, v, _ = dop["sig"]
                            if waits.get(k, 0) < v:
                                waits[k] = v
                        if op["dma"] and op["prev"] is not None:
                            k, v = op["prev"]
                            if waits.get(k, 0) < v:
                                waits[k] = v
                        for k, v in waits.items():
                            if seen.get(k, 0) >= v:
                                continue
                            seen[k] = v
                            eng.wait_ge(sems[k], v)
                            self.n_wait += 1
                        if op["fn"] is None:
                            continue
                        ins = op["fn"](eng)
                        if op["sig"] is not None:
                            k, v, inc = op["sig"]
                            ins.then_inc(sems[k], inc)

                getattr(block, bname[E])(body)


def rms_to_hT(P, xT, gain_sb, hT, ones_bf, psA, psB, scratch_bf, rr, eps_col, tag):
    nc = P.nc
    pss = [psA, psB]
    for kc in range(KC):
        sl = kc % 2
        P.add("act", lambda e, kc=kc, sl=sl: e.activation(out=scratch_bf[sl][:], in_=xT[:, kc, :], func=AF.Square),
              r=[("xT", kc)], w=[("sq", sl)])
        for hf in range(2):
            P.add("pe", lambda e, kc=kc, sl=sl, hf=hf: e.matmul(
                pss[hf][:], lhsT=ones_bf[:], rhs=scratch_bf[sl][:, hf * 512:(hf + 1) * 512],
                start=(kc == 0), stop=(kc == KC - 1)),
                r=[("sq", sl), "ones"], w=[("ps", tag, hf)])
    for hf in range(2):
        P.add("act", lambda e, hf=hf: e.activation(
            out=rr[:, hf * 512:(hf + 1) * 512], in_=pss[hf][:], func=AF.Sqrt, bias=eps_col[:], scale=1.0),
            r=[("ps", tag, hf), "epsc"], w=[("rr", hf)])
        P.add("dve", lambda e, hf=hf: e.reciprocal(out=rr[:, hf * 512:(hf + 1) * 512], in_=rr[:, hf * 512:(hf + 1) * 512]),
              r=[("rr", hf)], w=[("rr", hf)])
    for kc in range(KC):
        P.add("dve", lambda e, kc=kc: e.scalar_tensor_tensor(
            out=hT[:, kc, :], in0=xT[:, kc, :], scalar=gain_sb[:, kc:kc + 1], in1=rr[:],
            op0=ALU.mult, op1=ALU.mult), r=[("xT", kc), ("rr", 0), ("rr", 1), "gain"], w=[("hT", kc)])


def build_phaseA(nch):
    nc = bass.Bass("TRN2", target_bir_lowering=False)
    xT_d = nc.dram_tensor("xT", [128, KC, TOK], F32, kind="ExternalInput").ap()
    gain_d = nc.dram_tensor("gain", [128, KC], F32, kind="ExternalInput").ap()
    w_d = nc.dram_tensor("w", [nch, 128, KC, 128], F32, kind="ExternalInput").ap()
    out_d = nc.dram_tensor("projT", [nch * 128, TOK], F32, kind="ExternalOutput").ap()
    with ExitStack() as stack:
        P = Prog(nc, stack)
        xT = P.sb([128, KC, TOK], F32, "xT_sb")
        hT = P.sb([128, KC, TOK], BF16, "hT_sb")
        gain_sb = P.sb([128, KC], F32, "gain_sb")
        ones_bf = P.sb([128, 128], BF16, "ones_bf")
        sq = [P.sb([128, TOK], BF16, f"sq{i}") for i in range(2)]
        rr = P.sb([128, TOK], F32, "rr")
        NW = 4
        wt = [P.sb([128, KC, 128], BF16, f"wt{i}") for i in range(NW)]
        NO = 3
        ot = [P.sb([128, TOK], F32, f"ot{i}") for i in range(NO)]
        ps = [P.ps([128, 512], F32, f"psb{i}") for i in range(8)]

        P.add("pool", lambda e: e.memset(ones_bf[:], 1.0), w=["ones"])
        eps_col = P.sb([128, 1], F32, "eps_col")
        P.add("pool", lambda e: e.memset(eps_col[:], float(D_MODEL * NORM_EPS)), w=["epsc"])
        P.dma("sp", gain_sb[:], gain_d, w=["gain"])
        for kc in range(KC):
            P.dma("sp", xT[:, kc, :], xT_d[:, kc, :], w=[("xT", kc)])
        P.add("act", lambda e: e.mul(out=gain_sb[:], in_=gain_sb[:], mul=float(np.sqrt(D_MODEL))), r=["gain"], w=["gain"])
        rms_to_hT(P, xT, gain_sb, hT, ones_bf, ps[0], ps[1], sq, rr, eps_col, "n")
        allh = [("hT", kc) for kc in range(KC)]
        for j in range(nch):
            s = j % NW
            P.dma("pool", wt[s][:], w_d[j], w=[("wt", s)])
            pb = (j % 3) * 2 + 2
            for kc in range(KC):
                for hf in range(2):
                    P.add("pe", lambda e, s=s, kc=kc, hf=hf, pb=pb: e.matmul(
                        ps[pb + hf][:], lhsT=wt[s][:, kc, :], rhs=hT[:, kc, hf * 512:(hf + 1) * 512],
                        start=(kc == 0), stop=(kc == KC - 1)),
                        r=[("wt", s)] + (allh if kc == 0 else []), w=[("ps", pb + hf)])
            o = j % NO
            P.add("act", lambda e, o=o, pb=pb: e.copy(out=ot[o][:, 0:512], in_=ps[pb][:]), r=[("ps", pb)], w=[("ot", o, 0)])
            P.add("dve", lambda e, o=o, pb=pb: e.tensor_copy(out=ot[o][:, 512:1024], in_=ps[pb + 1][:]), r=[("ps", pb + 1)], w=[("ot", o, 1)])
            P.dma("sp", out_d[j * 128:(j + 1) * 128, :], ot[o][:], r=[("ot", o, 0), ("ot", o, 1)], w=[("out", j)])
        P.finish([("out", j) for j in range(nch)])
        P.emit()
    return nc


def to_fm(a2d):
    r, c = a2d.shape
    return np.ascontiguousarray(a2d.reshape(r // 128, 128, c).transpose(1, 0, 2))


def prep_w_in(w, nch):
    d, n = w.shape
    wp = np.zeros((d, nch * 128), np.float32)
    wp[:, :n] = w
    return np.ascontiguousarray(wp.reshape(KC, 128, nch, 128).transpose(2, 1, 0, 3))


def run_phaseA(xT_full, gain, w):
    n = w.shape[1]
    nch = (n + 127) // 128
    nc = get_prog(("A", nch), lambda: build_phaseA(nch))
    wl = prep_w_in(w, nch)
    g = np.ascontiguousarray(gain.reshape(KC, 128).T)
    in_maps = []
    for c in range(NCORES):
        in_maps.append({"xT": to_fm(xT_full[:, c * TOK:(c + 1) * TOK]), "gain": g, "w": wl})
    res = run_bass_kernel_spmd(nc, in_maps, core_ids=list(range(NCORES)))
    return np.concatenate([r["projT"] for r in res.results], axis=1)[:n]


FB = 4
NFC = D_FF // 128


def build_phaseC():
    nc = bass.Bass("TRN2", target_bir_lowering=False)
    xT_d = nc.dram_tensor("xT", [128, KC, TOK], F32, kind="ExternalInput").ap()
    oT_d = nc.dram_tensor("oT", [128, KC, TOK], F32, kind="ExternalInput").ap()
    gain_d = nc.dram_tensor("gain", [128, KC], F32, kind="ExternalInput").ap()
    wo_d = nc.dram_tensor("wo", [KC, 128, KC, 128], F32, kind="ExternalInput").ap()
    wu_d = nc.dram_tensor("wu", [NFC, 128, KC, 128], F32, kind="ExternalInput").ap()
    wd_d = nc.dram_tensor("wd", [NFC, 128, D_MODEL], F32, kind="ExternalInput").ap()
    out_d = nc.dram_tensor("xoT", [128, KC, TOK], F32, kind="ExternalOutput").ap()
    with ExitStack() as stack:
        P = Prog(nc, stack)
        xT = P.sb([128, KC, TOK], F32, "xT_sb")
        hT = P.sb([128, KC, TOK], BF16, "hT_sb")
        gain_sb = P.sb([128, KC], F32, "gain_sb")
        ones_bf = P.sb([128, 128], BF16, "ones_bf")
        eps_col = P.sb([128, 1], F32, "eps_col")
        sq = [P.sb([128, TOK], BF16, f"sq{i}") for i in range(2)]
        rr = P.sb([128, TOK], F32, "rr")
        NW = 4
        wt = [P.sb([128, KC, 128], BF16, f"wt{i}") for i in range(NW)]
        wd = [P.sb([128, D_MODEL], BF16, f"wd{i}") for i in range(2 * FB)]
        act = [P.sb([128, TOK], BF16, f"act{i}") for i in range(2 * FB)]
        tmp = [P.sb([128, 512], F32, f"tmp{i}") for i in range(2)]
        ps = [P.ps([128, 512], F32, f"psb{i}") for i in range(8)]

        P.add("pool", lambda e: e.memset(ones_bf[:], 1.0), w=["ones"])
        P.add("pool", lambda e: e.memset(eps_col[:], float(D_MODEL * NORM_EPS)), w=["epsc"])
        P.dma("sp", gain_sb[:], gain_d, w=["gain"])
        for kc in range(KC):
            P.dma("sp", xT[:, kc, :], xT_d[:, kc, :], w=[("xT", kc)])
        for kc in range(KC):
            P.dma("pool", hT[:, kc, :], oT_d[:, kc, :], w=[("hT", kc)])
        P.add("act", lambda e: e.mul(out=gain_sb[:], in_=gain_sb[:], mul=float(np.sqrt(D_MODEL))), r=["gain"], w=["gain"])
        allh = [("hT", kc) for kc in range(KC)]
        wcnt = [0]
        pcnt = [0]

        def wtile(src):
            s = wcnt[0] % NW
            wcnt[0] += 1
            P.dma("pool", wt[s][:], src, w=[("wt", s)])
            return s

        def pbank():
            pb = 2 + (pcnt[0] % 3) * 2
            pcnt[0] += 1
            return pb

        for n in range(KC):
            s = wtile(wo_d[n])
            pb = pbank()
            for kc in range(KC):
                for hf in range(2):
                    P.add("pe", lambda e, s=s, kc=kc, hf=hf, pb=pb: e.matmul(
                        ps[pb + hf][:], lhsT=wt[s][:, kc, :], rhs=hT[:, kc, hf * 512:(hf + 1) * 512],
                        start=(kc == 0), stop=(kc == KC - 1)),
                        r=[("wt", s)] + (allh if kc == 0 else []), w=[("ps", pb + hf)])
            for hf in range(2):
                P.add("dve", lambda e, n=n, hf=hf, pb=pb: e.tensor_tensor(
                    out=xT[:, n, hf * 512:(hf + 1) * 512], in0=xT[:, n, hf * 512:(hf + 1) * 512], in1=ps[pb + hf][:], op=ALU.add),
                    r=[("ps", pb + hf), ("xT", n)], w=[("xT", n)])
        rms_to_hT(P, xT, gain_sb, hT, ones_bf, ps[0], ps[1], sq, rr, eps_col, "n")
        for blk in range(NFC // FB):
            par = blk % 2
            for fl in range(FB):
                f = blk * FB + fl
                s = wtile(wu_d[f])
                a = par * FB + fl
                P.dma("pool", wd[a][:], wd_d[f], w=[("wd", a)])
                pb = pbank()
                for kc in range(KC):
                    for hf in range(2):
                        P.add("pe", lambda e, s=s, kc=kc, hf=hf, pb=pb: e.matmul(
                            ps[pb + hf][:], lhsT=wt[s][:, kc, :], rhs=hT[:, kc, hf * 512:(hf + 1) * 512],
                            start=(kc == 0), stop=(kc == KC - 1)),
                            r=[("wt", s)] + (allh if kc == 0 else []), w=[("ps", pb + hf)])
                for hf in range(2):
                    P.add("act", lambda e, hf=hf, pb=pb: e.activation(out=tmp[hf][:], in_=ps[pb + hf][:], func=AF.Relu),
                          r=[("ps", pb + hf)], w=[("tmp", hf)])
                    P.add("pool", lambda e, hf=hf, a=a: e.tensor_tensor(
                        out=act[a][:, hf * 512:(hf + 1) * 512], in0=tmp[hf][:], in1=tmp[hf][:], op=ALU.mult),
                        r=[("tmp", hf)], w=[("act", a, hf)])
            for n in range(KC):
                pb = pbank()
                for fl in range(FB):
                    a = par * FB + fl
                    for hf in range(2):
                        P.add("pe", lambda e, a=a, n=n, hf=hf, pb=pb, fl=fl: e.matmul(
                            ps[pb + hf][:], lhsT=wd[a][:, n * 128:(n + 1) * 128], rhs=act[a][:, hf * 512:(hf + 1) * 512],
                            start=(fl == 0), stop=(fl == FB - 1)),
                            r=[("wd", a), ("act", a, hf)], w=[("ps", pb + hf)])
                for hf in range(2):
                    P.add("dve", lambda e, n=n, hf=hf, pb=pb: e.tensor_tensor(
                        out=xT[:, n, hf * 512:(hf + 1) * 512], in0=xT[:, n, hf * 512:(hf + 1) * 512], in1=ps[pb + hf][:], op=ALU.add),
                        r=[("ps", pb + hf), ("xT", n)], w=[("xT", n)])
        for kc in range(KC):
            P.dma("sp", out_d[:, kc, :], xT[:, kc, :], r=[("xT", kc)], w=[("out", kc)])
        P.finish([("out", kc) for kc in range(KC)])
        P.emit()
    return nc


_PROG_CACHE = {}


def get_prog(key, builder):
    if key not in _PROG_CACHE:
        _PROG_CACHE[key] = builder()
    return _PROG_CACHE[key]


def run_phaseC(xT_full, oT_full, w_out, gain, w_up, w_down, trace=False):
    nc = get_prog("C", build_phaseC)
    wo = prep_w_in(w_out, KC)
    wu = prep_w_in(w_up, NFC)
    wdl = np.ascontiguousarray(w_down.reshape(NFC, 128, D_MODEL))
    g = np.ascontiguousarray(gain.reshape(KC, 128).T)
    in_maps = []
    for c in range(NCORES):
        in_maps.append({"xT": to_fm(xT_full[:, c * TOK:(c + 1) * TOK]), "oT": to_fm(oT_full[:, c * TOK:(c + 1) * TOK]),
                        "gain": g, "wo": wo, "wu": wu, "wd": wdl})
    res = run_bass_kernel_spmd(nc, in_maps, core_ids=list(range(NCORES)), trace=trace)
    if trace:
        print("phaseC exec_time_ns", res.exec_time_ns)
    outs = [r["xoT"].transpose(1, 0, 2).reshape(D_MODEL, TOK) for r in res.results]
    return np.concatenate(outs, axis=1)


NEGB = -30000.0
ROPE_THETA = 500000.0


def rope_tables(d, pos, reps):
    rot = d // 4
    half = rot // 2
    inv = np.power(np.float32(ROPE_THETA), -np.arange(half, dtype=np.float32) / np.float32(half)).astype(np.float32)
    ang = pos.astype(np.float32)[None, :] * inv[:, None]
    c = np.ones((d, len(pos)), np.float32)
    s = np.zeros((d, len(pos)), np.float32)
    c[:half] = np.cos(ang)
    c[half:rot] = np.cos(ang)
    s[:half] = np.sin(ang)
    s[half:rot] = np.sin(ang)
    return np.tile(c, (reps, 1)), np.tile(s, (reps, 1))


def rope_matrix(d, reps):
    rot = d // 4
    half = rot // 2
    m = np.zeros((reps * d, reps * d), np.float32)
    for r in range(reps):
        o = r * d
        for i in range(half):
            m[o + i + half, o + i] = -1.0
            m[o + i, o + i + half] = 1.0
    return m


def mask_biases():
    kl = np.arange(128)[:, None]
    ql = np.arange(128)[None, :]
    diag = np.where(kl <= ql, 0.0, NEGB).astype(np.float32)
    prev = np.where(kl > ql, 0.0, NEGB).astype(np.float32)
    return np.tile(diag, (1, 4)), np.tile(prev, (1, 4))


def qk_norm_rope(P, src, dst_bf, gain_col, ones_blk, R_sb, cosf, sinf, sq, rr, tmpf, psbig, eps_col, inv_d, key, ntok):
    cols = [(c0, min(512, ntok - c0)) for c0 in range(0, ntok, 512)]
    ng = (len(cols) + 3) // 4
    gw = [sum(n for (_, n) in cols[4 * g:4 * g + 4]) for g in range(ng)]
    P.add("act", lambda e: e.activation(out=sq[:, :ntok], in_=src, func=AF.Square), r=[key], w=["sq"])
    for c, (c0, n) in enumerate(cols):
        P.add("pe", lambda e, c=c, c0=c0, n=n: e.matmul(psbig[c // 4][:, (c % 4) * 512:(c % 4) * 512 + n], lhsT=ones_blk[:],
                                                        rhs=sq[:, c0:c0 + n], start=True, stop=True),
              r=["sq", "onesblk"], w=[("psbig", c // 4)])
    for g in range(ng):
        n = gw[g]
        P.add("act", lambda e, g=g, n=n: e.activation(out=rr[:, g * 2048:g * 2048 + n], in_=psbig[g][:, :n], func=AF.Sqrt,
                                                      bias=eps_col[:], scale=inv_d),
              r=["epsc"], w=[("psbig", g), ("rr", g)])
        P.add("dve", lambda e, g=g, n=n: e.reciprocal(out=rr[:, g * 2048:g * 2048 + n], in_=rr[:, g * 2048:g * 2048 + n]),
              r=[("rr", g)], w=[("rr", g)])
    rrk = [("rr", g) for g in range(ng)]
    P.add("dve", lambda e: e.scalar_tensor_tensor(out=src, in0=src, scalar=gain_col, in1=rr[:, :ntok], op0=ALU.mult, op1=ALU.mult),
          r=[key, "gains"] + rrk, w=[key])
    for c, (c0, n) in enumerate(cols):
        P.add("pe", lambda e, c=c, c0=c0, n=n: e.matmul(psbig[c // 4][:, (c % 4) * 512:(c % 4) * 512 + n], lhsT=R_sb[:],
                                                        rhs=src[:, c0:c0 + n], start=True, stop=True),
              r=[key, "Rm"], w=[("psbig", c // 4)])
    for g in range(ng):
        n = gw[g]
        P.add("dve", lambda e, g=g, n=n: e.tensor_tensor(out=tmpf[:, g * 2048:g * 2048 + n], in0=psbig[g][:, :n],
                                                         in1=sinf[:, g * 2048:g * 2048 + n], op=ALU.mult),
              r=["tabs"], w=[("psbig", g), ("tmpf", g)])
    P.add("pool", lambda e: e.tensor_tensor(out=src, in0=src, in1=cosf[:, :ntok], op=ALU.mult), r=[key, "tabs"], w=[key])
    P.add("pool", lambda e: e.tensor_tensor(out=dst_bf, in0=src, in1=tmpf[:, :ntok], op=ALU.add),
          r=[key] + [("tmpf", g) for g in range(ng)], w=[key + "_bf"])


def build_swaB():
    S = SEQ
    nc = bass.Bass("TRN2", target_bir_lowering=False)
    q_d = nc.dram_tensor("q", [4, 128, S], F32, kind="ExternalInput").ap()
    k_d = nc.dram_tensor("k2", [128, S], F32, kind="ExternalInput").ap()
    v_d = nc.dram_tensor("v", [128, 32, 64], F32, kind="ExternalInput").ap()
    gq_d = nc.dram_tensor("gq", [128, 1], F32, kind="ExternalInput").ap()
    gk_d = nc.dram_tensor("gk", [128, 1], F32, kind="ExternalInput").ap()
    es_d = nc.dram_tensor("esink", [1, 2, 512], F32, kind="ExternalInput").ap()
    cos_d = nc.dram_tensor("cosf", [128, S], F32, kind="ExternalInput").ap()
    sin_d = nc.dram_tensor("sinf", [128, S], F32, kind="ExternalInput").ap()
    R_d = nc.dram_tensor("Rm", [128, 128], F32, kind="ExternalInput").ap()
    id_d = nc.dram_tensor("ident", [128, 128], F32, kind="ExternalInput").ap()
    bd_d = nc.dram_tensor("bdiag", [128, 512], F32, kind="ExternalInput").ap()
    bp_d = nc.dram_tensor("bprev", [128, 512], F32, kind="ExternalInput").ap()
    ob_d = nc.dram_tensor("onesblk", [128, 128], F32, kind="ExternalInput").ap()
    out_d = nc.dram_tensor("oT", [512, S], F32, kind="ExternalOutput").ap()
    with ExitStack() as stack:
        P = Prog(nc, stack)
        work = [P.sb([128, S], F32, f"work{i}") for i in range(2)]
        qbf = P.sb([128, 4, S], BF16, "qbf")
        kbf = P.sb([128, S], BF16, "kbf")
        vbf = P.sb([128, 32, 64], BF16, "vbf")
        cosf = P.sb([128, S], F32, "cosf")
        sinf = P.sb([128, S], F32, "sinf")
        rr = P.sb([128, S], F32, "rr")
        tmpf = P.sb([128, S], F32, "tmpf")
        sq = P.sb([128, S], BF16, "sq")
        R_sb = P.sb([128, 128], F32, "R_sb")
        ident = P.sb([128, 128], BF16, "ident")
        bdiag = P.sb([128, 512], BF16, "bdiag")
        bprev = P.sb([128, 512], BF16, "bprev")
        onesblk = P.sb([128, 128], BF16, "onesblk")
        ones64 = P.sb([128, 64], BF16, "ones64")
        ones1 = P.sb([1, 64], BF16, "ones1")
        esink = P.sb([1, 2, 512], BF16, "esink")
        gq = P.sb([128, 1], F32, "gq")
        gk = P.sb([128, 1], F32, "gk")
        eps_col = P.sb([128, 1], F32, "eps_col")
        pt = [P.sb([128, 512], BF16, f"pt{i}") for i in range(4)]
        rden = [P.sb([64, 512], F32, f"rden{i}") for i in range(2)]
        ost = [P.sb([64, 4, 512], F32, f"ost{i}") for i in range(2)]
        psbig = [P.ps([128, 2048], F32, f"psbig{i}") for i in range(2)]

        P.add("pool", lambda e: e.memset(eps_col[:], float(NORM_EPS)), w=["epsc"])
        P.add("pool", lambda e: e.memset(ones64[:], 1.0), w=["ones64"])
        P.add("pool", lambda e: e.memset(ones1[:], 1.0), w=["ones1"])
        P.dma("sp", cosf[:], cos_d, w=["tabs"])
        P.dma("sp", sinf[:], sin_d, w=["tabs"])
        P.dma("sp", R_sb[:], R_d, w=["Rm"])
        P.dma("sp", gq[:], gq_d, w=["gains"])
        P.dma("sp", gk[:], gk_d, w=["gains"])
        P.dma("pool", ident[:], id_d, w=["ident"])
        P.dma("pool", bdiag[:], bd_d, w=["bias"])
        P.dma("pool", bprev[:], bp_d, w=["bias"])
        P.dma("pool", onesblk[:], ob_d, w=["onesblk"])
        esf = P.sb([1, 2, 512], F32, "esf")
        P.dma("sp", esf[:], es_d, w=["esf"])
        P.add("act", lambda e: e.activation(out=esink[:], in_=esf[:], func=AF.Exp), r=["esf"], w=["esink"])
        P.dma("pool", vbf[:], v_d, w=["v"])
        P.dma("sp", work[0][:], k_d, w=["w0"])
        qk_norm_rope(P, work[0][:], kbf[:], gk[:, 0:1], onesblk, R_sb, cosf, sinf, sq, rr, tmpf, psbig, eps_col, 1.0 / 64, "w0", S)
        for p in range(4):
            wk = (p + 1) % 2
            P.dma("sp", work[wk][:], q_d[p], w=[f"w{wk}"])
            qk_norm_rope(P, work[wk][:], qbf[:, p, :], gq[:, 0:1], onesblk, R_sb, cosf, sinf, sq, rr, tmpf, psbig, eps_col, 1.0 / 64,
                         f"w{wk}", S)
        qkeys = ["w0_bf", "w1_bf"]
        u = 0
        outv = out_d.rearrange("(p e d) t -> e d p t", p=4, e=2, d=64)
        for qg in range(SEQ // 512):
            for e in range(2):
                os_ = ost[(qg * 2 + e) % 2]
                for qi in range(4):
                    qt = qg * 4 + qi
                    kts = [kt for kt in (qt - 1, qt) if kt >= 0]
                    oset = u % 2
                    u += 1
                    ops_ = psbig[1][0:64, oset * 1024:oset * 1024 + 512]
                    dps_ = psbig[1][0:64, oset * 1024 + 512:oset * 1024 + 1024]
                    okey = ("ops", oset)
                    for i, kt in enumerate(kts):
                        sb_ = (u * 2 + i) % 4
                        sps = psbig[0][:, sb_ * 512:(sb_ + 1) * 512]
                        bias = bdiag if kt == qt else bprev
                        P.add("pe", lambda e_, e=e, kt=kt, qt=qt, sps=sps: e_.matmul(
                            sps, lhsT=kbf[64 * e:64 * e + 64, kt * 128:(kt + 1) * 128],
                            rhs=qbf[64 * e:64 * e + 64, :, qt * 128:(qt + 1) * 128], start=True, stop=False),
                            r=qkeys, w=[("sps", sb_)])
                        P.add("pe", lambda e_, sps=sps, bias=bias: e_.matmul(sps, lhsT=ident[:], rhs=bias[:], start=False, stop=True),
                              r=["ident", "bias"], w=[("sps", sb_)])
                        P.add("act", lambda e_, sps=sps, sb_=sb_: e_.activation(out=pt[sb_][:], in_=sps, func=AF.Exp, scale=0.125),
                              r=[("sps", sb_)], w=[("pt", sb_)])
                        P.add("pe", lambda e_, kt=kt, sb_=sb_, ops_=ops_, i=i: e_.matmul(
                            ops_, lhsT=vbf[:, kt, :], rhs=pt[sb_][:], start=(i == 0), stop=(i == len(kts) - 1)),
                            r=[("pt", sb_), "v"], w=[okey])
                        P.add("pe", lambda e_, sb_=sb_, dps_=dps_, i=i: e_.matmul(
                            dps_, lhsT=ones64[:], rhs=pt[sb_][:], start=(i == 0), stop=False),
                            r=[("pt", sb_), "ones64"], w=[okey])
                    P.add("pe", lambda e_, dps_=dps_, e=e: e_.matmul(dps_, lhsT=ones1[:], rhs=esink[:, e, :], start=False, stop=True),
                          r=["ones1", "esink"], w=[okey])
                    P.add("dve", lambda e_, dps_=dps_, oset=oset: e_.reciprocal(out=rden[oset][:], in_=dps_), r=[okey], w=[("rden", oset)])
                    P.add("dve", lambda e_, ops_=ops_, oset=oset, os_=os_, qi=qi: e_.tensor_tensor(
                        out=os_[:, :, qi * 128:(qi + 1) * 128], in0=ops_.rearrange("d (p q) -> d p q", p=4),
                        in1=rden[oset][:].rearrange("d (p q) -> d p q", p=4), op=ALU.mult),
                        r=[okey, ("rden", oset)], w=[("ost", (qg * 2 + e) % 2)])
                P.dma("sp", outv[e, :, :, qg * 512:(qg + 1) * 512], os_[:], r=[("ost", (qg * 2 + e) % 2)], w=[("out", qg, e)])
        P.finish([("out", qg, e) for qg in range(SEQ // 512) for e in range(2)])
        P.emit()
    return nc


def run_swaB(projT, q_norm, k_norm, sinks, trace=False):
    S = SEQ
    nc = get_prog("swaB", build_swaB)
    cosf, sinf = rope_tables(64, np.arange(S), 2)
    Rm = rope_matrix(64, 2)
    ident = np.eye(128, dtype=np.float32)
    bdiag, bprev = mask_biases()
    onesblk = np.kron(np.eye(2, dtype=np.float32), np.ones((64, 64), np.float32))
    gq = np.tile(q_norm.astype(np.float32), 2).reshape(128, 1)
    gk = np.tile(k_norm.astype(np.float32), 2).reshape(128, 1)
    in_maps = []
    for c in range(NCORES):
        b, g = c // 4, c % 4
        t0 = b * S
        q = np.ascontiguousarray(projT[g * 512:(g + 1) * 512, t0:t0 + S].reshape(4, 128, S))
        k = projT[2048 + g * 64:2048 + (g + 1) * 64, t0:t0 + S]
        k2 = np.ascontiguousarray(np.concatenate([k, k], axis=0))
        v = projT[2304 + g * 64:2304 + (g + 1) * 64, t0:t0 + S].T
        v = np.ascontiguousarray(v.reshape(32, 128, 64).transpose(1, 0, 2))
        sk = sinks[g * 8:(g + 1) * 8].astype(np.float32).reshape(4, 2)
        es = np.ascontiguousarray(np.repeat(sk.T[:, :, None], 128, axis=2).reshape(1, 2, 512))
        in_maps.append(dict(q=q, k2=k2, v=v, gq=gq, gk=gk, esink=es, cosf=cosf, sinf=sinf, Rm=Rm, ident=ident,
                            bdiag=bdiag, bprev=bprev, onesblk=onesblk))
    res = run_bass_kernel_spmd(nc, in_maps, core_ids=list(range(NCORES)), trace=trace)
    if trace:
        print("swaB exec_time_ns", res.exec_time_ns)
    oT = np.zeros((D_MODEL, BATCH * S), np.float32)
    for c in range(NCORES):
        b, g = c // 4, c % 4
        oT[g * 512:(g + 1) * 512, b * S:(b + 1) * S] = res.results[c]["oT"]
    return oT


def gla_consts():
    j = np.arange(128)[:, None]
    i = np.arange(128)[None, :]
    same = (j // 64) == (i // 64)
    T2 = np.where(same & (j <= i), -1.0 / 16.0, 0.0).astype(np.float32)
    U2 = np.where(same & (j > i), -1.0 / 16.0, 0.0).astype(np.float32)
    M2 = np.where(same & (j <= i), 1.0, 0.0).astype(np.float32)
    return T2, U2, M2


def build_glaB(dbg=False):
    S = SEQ
    NT = S // 128
    nc = bass.Bass("TRN2", target_bir_lowering=False)
    glr_d = nc.dram_tensor("glrT", [16, S], F32, kind="ExternalInput").ap()
    q_d = nc.dram_tensor("qT", [128, 2, S], F32, kind="ExternalInput").ap()
    k_d = nc.dram_tensor("kT", [128, 2, S], F32, kind="ExternalInput").ap()
    ktm_d = nc.dram_tensor("ktm", [NT, 128, 256], F32, kind="ExternalInput").ap()
    v_d = nc.dram_tensor("vtm", [NT, 128, 512], F32, kind="ExternalInput").ap()
    r_d = nc.dram_tensor("rT", [128, 4, S], F32, kind="ExternalInput").ap()
    wg_d = nc.dram_tensor("wg", [16, 256], F32, kind="ExternalInput").ap()
    bg_d = nc.dram_tensor("bg", [1, 256], F32, kind="ExternalInput").ap()
    gn_d = nc.dram_tensor("gn", [128, 4], F32, kind="ExternalInput").ap()
    T2_d = nc.dram_tensor("T2", [128, 128], F32, kind="ExternalInput").ap()
    U2_d = nc.dram_tensor("U2", [128, 128], F32, kind="ExternalInput").ap()
    M2_d = nc.dram_tensor("M2", [128, 128], F32, kind="ExternalInput").ap()
    out_d = nc.dram_tensor("oT", [512, S], F32, kind="ExternalOutput").ap()
    with ExitStack() as stack:
        P = Prog(nc, stack)
        qp = P.sb([128, 2, S], BF16, "qp")
        kp = P.sb([128, 2, S], BF16, "kp")
        kpp = P.sb([128, NT, 256], BF16, "kpp")
        vbf = P.sb([128, NT, 512], BF16, "vbf")
        att = P.sb([128, NT, 128], BF16, "att")
        explast = P.sb([128, 2, 2 * NT], F32, "explast")
        glr = P.sb([16, S], F32, "glr")
        wg = P.sb([16, 256], F32, "wg")
        bg = P.sb([1, 256], F32, "bg")
        gn = P.sb([128, 4], F32, "gn")
        T2 = P.sb([128, 128], F32, "T2")
        U2 = P.sb([128, 128], F32, "U2")
        M2 = P.sb([128, 128], F32, "M2")
        ones1 = P.sb([1, 128], F32, "ones1")
        onesb = P.sb([128, 128], BF16, "onesb")
        eps_col = P.sb([128, 1], F32, "eps_col")
        qt_ = [P.sb([128, 2, 128], F32, f"qt{i}") for i in range(2)]
        kt_ = [P.sb([128, 2, 128], F32, f"kt{i}") for i in range(2)]
        ktm = [P.sb([128, 256], F32, f"ktm{i}") for i in range(2)]
        e1 = [P.sb([128, 256], F32, f"e1{i}") for i in range(2)]
        la = [P.sb([128, 256], F32, f"la{i}") for i in range(2)]
        ET = [P.sb([128, 2, 128], F32, f"ET{i}") for i in range(2)]
        EinvT = [P.sb([128, 2, 128], F32, f"EinvT{i}") for i in range(2)]
        Elmc = [P.sb([128, 256], F32, f"Elmc{i}") for i in range(2)]
        Sst = P.sb([128, 2, 512], F32, "Sst")
        Sbf = [P.sb([128, 2, 512], BF16, f"Sbf{i}") for i in range(2)]
        ot = [P.sb([128, 4, 128], F32, f"ot{i}") for i in range(2)]
        osq = [P.sb([128, 4, 128], BF16, f"osq{i}") for i in range(2)]
        rs = [P.sb([128, 128], F32, f"rs{i}") for i in range(2)]
        rt = [P.sb([128, 4, 128], F32, f"rt{i}") for i in range(2)]
        ost = [P.sb([128, 4, 512], F32, f"ost{i}") for i in range(2)]
        ps = [P.ps([128, 512], F32, f"psb{i}") for i in range(8)]

        P.add("pool", lambda e: e.memset(eps_col[:], float(NORM_EPS)), w=["epsc"])
        P.add("pool", lambda e: e.memset(ones1[:], 1.0), w=["ones1"])
        P.add("pool", lambda e: e.memset(onesb[:], 1.0), w=["onesb"])
        P.add("pool", lambda e: e.memset(Sst[:], 0.0), w=["S"])
        for t_, d_, k_ in ((glr, glr_d, "glr"), (wg, wg_d, "wg"), (bg, bg_d, "bg"), (gn, gn_d, "gn"), (T2, T2_d, "T2"),
                           (U2, U2_d, "U2"), (M2, M2_d, "M2")):
            P.dma("sp", t_[:], d_, w=[k_])

        def pass1(t):
            b = t % 2
            tok = slice(t * 128, (t + 1) * 128)
            P.dma("sp", qt_[b][:], q_d[:, :, tok], w=[("qt", b)])
            P.dma("sp", kt_[b][:], k_d[:, :, tok], w=[("kt", b)])
            P.dma("sp", ktm[b][:], ktm_d[t], w=[("ktm", b)])
            P.dma("pool", vbf[:, t, :], v_d[t], w=[("v", t)])
            pA = ps[2 * b]
            pB = ps[2 * b + 1]
            P.add("pe", lambda e: e.matmul(pA[:, 0:256], lhsT=glr[:, tok], rhs=wg[:], start=True, stop=False),
                  r=["glr", "wg"], w=[("pA", b)])
            P.add("pe", lambda e: e.matmul(pA[:, 0:256], lhsT=ones1[:], rhs=bg[:], start=False, stop=True),
                  r=["ones1", "bg"], w=[("pA", b)])
            P.add("act", lambda e: e.activation(out=e1[b][:], in_=pA[:, 0:256], func=AF.Exp, scale=-1.0), r=[("pA", b)], w=[("e1", b)])
            P.add("act", lambda e: e.activation(out=la[b][:], in_=e1[b][:], func=AF.Ln, bias=1.0), r=[("e1", b)], w=[("la", b)])
            for dc in range(2):
                P.add("pe", lambda e, dc=dc: e.matmul(pA[:, 256 + dc * 128:256 + (dc + 1) * 128], lhsT=la[b][:, dc * 128:(dc + 1) * 128],
                                                      rhs=T2[:], start=True, stop=True), r=[("la", b), "T2"], w=[("pA2", b)])
            P.add("pe", lambda e: e.matmul(pB[:, 0:256], lhsT=U2[:], rhs=la[b][:], start=True, stop=True), r=[("la", b), "U2"], w=[("pB", b)])
            cumT = pA[:, 256:512].rearrange("p (c i) -> p c i", c=2)
            P.add("act", lambda e: e.activation(out=ET[b][:], in_=cumT, func=AF.Exp), r=[("pA2", b)], w=[("ET", b)])
            P.add("act", lambda e: e.activation(out=EinvT[b][:], in_=cumT, func=AF.Exp, scale=-1.0), r=[("pA2", b)], w=[("EinvT", b)])
            P.add("act", lambda e: e.activation(out=Elmc[b][:], in_=pB[:, 0:256], func=AF.Exp), w=[("pB", b), ("Elmc", b)])
            P.add("dve", lambda e: e.scalar_tensor_tensor(out=qp[:, :, tok], in0=qt_[b][:], scalar=float(256 ** -0.5), in1=ET[b][:],
                                                          op0=ALU.mult, op1=ALU.mult), r=[("qt", b), ("ET", b)], w=[("qp", t)])
            P.add("dve", lambda e: e.tensor_tensor(out=kp[:, :, tok], in0=kt_[b][:], in1=EinvT[b][:], op=ALU.mult),
                  r=[("kt", b), ("EinvT", b)], w=[("kp", t)])
            P.add("pool", lambda e: e.tensor_tensor(out=kpp[:, t, :], in0=ktm[b][:], in1=Elmc[b][:], op=ALU.mult),
                  r=[("ktm", b), ("Elmc", b)], w=[("kpp", t)])
            P.add("pool", lambda e: e.tensor_copy(out=explast[:, :, 2 * t:2 * t + 2], in_=ET[b][:, :, 63:128:64]),
                  r=[("ET", b)], w=[("explast", t)])
            for dc in range(2):
                P.add("pe", lambda e, dc=dc: e.matmul(pB[:, 256:384], lhsT=kp[:, dc, tok], rhs=qp[:, dc, tok], start=(dc == 0), stop=(dc == 1)),
                      r=[("kp", t), ("qp", t)], w=[("pB", b)])
            P.add("dve", lambda e: e.tensor_tensor(out=att[:, t, :], in0=pB[:, 256:384], in1=M2[:], op=ALU.mult),
                  r=["M2"], w=[("pB", b), ("att", t)])

        sidx = [0]

        def pass2(t):
            b = t % 2
            tok0 = t * 128
            pO = ps[6]
            pN = ps[7]
            P.dma("sp", rt[b][:], r_d[:, :, tok0:tok0 + 128], w=[("rt", b)])
            for dvc in range(4):
                P.add("pe", lambda e, dvc=dvc: e.matmul(pO[:, dvc * 128:(dvc + 1) * 128], lhsT=vbf[:, t, dvc * 128:(dvc + 1) * 128],
                                                        rhs=att[:, t, :], start=(dvc == 0), stop=False),
                      r=[("v", t), ("att", t)], w=["pO"])
            for c in range(2):
                ch = 2 * t + c
                cs = slice(tok0 + 64 * c, tok0 + 64 * c + 64)
                if ch > 0:
                    sb_ = Sbf[sidx[0] % 2]
                    sk = ("Sbf", sidx[0] % 2)
                    for dvc in range(4):
                        for dc in range(2):
                            P.add("pe", lambda e, dvc=dvc, dc=dc, sb_=sb_, c=c, cs=cs: e.matmul(
                                pO[:, dvc * 128 + 64 * c:dvc * 128 + 64 * c + 64], lhsT=sb_[:, dc, dvc * 128:(dvc + 1) * 128],
                                rhs=qp[:, dc, cs], start=False, stop=(dc == 1 and c == 1)),
                                r=[sk, ("qp", t)], w=["pO"])
                for dc in range(2):
                    pk = ps[4 + dc]
                    P.add("pe", lambda e, dc=dc, pk=pk, c=c: e.matmul(pk[:], lhsT=kpp[64 * c:64 * c + 64, t, dc * 128:(dc + 1) * 128],
                                                                 rhs=vbf[64 * c:64 * c + 64, t, :], start=True, stop=True),
                          r=[("kpp", t), ("v", t)], w=[("pk", dc)])
                sidx[0] += 1
                sb_ = Sbf[sidx[0] % 2]
                sk = ("Sbf", sidx[0] % 2)
                for dc in range(2):
                    pk = ps[4 + dc]
                    P.add("dve", lambda e, dc=dc, pk=pk, ch=ch: e.scalar_tensor_tensor(
                        out=Sst[:, dc, :], in0=Sst[:, dc, :], scalar=explast[:, dc, ch:ch + 1], in1=pk[:], op0=ALU.mult, op1=ALU.add),
                        r=[("pk", dc), ("explast", t), "S"], w=["S"])
                P.add("act", lambda e, sb_=sb_: e.copy(out=sb_[:], in_=Sst[:]), r=["S"], w=[sk])
                if dbg and t == 0 and c == 0:
                    P.dump("d_S0", Sst[:], ["S"])
                    P.dump("d_Sbf0", sb_[:], [sk])
            P.add("act", lambda e: e.copy(out=ot[b][:], in_=pO[:].rearrange("p (c i) -> p c i", c=4)), r=["pO"], w=[("ot", b)])
            if dbg and t == 0:
                P.dump("d_ot", ot[0][:], [("ot", 0)])
                P.dump("d_S", Sst[:], ["S"])
            P.add("act", lambda e: e.activation(out=osq[b][:], in_=ot[b][:], func=AF.Square), r=[("ot", b)], w=[("osq", b)])
            for dvc in range(4):
                P.add("pe", lambda e, dvc=dvc: e.matmul(pN[:, 0:128], lhsT=onesb[:], rhs=osq[b][:, dvc, :], start=(dvc == 0), stop=(dvc == 3)),
                      r=[("osq", b), "onesb"], w=["pN"])
            P.add("act", lambda e: e.activation(out=rs[b][:], in_=pN[:, 0:128], func=AF.Sqrt, bias=eps_col[:], scale=1.0 / 512),
                  r=["pN", "epsc"], w=[("rs", b)])
            P.add("dve", lambda e: e.reciprocal(out=rs[b][:], in_=rs[b][:]), r=[("rs", b)], w=[("rs", b)])
            P.add("act", lambda e: e.activation(out=rt[b][:], in_=rt[b][:], func=AF.Silu), r=[("rt", b)], w=[("rt", b)])
            o4 = ost[(t // 4) % 2]
            for dvc in range(4):
                P.add("dve", lambda e, dvc=dvc: e.scalar_tensor_tensor(out=ot[b][:, dvc, :], in0=ot[b][:, dvc, :], scalar=gn[:, dvc:dvc + 1],
                                                                       in1=rs[b][:], op0=ALU.mult, op1=ALU.mult),
                      r=[("ot", b), ("rs", b), "gn"], w=[("ot", b)])
            P.add("pool", lambda e: e.tensor_tensor(out=o4[:, :, (t % 4) * 128:(t % 4 + 1) * 128], in0=ot[b][:], in1=rt[b][:], op=ALU.mult),
                  r=[("ot", b), ("rt", b)], w=[("ost", (t // 4) % 2)])
            if t % 4 == 3:
                g4 = t // 4
                P.dma("sp", out_d.rearrange("(c p) t -> p c t", p=128)[:, :, g4 * 512:(g4 + 1) * 512], o4[:],
                      r=[("ost", g4 % 2)], w=[("out", g4)])

        pass1(0)
        if dbg:
            P.dump("d_la", la[0][:], [("la", 0)])
            P.dump("d_ET", ET[0][:], [("ET", 0)])
            P.dump("d_Elmc", Elmc[0][:], [("Elmc", 0)])
            P.dump("d_att", att[:, 0, :], [("att", 0)])
            P.dump("d_qp", qp[:, :, 0:128], [("qp", 0)])
            P.dump("d_kp", kp[:, :, 0:128], [("kp", 0)])
            P.dump("d_kpp", kpp[:, 0, :], [("kpp", 0)])
            P.dump("d_explast", explast[:, :, 0:2], [("explast", 0)])
        pass1(1)
        for t in range(NT):
            if t + 2 < NT:
                pass1(t + 2)
            pass2(t)
        P.finish([("out", g4) for g4 in range(NT // 4)])
        P.emit()
    return nc


def run_glaB(projT, w_gate_up, b_gate, o_norm, trace=False):
    S = SEQ
    nc = get_prog("glaB", build_glaB)
    T2, U2, M2 = gla_consts()
    gn = np.ascontiguousarray(o_norm.astype(np.float32).reshape(4, 128).T)
    in_maps = []
    for c in range(NCORES):
        b, hd = c // 4, c % 4
        ts = slice(b * S, (b + 1) * S)
        qT = to_fm(projT[hd * 256:(hd + 1) * 256, ts])
        kT_ = projT[1024 + hd * 256:1024 + (hd + 1) * 256, ts]
        kT = to_fm(kT_)
        ktm = np.ascontiguousarray(kT_.T.reshape(S // 128, 128, 256))
        vtm = np.ascontiguousarray(projT[2048 + hd * 512:2048 + (hd + 1) * 512, ts].T.reshape(S // 128, 128, 512))
        glrT = np.ascontiguousarray(projT[4096:4112, ts])
        rT = to_fm(projT[4112 + hd * 512:4112 + (hd + 1) * 512, ts])
        wg = np.ascontiguousarray(w_gate_up[:, hd * 256:(hd + 1) * 256])
        bg = np.ascontiguousarray(b_gate[hd * 256:(hd + 1) * 256].reshape(1, 256))
        in_maps.append(dict(glrT=glrT, qT=qT, kT=kT, ktm=ktm, vtm=vtm, rT=rT, wg=wg, bg=bg, gn=gn, T2=T2, U2=U2, M2=M2))
    res = run_bass_kernel_spmd(nc, in_maps, core_ids=list(range(NCORES)), trace=trace)
    if trace:
        print("glaB exec_time_ns", res.exec_time_ns)
    oT = np.zeros((D_MODEL, BATCH * S), np.float32)
    for c in range(NCORES):
        b, hd = c // 4, c % 4
        oT[hd * 512:(hd + 1) * 512, b * S:(b + 1) * S] = res.results[c]["oT"]
    return oT


NSA_NT = SEQ // 128
NSA_NCMP = (SEQ - 32) // 16 + 1


def nsa_consts():
    S = SEQ
    NT = NSA_NT
    cm = np.zeros((128, 48, 128), np.float32)
    ql = np.arange(128)[None, :]
    cl = np.arange(128)[:, None]
    for qt in range(NT):
        for ct in range(2):
            if ct == 1 and qt < 16:
                continue
            idx = qt if ct == 0 else 32 + qt - 16
            c = cl + 128 * ct
            vis = (16 * c + 31 <= 128 * qt + ql) & (c < NSA_NCMP)
            cm[:, idx, :] = np.where(vis, 0.0, NEGB)
    bonus = np.zeros((128, NT, 64), np.float32)
    j = np.arange(64)[None, :]
    for qt in range(NT):
        pos = 128 * qt + np.arange(128)[:, None]
        bq = pos // 64
        forced = (j == 0) | (j == bq) | (j == bq - 1)
        bonus[:, qt, :] = np.where(j <= bq, np.where(forced, 1e4, 0.0), -1e30)
    E = np.zeros((64, NT, 128), np.float32)
    for kt in range(NT):
        E[2 * kt, kt, :64] = 1.0
        E[2 * kt + 1, kt, 64:] = 1.0
    c0 = np.arange(256) * 16
    s0 = np.arange(64) * 64
    ov = np.minimum(c0[:, None] + 32, s0[None, :] + 64) - np.maximum(c0[:, None], s0[None, :])
    ov = (np.clip(ov, 0, None) / 32.0).astype(np.float32)
    ov[NSA_NCMP:] = 0.0
    ov = np.ascontiguousarray(ov.reshape(2, 128, 64).transpose(1, 0, 2))
    return cm, bonus, E, ov


def build_nsaB(dbg=False):
    S = SEQ
    NT = NSA_NT
    SCALE = float(128 ** -0.5)
    nc = bass.Bass("TRN2", target_bir_lowering=False)

    def din(name, shape):
        return nc.dram_tensor(name, list(shape), F32, kind="ExternalInput").ap()

    q_d = din("q", [4, 128, S])
    kc_d = din("kc", [128, S])
    vc_d = din("vc", [128, S])
    ks_d = din("ks", [128, S])
    kw_d = din("kw", [128, S])
    vs_d = din("vs", [128, NT, 128])
    vw_d = din("vw", [128, NT, 128])
    gate_d = din("gate", [12, S])
    gq_d = din("gq", [128, 1])
    gk_d = din("gk", [128, 3])
    pe_d = din("peT", [2, 128, 32])
    w1_d = din("w1", [2, 128, 32, 128])
    w2_d = din("w2", [2, 128, 128])
    cos_d = din("cosf", [128, S])
    sin_d = din("sinf", [128, S])
    cosc_d = din("cosc", [128, 256])
    sinc_d = din("sinc", [128, 256])
    R_d = din("Rm", [128, 128])
    id_d = din("ident", [128, 128])
    bd_d = din("bdiag", [128, 512])
    bp_d = din("bprev", [128, 512])
    cm_d = din("cmask", [128, 48, 128])
    bon_d = din("bonus", [128, NT, 64])
    E_d = din("Emat", [64, NT, 128])
    ov_d = din("ov", [128, 2, 64])
    out_d = nc.dram_tensor("oT", [512, S], F32, kind="ExternalOutput").ap()
    with ExitStack() as stack:
        P = Prog(nc, stack)
        qbf = P.sb([128, 4, S], BF16, "qbf")
        ksbf = P.sb([128, S], BF16, "ksbf")
        kwbf = P.sb([128, S], BF16, "kwbf")
        vsb = P.sb([128, NT, 128], BF16, "vsb")
        vwb = P.sb([128, NT, 128], BF16, "vwb")
        kcm = P.sb([128, 256], BF16, "kcm")
        vcm = P.sb([128, 2, 128], BF16, "vcm")
        R_sb = P.sb([128, 128], F32, "R_sb")
        identf = P.sb([128, 128], F32, "identf")
        ident = P.sb([128, 128], BF16, "identb")
        bdiag = P.sb([128, 512], BF16, "bdiag")
        bprev = P.sb([128, 512], BF16, "bprev")
        cmask = P.sb([128, 48, 128], BF16, "cmask")
        bonus = P.sb([128, NT, 64], F32, "bonus")
        Emat = P.sb([64, NT, 128], BF16, "Emat")
        ov = P.sb([128, 2, 64], BF16, "ov")
        onesb = P.sb([128, 128], BF16, "onesb")
        gq = P.sb([128, 1], F32, "gq")
        gk = P.sb([128, 3], F32, "gk")
        eps_col = P.sb([128, 1], F32, "eps_col")
        region = P.sb([128, 12288], F32, "region")
        psbig = [P.ps([128, 2048], F32, f"psbig{i}") for i in range(2)]

        def bank(i):
            return psbig[i // 4][:, (i % 4) * 512:(i % 4 + 1) * 512]

        roff = [0]

        def rsb(shape, dtype):
            exact = int(np.prod(shape[1:])) * (2 if dtype == BF16 else 4)
            assert exact % 4 == 0
            nbytes = (exact + 31) // 32 * 32
            a = roff[0] // 4
            v = region[0:shape[0], a:a + exact // 4]
            roff[0] += nbytes
            assert roff[0] <= 12288 * 4, roff[0]
            if dtype == BF16:
                v = v.bitcast(BF16)
            if len(shape) == 3:
                v = v.rearrange("p (a b) -> p a b", a=shape[1])
            return v

        P.add("pool", lambda e: e.memset(eps_col[:], float(NORM_EPS)), w=["epsc"])
        P.add("pool", lambda e: e.memset(onesb[:], 1.0), w=["onesblk"])
        for t_, d_, k_ in ((R_sb, R_d, "Rm"), (identf, id_d, "identf"), (gq, gq_d, "gains"), (gk, gk_d, "gains"), (bonus, bon_d, "bonus")):
            P.dma("sp", t_[:], d_, w=[k_])
        for t_, d_, k_ in ((ident, id_d, "ident"), (bdiag, bd_d, "bias"), (bprev, bp_d, "bias"), (cmask, cm_d, "cmask"),
                           (Emat, E_d, "Emat"), (ov, ov_d, "ov"), (vsb, vs_d, "vs"), (vwb, vw_d, "vw")):
            P.dma("pool", t_[:], d_, w=[k_])

        CH = 1024
        work = [rsb([128, CH], F32) for _ in range(2)]
        tcos = [rsb([128, CH], F32) for _ in range(2)]
        tsin = [rsb([128, CH], F32) for _ in range(2)]
        rr = rsb([128, CH], F32)
        tmpf = rsb([128, CH], F32)
        sq = rsb([128, CH], BF16)
        jobs = [(q_d[h], gq[:, 0:1], lambda c0, h=h: qbf[:, h, c0:c0 + CH]) for h in range(4)]
        jobs.append((ks_d, gk[:, 1:2], lambda c0: ksbf[:, c0:c0 + CH]))
        jobs.append((kw_d, gk[:, 2:3], lambda c0: kwbf[:, c0:c0 + CH]))
        n1 = 0
        for (src_d, gcol, dstf) in jobs:
            for ci in range(S // CH):
                w_ = n1 % 2
                n1 += 1
                c0 = ci * CH
                P.dma("sp", work[w_], src_d[:, c0:c0 + CH], w=[f"w{w_}"])
                P.dma("sp", tcos[w_], cos_d[:, c0:c0 + CH], w=["tabs"])
                P.dma("sp", tsin[w_], sin_d[:, c0:c0 + CH], w=["tabs"])
                qk_norm_rope(P, work[w_], dstf(c0), gcol, onesb, R_sb, tcos[w_], tsin[w_], sq, rr, tmpf, psbig, eps_col, 1.0 / 128,
                             f"w{w_}", CH)
        P.barrier()

        roff[0] = 0
        kcbf = rsb([128, S], BF16)
        vcbf = rsb([128, S], BF16)
        w1b = [rsb([128, 32, 128], BF16) for _ in range(2)]
        w2b = [rsb([128, 128], BF16) for _ in range(2)]
        peb = [rsb([128, 32], BF16) for _ in range(2)]
        ccol = [rsb([128, 1], F32) for _ in range(2)]
        xg = rsb([128, 256], F32)
        x2 = rsb([128, 256], F32)
        th = rsb([128, 256], F32)
        gel = rsb([128, 256], BF16)
        kcmf = rsb([128, 256], F32)
        tcc = rsb([128, 256], F32)
        tsc = rsb([128, 256], F32)
        rr2 = rsb([128, 256], F32)
        tmp2 = rsb([128, 256], F32)
        sq2 = rsb([128, 256], BF16)
        P.dma("pool", kcbf, kc_d, w=["kcbf"])
        P.dma("pool", vcbf, vc_d, w=["vcbf"])
        P.dma("sp", tcc, cosc_d, w=["tabs2"])
        P.dma("sp", tsc, sinc_d, w=["tabs2"])
        for i in range(2):
            P.dma("pool", w1b[i], w1_d[i], w=[("w1", i)])
            P.dma("pool", w2b[i], w2_d[i], w=[("w2", i)])
            P.dma("pool", peb[i], pe_d[i], w=[("pe", i)])
        for i, srcbf, skey in ((0, kcbf, "kcbf"), (1, vcbf, "vcbf")):
            pc = bank(0)
            pv = bank(1)
            for l in range(32):
                P.add("pe", lambda e, i=i, l=l, pc=pc: e.matmul(pc[:, 0:1], lhsT=w1b[i][:, l, :], rhs=peb[i][:, l:l + 1],
                                                             start=(l == 0), stop=(l == 31)),
                      r=[("w1", i), ("pe", i)], w=["pc"])
            P.add("act", lambda e, i=i, pc=pc: e.copy(out=ccol[i], in_=pc[:, 0:1]), r=[], w=["pc", ("ccol", i)])
            for l in range(32):
                P.add("pe", lambda e, i=i, l=l, pv=pv, srcbf=srcbf: e.matmul(
                    pv[:, 0:NSA_NCMP], lhsT=w1b[i][:, l, :], rhs=srcbf[:, l:l + 16 * (NSA_NCMP - 1) + 1:16],
                    start=(l == 0), stop=(l == 31)), r=[("w1", i), skey], w=["pv"])
            P.add("pool", lambda e: e.memset(xg, 0.0), w=["xg"])
            P.add("act", lambda e, i=i, pv=pv: e.activation(out=xg[:, 0:NSA_NCMP], in_=pv[:, 0:NSA_NCMP], func=AF.Identity, bias=ccol[i]),
                  r=[("ccol", i)], w=["pv", "xg"])
            P.add("pool", lambda e: e.tensor_tensor(out=x2, in0=xg, in1=xg, op=ALU.mult), r=["xg"], w=["x2"])
            P.add("pool", lambda e: e.tensor_scalar(out=x2, in0=x2, scalar1=0.044715, scalar2=1.0, op0=ALU.mult, op1=ALU.add),
                  r=["x2"], w=["x2"])
            P.add("pool", lambda e: e.tensor_tensor(out=x2, in0=x2, in1=xg, op=ALU.mult), r=["x2", "xg"], w=["x2"])
            P.add("act", lambda e: e.activation(out=th, in_=x2, func=AF.Tanh, scale=float(np.sqrt(2.0 / np.pi))), r=["x2"], w=["th"])
            P.add("pool", lambda e: e.tensor_scalar(out=th, in0=th, scalar1=1.0, scalar2=0.5, op0=ALU.add, op1=ALU.mult), r=["th"], w=["th"])
            P.add("pool", lambda e: e.tensor_tensor(out=gel, in0=th, in1=xg, op=ALU.mult), r=["th", "xg"], w=["gel"])
            if i == 0:
                pk2 = bank(2)
                P.add("pe", lambda e, pk2=pk2: e.matmul(pk2[:, 0:256], lhsT=w2b[0], rhs=gel, start=True, stop=True),
                      r=[("w2", 0), "gel"], w=["pk2"])
                P.add("act", lambda e, pk2=pk2: e.copy(out=kcmf, in_=pk2[:, 0:256]), r=[], w=["pk2", "kcmf"])
                qk_norm_rope(P, kcmf, kcm[:], gk[:, 0:1], onesb, R_sb, tcc, tsc, sq2, rr2, tmp2, [psbig[1]], eps_col, 1.0 / 128, "kcmf", 256)
            else:
                pk2 = bank(3)
                for ct in range(2):
                    P.add("pe", lambda e, ct=ct, pk2=pk2: e.matmul(pk2[:, ct * 128:(ct + 1) * 128], lhsT=gel[:, ct * 128:(ct + 1) * 128],
                                                                   rhs=w2b[1], start=(ct == 0), stop=True),
                          r=[("w2", 1), "gel"], w=["pk3"])
                P.add("act", lambda e, pk2=pk2: e.copy(out=vcm[:], in_=pk2[:, 0:256].rearrange("p (a b) -> p a b", a=2)),
                      r=[], w=["pk3", "vcm"])
        if dbg:
            P.dump("d_kcm", kcm[:], ["kcmf_bf"])
            P.dump("d_vcm", vcm[:], ["vcm"])
            P.dump("d_qbf", qbf[:, :, 0:256], ["w0_bf", "w1_bf"])
        P.barrier()

        roff[0] = 0
        pt = [rsb([128, 512], BF16) for _ in range(4)]
        G = [rsb([128, 12, 128], F32) for _ in range(2)]
        rden = [rsb([128, 512], F32) for _ in range(2)]
        fac = [rsb([128, 512], F32) for _ in range(2)]
        oft = [rsb([128, 512], F32) for _ in range(2)]
        ostage = [rsb([128, 4, 512], F32) for _ in range(2)]
        tmpU = rsb([64, 512], F32)
        impT = rsb([64, 128], F32)
        score = rsb([128, 64], F32)
        wk = rsb([128, 64], F32)
        m1 = rsb([128, 8], F32)
        m2 = rsb([128, 8], F32)
        negb = rsb([128, 64], F32)
        selT = [rsb([64, 128], BF16) for _ in range(2)]
        tiny = rsb([128, 1], F32)
        P.add("pool", lambda e: e.memset(tiny, 1e-30), w=["tiny"])
        cnt = dict(s=0, pt=0, o=0)
        qall = ["w0_bf", "w1_bf"]

        def branch(qt, kts, kfn, vfn, maskfn, extra=None, extra_key=None):
            oset = cnt["o"] % 2
            cnt["o"] += 1
            ob, db = 4 + 2 * oset, 5 + 2 * oset
            okey = ("oset", oset)
            nk = len(kts)
            for i, kt in enumerate(kts):
                sl = cnt["s"] % 2
                cnt["s"] += 1
                sps = bank(sl)
                skey = ("sps", sl)
                p_ = cnt["pt"] % 4
                cnt["pt"] += 1
                masks = maskfn(kt)
                kl, kr = kfn(kt)
                P.add("pe", lambda e, sps=sps, kl=kl, qt=qt, masks=masks: e.matmul(
                    sps, lhsT=kl, rhs=qbf[:, :, qt * 128:(qt + 1) * 128], start=True, stop=(len(masks) == 0)),
                    r=qall + kr, w=[skey])
                for mi, (ml, mr, mk) in enumerate(masks):
                    P.add("pe", lambda e, sps=sps, ml=ml, mr=mr, mi=mi, masks=masks: e.matmul(
                        sps, lhsT=ml, rhs=mr, start=False, stop=(mi == len(masks) - 1)), r=mk, w=[skey])
                P.add("act", lambda e, sps=sps, p_=p_: e.activation(out=pt[p_], in_=sps, func=AF.Exp, scale=SCALE),
                      r=[], w=[skey, ("pt", p_)])
                vl, vr = vfn(kt)
                P.add("pe", lambda e, ob=ob, vl=vl, p_=p_, i=i, nk=nk: e.matmul(bank(ob), lhsT=vl, rhs=pt[p_], start=(i == 0), stop=(i == nk - 1)),
                      r=[("pt", p_)] + vr, w=[okey])
                P.add("pe", lambda e, db=db, p_=p_, i=i, nk=nk: e.matmul(bank(db), lhsT=onesb[:], rhs=pt[p_], start=(i == 0), stop=(i == nk - 1)),
                      r=[("pt", p_), "onesblk"], w=[okey])
                if extra is not None:
                    extra(kt, i, nk, p_)
            return ob, db, okey, oset

        def combine(qt, br, ob, db, okey, oset, add_tiny):
            st = ostage[(qt // 4) % 2]
            stkey = ("ostage", (qt // 4) % 2)
            gb = qt % 2
            dst = st[:, :, (qt % 4) * 128:(qt % 4 + 1) * 128]
            if add_tiny:
                P.add("dve", lambda e, db=db, oset=oset: e.tensor_scalar(out=rden[oset], in0=bank(db), scalar1=tiny[:, 0:1], scalar2=None, op0=ALU.add),
                      r=["tiny"], w=[okey, ("rden", oset)])
                P.add("dve", lambda e, oset=oset: e.reciprocal(out=rden[oset], in_=rden[oset]), r=[], w=[("rden", oset)])
            else:
                P.add("dve", lambda e, db=db, oset=oset: e.reciprocal(out=rden[oset], in_=bank(db)), r=[], w=[okey, ("rden", oset)])
            P.add("pool", lambda e, oset=oset, gb=gb, br=br: e.tensor_tensor(
                out=fac[oset].rearrange("p (h q) -> p h q", h=4), in0=rden[oset].rearrange("p (h q) -> p h q", h=4),
                in1=G[gb].rearrange("p (h b) q -> p h b q", b=3)[:, :, br, :], op=ALU.mult),
                r=[("rden", oset), ("G", gb)], w=[("fac", oset)])
            if br == 0:
                P.add("dve", lambda e, ob=ob, oset=oset, dst=dst: e.tensor_tensor(
                    out=dst, in0=bank(ob).rearrange("p (h q) -> p h q", h=4), in1=fac[oset].rearrange("p (h q) -> p h q", h=4), op=ALU.mult),
                    r=[("fac", oset)], w=[okey, stkey])
            else:
                P.add("dve", lambda e, ob=ob, oset=oset: e.tensor_tensor(out=oft[oset], in0=bank(ob), in1=fac[oset], op=ALU.mult),
                      r=[("fac", oset)], w=[okey, ("oft", oset)])
                P.add("pool", lambda e, oset=oset, dst=dst: e.tensor_tensor(
                    out=dst, in0=dst, in1=oft[oset].rearrange("p (h q) -> p h q", h=4), op=ALU.add),
                    r=[("oft", oset)], w=[stkey])

        for qt in range(NT):
            gb = qt % 2
            tok = slice(qt * 128, (qt + 1) * 128)
            P.dma("sp", G[gb], gate_d[:, tok].partition_broadcast(128), w=[("G", gb)])
            P.add("act", lambda e, gb=gb: e.activation(out=G[gb], in_=G[gb], func=AF.Sigmoid), r=[], w=[("G", gb)])
            cts = [0] if qt < 16 else [0, 1]
            ub = bank(3)

            def cmask_fn(ct, qt=qt):
                idx = qt if ct == 0 else 32 + qt - 16
                return [(ident[:], cmask[:, idx, :].unsqueeze(1).to_broadcast([128, 4, 128]), ["ident", "cmask"])]

            def uextra(ct, i, nk, p_, ub=ub):
                P.add("pe", lambda e, ct=ct, i=i, nk=nk, p_=p_: e.matmul(ub[0:64, :], lhsT=ov[:, ct, :], rhs=pt[p_], start=(i == 0), stop=(i == nk - 1)),
                      r=[("pt", p_), "ov"], w=["ub"])

            ob, db, okey, oset = branch(qt, cts, lambda ct: (kcm[:, ct * 128:(ct + 1) * 128], ["kcmf_bf"]),
                                        lambda ct: (vcm[:, ct, :], ["vcm"]), cmask_fn, extra=uextra)
            combine(qt, 0, ob, db, okey, oset, True)
            P.add("dve", lambda e, ub=ub, oset=oset: e.tensor_tensor(out=tmpU, in0=ub[0:64, :], in1=rden[oset][0:64, :], op=ALU.mult),
                  r=[("rden", oset)], w=["ub", "tmpU"])
            P.add("dve", lambda e: e.tensor_reduce(out=impT, in_=tmpU.rearrange("p (h q) -> p q h", h=4), axis=AX.X, op=ALU.add),
                  r=["tmpU"], w=["impT"])
            tb = bank(2)
            P.add("pe", lambda e, tb=tb: e.transpose(tb[:, 0:64], impT, identf[0:64, 0:64]), r=["impT", "identf"], w=["tb"])
            P.add("dve", lambda e, tb=tb, qt=qt: e.tensor_tensor(out=score, in0=tb[:, 0:64], in1=bonus[:, qt, :], op=ALU.add),
                  r=["bonus"], w=["tb", "score"])
            P.add("dve", lambda e: e.max(out=m1, in_=score), r=["score"], w=["m1"])
            P.add("dve", lambda e: e.match_replace(out=wk, in_to_replace=m1, in_values=score, imm_value=-3.0e38), r=["score", "m1"], w=["wk"])
            P.add("dve", lambda e: e.max(out=m2, in_=wk), r=["wk"], w=["m2"])
            P.add("dve", lambda e: e.tensor_scalar(out=negb, in0=score, scalar1=m2[:, 7:8], scalar2=None, op0=ALU.is_ge), r=["score", "m2"], w=["negb"])
            P.add("dve", lambda e: e.tensor_scalar(out=negb, in0=negb, scalar1=1.0, scalar2=-NEGB, op0=ALU.subtract, op1=ALU.mult),
                  r=["negb"], w=["negb"])
            P.add("pe", lambda e, tb=tb: e.transpose(tb[0:64, 128:256], negb, identf[:]), r=["negb", "identf"], w=["tb"])
            sT = selT[qt % 2]
            P.add("act", lambda e, tb=tb, sT=sT: e.copy(out=sT, in_=tb[0:64, 128:256]), r=[], w=["tb", ("selT", qt % 2)])
            if dbg and qt in (3, 20):
                P.dump(f"d_score{qt}", score, ["score"])
                P.dump(f"d_negb{qt}", negb, ["negb"])
                P.dump(f"d_selT{qt}", sT, [("selT", qt % 2)])

            def sel_mask(kt, qt=qt, sT=sT):
                ms = [(Emat[:, kt, :], sT.unsqueeze(1).to_broadcast([64, 4, 128]), ["Emat", ("selT", qt % 2)])]
                if kt == qt:
                    ms.append((ident[:], bdiag[:], ["ident", "bias"]))
                return ms

            ob, db, okey, oset = branch(qt, list(range(qt + 1)), lambda kt: (ksbf[:, kt * 128:(kt + 1) * 128], ["w0_bf", "w1_bf"]),
                                        lambda kt: (vsb[:, kt, :], ["vs"]), sel_mask)
            combine(qt, 1, ob, db, okey, oset, False)

            def win_mask(kt, qt=qt):
                if kt == qt:
                    return [(ident[:], bdiag[:], ["ident", "bias"])]
                if kt == qt - 4:
                    return [(ident[:], bprev[:], ["ident", "bias"])]
                return []

            ob, db, okey, oset = branch(qt, [kt for kt in range(qt - 4, qt + 1) if kt >= 0],
                                        lambda kt: (kwbf[:, kt * 128:(kt + 1) * 128], ["w0_bf", "w1_bf"]),
                                        lambda kt: (vwb[:, kt, :], ["vw"]), win_mask)
            combine(qt, 2, ob, db, okey, oset, False)
            if qt % 4 == 3:
                g4 = qt // 4
                P.dma("sp", out_d.rearrange("(h d) t -> d h t", h=4)[:, :, g4 * 512:(g4 + 1) * 512], ostage[g4 % 2],
                      r=[("ostage", g4 % 2)], w=[("out", g4)])
        P.finish([("out", g4) for g4 in range(NT // 4)])
        P.emit()
    return nc


def run_nsaB(projT, q_norm, k_norm, cmp_pe, cmp_w1, cmp_w2, trace=False, dbg=False, cores=None):
    S = SEQ
    NT = NSA_NT
    nc = get_prog(("nsaB", dbg), lambda: build_nsaB(dbg))
    cosf, sinf = rope_tables(128, np.arange(S), 1)
    cend = np.arange(256) * 16 + 31
    cosc, sinc = rope_tables(128, cend, 1)
    Rm = rope_matrix(128, 1)
    ident = np.eye(128, dtype=np.float32)
    bdiag, bprev = mask_biases()
    cm, bonus, E, ov = nsa_consts()
    gq = q_norm.astype(np.float32).reshape(128, 1)
    gk = np.ascontiguousarray(k_norm.astype(np.float32).T)
    peT = np.ascontiguousarray(cmp_pe.transpose(0, 2, 1))
    w1 = np.ascontiguousarray(cmp_w1.reshape(2, 32, 128, 128).transpose(0, 2, 1, 3))
    w2 = np.ascontiguousarray(cmp_w2)
    in_maps = []
    clist = list(range(NCORES)) if cores is None else cores
    for c in clist:
        b, g = c // 4, c % 4
        ts = slice(b * S, (b + 1) * S)

        def rows(base):
            return np.ascontiguousarray(projT[base + g * 128:base + (g + 1) * 128, ts])

        def tm(base):
            return np.ascontiguousarray(projT[base + g * 128:base + (g + 1) * 128, ts].T.reshape(NT, 128, 128).transpose(1, 0, 2))

        q = np.ascontiguousarray(projT[g * 512:(g + 1) * 512, ts].reshape(4, 128, S))
        gate = np.ascontiguousarray(projT[5120 + g * 12:5120 + (g + 1) * 12, ts])
        in_maps.append(dict(q=q, kc=rows(2048), vc=rows(2560), ks=rows(3072), vs=tm(3584), kw=rows(4096), vw=tm(4608), gate=gate,
                            gq=gq, gk=gk, peT=peT, w1=w1, w2=w2, cosf=cosf, sinf=sinf, cosc=cosc, sinc=sinc, Rm=Rm, ident=ident,
                            bdiag=bdiag, bprev=bprev, cmask=cm, bonus=bonus, Emat=E, ov=ov))
    res = run_bass_kernel_spmd(nc, in_maps, core_ids=list(range(len(clist))), trace=trace)
    if trace:
        print("nsaB exec_time_ns", res.exec_time_ns)
    if dbg:
        return res.results
    oT = np.zeros((D_MODEL, BATCH * S), np.float32)
    for i, c in enumerate(clist):
        b, g = c // 4, c % 4
        oT[g * 512:(g + 1) * 512, b * S:(b + 1) * S] = res.results[i]["oT"]
    return oT


def kernel(x, norm_mix, norm_mlp, mlp_w_up, mlp_w_down,
           nsa_w_in, nsa_w_out, nsa_q_norm, nsa_k_norm, nsa_cmp_pe, nsa_cmp_w1, nsa_cmp_w2,
           gla_w_in, gla_w_gate_up, gla_b_gate, gla_o_norm, gla_w_out,
           swa_w_in, swa_w_out, swa_q_norm, swa_k_norm, swa_sinks):
    f = lambda a: np.asarray(a, dtype=np.float32)
    x = f(x)
    xT = np.ascontiguousarray(x.reshape(BATCH * SEQ, D_MODEL).T)
    ia = ib = ic = 0
    for i in range(DEPTH):
        kind = i % 3
        if kind == 0:
            projT = run_phaseA(xT, f(norm_mix[i]), f(nsa_w_in[ia]))
            oT = run_nsaB(projT, f(nsa_q_norm[ia]), f(nsa_k_norm[ia]), f(nsa_cmp_pe[ia]), f(nsa_cmp_w1[ia]), f(nsa_cmp_w2[ia]))
            w_out = f(nsa_w_out[ia])
            ia += 1
        elif kind == 1:
            projT = run_phaseA(xT, f(norm_mix[i]), f(gla_w_in[ib]))
            oT = run_glaB(projT, f(gla_w_gate_up[ib]), f(gla_b_gate[ib]), f(gla_o_norm[ib]))
            w_out = f(gla_w_out[ib])
            ib += 1
        else:
            projT = run_phaseA(xT, f(norm_mix[i]), f(swa_w_in[ic]))
            oT = run_swaB(projT, f(swa_q_norm[ic]), f(swa_k_norm[ic]), f(swa_sinks[ic]))
            w_out = f(swa_w_out[ic])
            ic += 1
        xT = run_phaseC(xT, oT, w_out, f(norm_mlp[i]), f(mlp_w_up[i]), f(mlp_w_down[i]))
    return np.ascontiguousarray(xT.T).reshape(BATCH, SEQ, D_MODEL).astype(np.float32)
```

```python
import numpy as np
from contextlib import ExitStack
import concourse.bass as bass
import concourse.mybir as mybir
from concourse.bass_utils import run_bass_kernel_spmd

F32 = mybir.dt.float32
BF16 = mybir.dt.bfloat16
ALU = mybir.AluOpType
AF = mybir.ActivationFunctionType
AX = mybir.AxisListType

D_MODEL = 2048
BATCH = 2
SEQ = 4096
DEPTH = 4
D_FF = 4 * D_MODEL
NORM_EPS = 1e-6
NCORES = 8
TOK = BATCH * SEQ // NCORES
KC = D_MODEL // 128

SEM_LIM = 16000
NDMASEM = 8


class Prog:
    ENGS = ("pe", "act", "dve", "pool", "sp")

    def __init__(self, nc, stack):
        self.nc = nc
        self.stack = stack
        self.ops = []
        self.lastw = {}
        self.readers = {}
        self._n = 0
        self.fence = None

    def sb(self, shape, dtype, name=None):
        self._n += 1
        return self.stack.enter_context(self.nc.sbuf_tensor((name + "_s") if name else f"sb{self._n}", list(shape), dtype))

    def ps(self, shape, dtype=F32, name=None):
        self._n += 1
        return self.stack.enter_context(self.nc.psum_tensor(name or f"ps{self._n}", list(shape), dtype))

    def add(self, eng, fn, r=(), w=(), dma=False):
        oid = len(self.ops)
        deps = set()
        for k in r:
            if k in self.lastw:
                deps.add(self.lastw[k])
        for k in w:
            if k in self.lastw:
                deps.add(self.lastw[k])
            deps.update(self.readers.get(k, ()))
        if self.fence is not None:
            deps.add(self.fence)
        for k in r:
            self.readers.setdefault(k, []).append(oid)
        for k in w:
            self.lastw[k] = oid
            self.readers[k] = []
        self.ops.append(dict(eng=eng, fn=fn, deps=deps, dma=dma))
        return oid

    def barrier(self):
        if not hasattr(self, "_fdummy"):
            self._fdummy = self.sb([1, 8], F32, "fence_dummy")
        keys = list(set(self.lastw.keys()) | set(self.readers.keys()))
        d = self._fdummy
        self.fence = None
        self.fence = self.add("pool", lambda e: e.memset(d[:], 0.0), r=(), w=keys)

    def dma(self, eng, out, in_, r=(), w=()):
        return self.add(eng, lambda e: e.dma_start(out=out, in_=in_), r=r, w=w, dma=True)

    def dump(self, name, ap, r):
        d = self.nc.dram_tensor(name, list(ap.shape), ap.dtype, kind="ExternalOutput").ap()
        self.dma("sp", d, ap, r=r, w=[("dump", name)])
        self.dumps = getattr(self, "dumps", []) + [("dump", name)]

    def finish(self, r):
        r = list(r) + getattr(self, "dumps", [])
        self.add("sp", None, r=r, w=())

    def emit(self):
        nc = self.nc
        ops = self.ops
        has_dep = [False] * len(ops)
        for op in ops:
            for d in op["deps"]:
                has_dep[d] = True
        cnt = {e: 0 for e in self.ENGS}
        dcnt = {e: 0 for e in self.ENGS}
        nsem = {}
        for i, op in enumerate(ops):
            e = op["eng"]
            if op["dma"]:
                j = dcnt[e]
                dcnt[e] += 1
                op["sig"] = (("d", e, j % NDMASEM), 16 * (j // NDMASEM + 1), 16)
                op["prev"] = (("d", e, j % NDMASEM), 16 * (j // NDMASEM)) if j >= NDMASEM else None
            elif has_dep[i] and op["fn"] is not None:
                c = cnt[e]
                cnt[e] += 1
                op["sig"] = (("c", e, c // SEM_LIM), c % SEM_LIM + 1, 1)
            else:
                op["sig"] = None
        keys = set()
        for op in ops:
            if op["sig"]:
                keys.add(op["sig"][0])
        sems = {}
        for k in sorted(keys):
            sems[k] = self.stack.enter_context(nc.semaphore("s_" + "_".join(str(x) for x in k)))
        by_eng = {e: [] for e in self.ENGS}
        for i, op in enumerate(ops):
            by_eng[op["eng"]].append(i)
        bname = {"pe": "tensor", "act": "scalar", "dve": "vector", "pool": "gpsimd", "sp": "sync"}
        self.n_wait = 0
        with nc.Block() as block:
            for E in self.ENGS:
                if not by_eng[E]:
                    continue

                def body(eng, E=E):
                    seen = {}
                    for i in by_eng[E]:
                        op = ops[i]
                        waits = {}
                        for d in op["deps"]:
                            dop = ops[d]
                            if dop["sig"] is None:
                                continue
                            if dop["eng"] == E and E == "pe" and not dop["dma"]:
                                continue
                            k, v, _ = dop["sig"]
                            if waits.get(k, 0) < v:
                                waits[k] = v
                        if op["dma"] and op["prev"] is not None:
                            k, v = op["prev"]
                            if waits.get(k, 0) < v:
                                waits[k] = v
                        for k, v in waits.items():
                            if seen.get(k, 0) >= v:
                                continue
                            seen[k] = v
                            eng.wait_ge(sems[k], v)
                            self.n_wait += 1
                        if op["fn"] is None:
                            continue
                        ins = op["fn"](eng)
                        if op["sig"] is not None:
                            k, v, inc = op["sig"]
                            ins.then_inc(sems[k], inc)

                getattr(block, bname[E])(body)


def rms_to_hT(P, xT, gain_sb, hT, ones_bf, psA, psB, scratch_bf, rr, eps_col, tag, gkey="gain"):
    nc = P.nc
    pss = [psA, psB]
    for kc in range(KC):
        sl = kc % 2
        P.add("act", lambda e, kc=kc, sl=sl: e.activation(out=scratch_bf[sl][:], in_=xT[:, kc, :], func=AF.Square),
              r=[("xT", kc)], w=[("sq", sl)])
        for hf in range(2):
            P.add("pe", lambda e, kc=kc, sl=sl, hf=hf: e.matmul(
                pss[hf][:], lhsT=ones_bf[:], rhs=scratch_bf[sl][:, hf * 512:(hf + 1) * 512],
                start=(kc == 0), stop=(kc == KC - 1)),
                r=[("sq", sl), "ones"], w=[("ps", tag, hf)])
    for hf in range(2):
        P.add("act", lambda e, hf=hf: e.activation(
            out=rr[:, hf * 512:(hf + 1) * 512], in_=pss[hf][:], func=AF.Sqrt, bias=eps_col[:], scale=1.0),
            r=[("ps", tag, hf), "epsc"], w=[("rr", hf)])
        P.add("dve", lambda e, hf=hf: e.reciprocal(out=rr[:, hf * 512:(hf + 1) * 512], in_=rr[:, hf * 512:(hf + 1) * 512]),
              r=[("rr", hf)], w=[("rr", hf)])
    for kc in range(KC):
        P.add("dve", lambda e, kc=kc: e.scalar_tensor_tensor(
            out=hT[:, kc, :], in0=xT[:, kc, :], scalar=gain_sb[:, kc:kc + 1], in1=rr[:],
            op0=ALU.mult, op1=ALU.mult), r=[("xT", kc), ("rr", 0), ("rr", 1), gkey], w=[("hT", kc)])


def build_phaseA(nch):
    nc = bass.Bass("TRN2", target_bir_lowering=False)
    xT_d = nc.dram_tensor("xT", [128, KC, TOK], F32, kind="ExternalInput").ap()
    gain_d = nc.dram_tensor("gain", [128, KC], F32, kind="ExternalInput").ap()
    w_d = nc.dram_tensor("w", [nch, 128, KC, 128], F32, kind="ExternalInput").ap()
    out_d = nc.dram_tensor("projT", [nch * 128, TOK], F32, kind="ExternalOutput").ap()
    with ExitStack() as stack:
        P = Prog(nc, stack)
        xT = P.sb([128, KC, TOK], F32, "xT_sb")
        hT = P.sb([128, KC, TOK], BF16, "hT_sb")
        gain_sb = P.sb([128, KC], F32, "gain_sb")
        ones_bf = P.sb([128, 128], BF16, "ones_bf")
        sq = [P.sb([128, TOK], BF16, f"sq{i}") for i in range(2)]
        rr = P.sb([128, TOK], F32, "rr")
        NW = 4
        wt = [P.sb([128, KC, 128], BF16, f"wt{i}") for i in range(NW)]
        NO = 3
        ot = [P.sb([128, TOK], F32, f"ot{i}") for i in range(NO)]
        ps = [P.ps([128, 512], F32, f"psb{i}") for i in range(8)]

        P.add("pool", lambda e: e.memset(ones_bf[:], 1.0), w=["ones"])
        eps_col = P.sb([128, 1], F32, "eps_col")
        P.add("pool", lambda e: e.memset(eps_col[:], float(D_MODEL * NORM_EPS)), w=["epsc"])
        P.dma("sp", gain_sb[:], gain_d, w=["gain"])
        for kc in range(KC):
            P.dma("sp", xT[:, kc, :], xT_d[:, kc, :], w=[("xT", kc)])
        P.add("act", lambda e: e.mul(out=gain_sb[:], in_=gain_sb[:], mul=float(np.sqrt(D_MODEL))), r=["gain"], w=["gain"])
        rms_to_hT(P, xT, gain_sb, hT, ones_bf, ps[0], ps[1], sq, rr, eps_col, "n")
        allh = [("hT", kc) for kc in range(KC)]
        for j in range(nch):
            s = j % NW
            P.dma("pool", wt[s][:], w_d[j], w=[("wt", s)])
            pb = (j % 3) * 2 + 2
            for kc in range(KC):
                for hf in range(2):
                    P.add("pe", lambda e, s=s, kc=kc, hf=hf, pb=pb: e.matmul(
                        ps[pb + hf][:], lhsT=wt[s][:, kc, :], rhs=hT[:, kc, hf * 512:(hf + 1) * 512],
                        start=(kc == 0), stop=(kc == KC - 1)),
                        r=[("wt", s)] + (allh if kc == 0 else []), w=[("ps", pb + hf)])
            o = j % NO
            P.add("act", lambda e, o=o, pb=pb: e.copy(out=ot[o][:, 0:512], in_=ps[pb][:]), r=[("ps", pb)], w=[("ot", o, 0)])
            P.add("dve", lambda e, o=o, pb=pb: e.tensor_copy(out=ot[o][:, 512:1024], in_=ps[pb + 1][:]), r=[("ps", pb + 1)], w=[("ot", o, 1)])
            P.dma("sp", out_d[j * 128:(j + 1) * 128, :], ot[o][:], r=[("ot", o, 0), ("ot", o, 1)], w=[("out", j)])
        P.finish([("out", j) for j in range(nch)])
        P.emit()
    return nc


def to_fm(a2d):
    r, c = a2d.shape
    return np.ascontiguousarray(a2d.reshape(r // 128, 128, c).transpose(1, 0, 2))


def prep_w_in(w, nch):
    d, n = w.shape
    wp = np.zeros((d, nch * 128), np.float32)
    wp[:, :n] = w
    return np.ascontiguousarray(wp.reshape(KC, 128, nch, 128).transpose(2, 1, 0, 3))


def run_phaseA(xT_full, gain, w):
    n = w.shape[1]
    nch = (n + 127) // 128
    nc = get_prog(("A", nch), lambda: build_phaseA(nch))
    wl = prep_w_in(w, nch)
    g = np.ascontiguousarray(gain.reshape(KC, 128).T)
    in_maps = []
    for c in range(NCORES):
        in_maps.append({"xT": to_fm(xT_full[:, c * TOK:(c + 1) * TOK]), "gain": g, "w": wl})
    res = run_bass_kernel_spmd(nc, in_maps, core_ids=list(range(NCORES)))
    return np.concatenate([r["projT"] for r in res.results], axis=1)[:n]


FB = 4
NFC = D_FF // 128


def build_phaseC(nch2=0):
    nc = bass.Bass("TRN2", target_bir_lowering=False)
    xT_d = nc.dram_tensor("xT", [128, KC, TOK], F32, kind="ExternalInput").ap()
    oT_d = nc.dram_tensor("oT", [128, KC, TOK], F32, kind="ExternalInput").ap()
    gain_d = nc.dram_tensor("gain", [128, KC], F32, kind="ExternalInput").ap()
    wo_d = nc.dram_tensor("wo", [KC, 128, KC, 128], F32, kind="ExternalInput").ap()
    wu_d = nc.dram_tensor("wu", [NFC, 128, KC, 128], F32, kind="ExternalInput").ap()
    wd_d = nc.dram_tensor("wd", [NFC, 128, D_MODEL], F32, kind="ExternalInput").ap()
    out_d = nc.dram_tensor("xoT", [128, KC, TOK], F32, kind="ExternalOutput").ap()
    if nch2:
        gain2_d = nc.dram_tensor("gain2", [128, KC], F32, kind="ExternalInput").ap()
        w2_d = nc.dram_tensor("w2", [nch2, 128, KC, 128], F32, kind="ExternalInput").ap()
        proj_d = nc.dram_tensor("projT", [nch2 * 128, TOK], F32, kind="ExternalOutput").ap()
    with ExitStack() as stack:
        P = Prog(nc, stack)
        xT = P.sb([128, KC, TOK], F32, "xT_sb")
        hT = P.sb([128, KC, TOK], BF16, "hT_sb")
        gain_sb = P.sb([128, KC], F32, "gain_sb")
        ones_bf = P.sb([128, 128], BF16, "ones_bf")
        eps_col = P.sb([128, 1], F32, "eps_col")
        sq = [P.sb([128, TOK], BF16, f"sq{i}") for i in range(2)]
        rr = P.sb([128, TOK], F32, "rr")
        NW = 4
        wt = [P.sb([128, KC, 128], BF16, f"wt{i}") for i in range(NW)]
        wd = [P.sb([128, D_MODEL], BF16, f"wd{i}") for i in range(2 * FB)]
        if nch2:
            gain2_sb = P.sb([128, KC], F32, "gain2_sb")
            ot2 = [P.sb([128, TOK], F32, f"ot2{i}") for i in range(2)]
        act = [P.sb([128, TOK], BF16, f"act{i}") for i in range(2 * FB)]
        tmp = [P.sb([128, 512], F32, f"tmp{i}") for i in range(2)]
        ps = [P.ps([128, 512], F32, f"psb{i}") for i in range(8)]

        P.add("pool", lambda e: e.memset(ones_bf[:], 1.0), w=["ones"])
        P.add("pool", lambda e: e.memset(eps_col[:], float(D_MODEL * NORM_EPS)), w=["epsc"])
        P.dma("sp", gain_sb[:], gain_d, w=["gain"])
        for kc in range(KC):
            P.dma("sp", xT[:, kc, :], xT_d[:, kc, :], w=[("xT", kc)])
        for kc in range(KC):
            P.dma("pool", hT[:, kc, :], oT_d[:, kc, :], w=[("hT", kc)])
        P.add("act", lambda e: e.mul(out=gain_sb[:], in_=gain_sb[:], mul=float(np.sqrt(D_MODEL))), r=["gain"], w=["gain"])
        allh = [("hT", kc) for kc in range(KC)]
        wcnt = [0]
        pcnt = [0]

        def wtile(src):
            s = wcnt[0] % NW
            wcnt[0] += 1
            P.dma("pool", wt[s][:], src, w=[("wt", s)])
            return s

        def pbank():
            pb = 2 + (pcnt[0] % 3) * 2
            pcnt[0] += 1
            return pb

        for n in range(KC):
            s = wtile(wo_d[n])
            pb = pbank()
            for kc in range(KC):
                for hf in range(2):
                    P.add("pe", lambda e, s=s, kc=kc, hf=hf, pb=pb: e.matmul(
                        ps[pb + hf][:], lhsT=wt[s][:, kc, :], rhs=hT[:, kc, hf * 512:(hf + 1) * 512],
                        start=(kc == 0), stop=(kc == KC - 1)),
                        r=[("wt", s)] + (allh if kc == 0 else []), w=[("ps", pb + hf)])
            for hf in range(2):
                P.add("dve", lambda e, n=n, hf=hf, pb=pb: e.tensor_tensor(
                    out=xT[:, n, hf * 512:(hf + 1) * 512], in0=xT[:, n, hf * 512:(hf + 1) * 512], in1=ps[pb + hf][:], op=ALU.add),
                    r=[("ps", pb + hf), ("xT", n)], w=[("xT", n)])
        rms_to_hT(P, xT, gain_sb, hT, ones_bf, ps[0], ps[1], sq, rr, eps_col, "n")
        for blk in range(NFC // FB):
            par = blk % 2
            for fl in range(FB):
                f = blk * FB + fl
                s = wtile(wu_d[f])
                a = par * FB + fl
                P.dma("pool", wd[a][:], wd_d[f], w=[("wd", a)])
                pb = pbank()
                for kc in range(KC):
                    for hf in range(2):
                        P.add("pe", lambda e, s=s, kc=kc, hf=hf, pb=pb: e.matmul(
                            ps[pb + hf][:], lhsT=wt[s][:, kc, :], rhs=hT[:, kc, hf * 512:(hf + 1) * 512],
                            start=(kc == 0), stop=(kc == KC - 1)),
                            r=[("wt", s)] + (allh if kc == 0 else []), w=[("ps", pb + hf)])
                for hf in range(2):
                    P.add("act", lambda e, hf=hf, pb=pb: e.activation(out=tmp[hf][:], in_=ps[pb + hf][:], func=AF.Relu),
                          r=[("ps", pb + hf)], w=[("tmp", hf)])
                    P.add("act", lambda e, hf=hf, a=a: e.activation(
                        out=act[a][:, hf * 512:(hf + 1) * 512], in_=tmp[hf][:], func=AF.Square),
                        r=[("tmp", hf)], w=[("act", a, hf)])
            for n in range(KC):
                pb = pbank()
                for fl in range(FB):
                    a = par * FB + fl
                    for hf in range(2):
                        P.add("pe", lambda e, a=a, n=n, hf=hf, pb=pb, fl=fl: e.matmul(
                            ps[pb + hf][:], lhsT=wd[a][:, n * 128:(n + 1) * 128], rhs=act[a][:, hf * 512:(hf + 1) * 512],
                            start=(fl == 0), stop=(fl == FB - 1)),
                            r=[("wd", a), ("act", a, hf)], w=[("ps", pb + hf)])
                for hf in range(2):
                    P.add("dve", lambda e, n=n, hf=hf, pb=pb: e.tensor_tensor(
                        out=xT[:, n, hf * 512:(hf + 1) * 512], in0=xT[:, n, hf * 512:(hf + 1) * 512], in1=ps[pb + hf][:], op=ALU.add),
                        r=[("ps", pb + hf), ("xT", n)], w=[("xT", n)])
        for kc in range(KC):
            P.dma("sp", out_d[:, kc, :], xT[:, kc, :], r=[("xT", kc)], w=[("out", kc)])
        fin = [("out", kc) for kc in range(KC)]
        if nch2:
            P.dma("sp", gain2_sb[:], gain2_d, w=["gain2"])
            P.add("act", lambda e: e.mul(out=gain2_sb[:], in_=gain2_sb[:], mul=float(np.sqrt(D_MODEL))), r=["gain2"], w=["gain2"])
            rms_to_hT(P, xT, gain2_sb, hT, ones_bf, ps[0], ps[1], sq, rr, eps_col, "n", gkey="gain2")
            for j in range(nch2):
                s = wtile(w2_d[j])
                pb = pbank()
                for kc in range(KC):
                    for hf in range(2):
                        P.add("pe", lambda e, s=s, kc=kc, hf=hf, pb=pb: e.matmul(
                            ps[pb + hf][:], lhsT=wt[s][:, kc, :], rhs=hT[:, kc, hf * 512:(hf + 1) * 512],
                            start=(kc == 0), stop=(kc == KC - 1)),
                            r=[("wt", s)] + (allh if kc == 0 else []), w=[("ps", pb + hf)])
                o = j % 2
                P.add("act", lambda e, o=o, pb=pb: e.copy(out=ot2[o][:, 0:512], in_=ps[pb][:]), r=[("ps", pb)], w=[("ot2", o, 0)])
                P.add("dve", lambda e, o=o, pb=pb: e.tensor_copy(out=ot2[o][:, 512:1024], in_=ps[pb + 1][:]), r=[("ps", pb + 1)], w=[("ot2", o, 1)])
                P.dma("sp", proj_d[j * 128:(j + 1) * 128, :], ot2[o][:], r=[("ot2", o, 0), ("ot2", o, 1)], w=[("pout", j)])
            fin += [("pout", j) for j in range(nch2)]
        P.finish(fin)
        P.emit()
    return nc


_PROG_CACHE = {}


def get_prog(key, builder):
    if key not in _PROG_CACHE:
        _PROG_CACHE[key] = builder()
    return _PROG_CACHE[key]


def run_phaseC(xT_full, oT_full, w_out, gain, w_up, w_down, trace=False, gain2=None, w_in2=None):
    nch2 = 0 if w_in2 is None else (w_in2.shape[1] + 127) // 128
    nc = get_prog(("C", nch2), lambda: build_phaseC(nch2))
    wo = prep_w_in(w_out, KC)
    wu = prep_w_in(w_up, NFC)
    wdl = np.ascontiguousarray(w_down.reshape(NFC, 128, D_MODEL))
    g = np.ascontiguousarray(gain.reshape(KC, 128).T)
    extra = {}
    if nch2:
        extra = {"gain2": np.ascontiguousarray(gain2.reshape(KC, 128).T), "w2": prep_w_in(w_in2, nch2)}
    in_maps = []
    for c in range(NCORES):
        m = {"xT": to_fm(xT_full[:, c * TOK:(c + 1) * TOK]), "oT": to_fm(oT_full[:, c * TOK:(c + 1) * TOK]),
             "gain": g, "wo": wo, "wu": wu, "wd": wdl}
        m.update(extra)
        in_maps.append(m)
    res = run_bass_kernel_spmd(nc, in_maps, core_ids=list(range(NCORES)), trace=trace)
    if trace:
        print("phaseC exec_time_ns", res.exec_time_ns)
    outs = [r["xoT"].transpose(1, 0, 2).reshape(D_MODEL, TOK) for r in res.results]
    xo = np.concatenate(outs, axis=1)
    if nch2:
        pj = np.concatenate([r["projT"] for r in res.results], axis=1)[:w_in2.shape[1]]
        return xo, pj
    return xo


NEGB = -30000.0
ROPE_THETA = 500000.0


def rope_tables(d, pos, reps):
    rot = d // 4
    half = rot // 2
    inv = np.power(np.float32(ROPE_THETA), -np.arange(half, dtype=np.float32) / np.float32(half)).astype(np.float32)
    ang = pos.astype(np.float32)[None, :] * inv[:, None]
    c = np.ones((d, len(pos)), np.float32)
    s = np.zeros((d, len(pos)), np.float32)
    c[:half] = np.cos(ang)
    c[half:rot] = np.cos(ang)
    s[:half] = np.sin(ang)
    s[half:rot] = np.sin(ang)
    return np.tile(c, (reps, 1)), np.tile(s, (reps, 1))


def rope_matrix(d, reps):
    rot = d // 4
    half = rot // 2
    m = np.zeros((reps * d, reps * d), np.float32)
    for r in range(reps):
        o = r * d
        for i in range(half):
            m[o + i + half, o + i] = -1.0
            m[o + i, o + i + half] = 1.0
    return m


def mask_biases():
    kl = np.arange(128)[:, None]
    ql = np.arange(128)[None, :]
    diag = np.where(kl <= ql, 0.0, NEGB).astype(np.float32)
    prev = np.where(kl > ql, 0.0, NEGB).astype(np.float32)
    return np.tile(diag, (1, 4)), np.tile(prev, (1, 4))


def qk_norm_rope(P, src, dst_bf, gain_col, ones_blk, R_sb, cosf, sinf, sq, rr, tmpf, psbig, eps_col, inv_d, key, ntok):
    cols = [(c0, min(512, ntok - c0)) for c0 in range(0, ntok, 512)]
    ng = (len(cols) + 3) // 4
    gw = [sum(n for (_, n) in cols[4 * g:4 * g + 4]) for g in range(ng)]
    P.add("act", lambda e: e.activation(out=sq[:, :ntok], in_=src, func=AF.Square), r=[key], w=["sq"])
    for c, (c0, n) in enumerate(cols):
        P.add("pe", lambda e, c=c, c0=c0, n=n: e.matmul(psbig[c // 4][:, (c % 4) * 512:(c % 4) * 512 + n], lhsT=ones_blk[:],
                                                        rhs=sq[:, c0:c0 + n], start=True, stop=True),
              r=["sq", "onesblk"], w=[("psbig", c // 4)])
    for g in range(ng):
        n = gw[g]
        P.add("act", lambda e, g=g, n=n: e.activation(out=rr[:, g * 2048:g * 2048 + n], in_=psbig[g][:, :n], func=AF.Sqrt,
                                                      bias=eps_col[:], scale=inv_d),
              r=["epsc"], w=[("psbig", g), ("rr", g)])
        P.add("dve", lambda e, g=g, n=n: e.reciprocal(out=rr[:, g * 2048:g * 2048 + n], in_=rr[:, g * 2048:g * 2048 + n]),
              r=[("rr", g)], w=[("rr", g)])
    rrk = [("rr", g) for g in range(ng)]
    P.add("dve", lambda e: e.scalar_tensor_tensor(out=src, in0=src, scalar=gain_col, in1=rr[:, :ntok], op0=ALU.mult, op1=ALU.mult),
          r=[key, "gains"] + rrk, w=[key])
    for c, (c0, n) in enumerate(cols):
        P.add("pe", lambda e, c=c, c0=c0, n=n: e.matmul(psbig[c // 4][:, (c % 4) * 512:(c % 4) * 512 + n], lhsT=R_sb[:],
                                                        rhs=src[:, c0:c0 + n], start=True, stop=True),
              r=[key, "Rm"], w=[("psbig", c // 4)])
    for g in range(ng):
        n = gw[g]
        P.add("dve", lambda e, g=g, n=n: e.tensor_tensor(out=tmpf[:, g * 2048:g * 2048 + n], in0=psbig[g][:, :n],
                                                         in1=sinf[:, g * 2048:g * 2048 + n], op=ALU.mult),
              r=["tabs"], w=[("psbig", g), ("tmpf", g)])
    P.add("pool", lambda e: e.tensor_tensor(out=src, in0=src, in1=cosf[:, :ntok], op=ALU.mult), r=[key, "tabs"], w=[key])
    P.add("pool", lambda e: e.tensor_tensor(out=dst_bf, in0=src, in1=tmpf[:, :ntok], op=ALU.add),
          r=[key] + [("tmpf", g) for g in range(ng)], w=[key + "_bf"])


def build_swaB():
    S = SEQ
    nc = bass.Bass("TRN2", target_bir_lowering=False)
    q_d = nc.dram_tensor("q", [4, 128, S], F32, kind="ExternalInput").ap()
    k_d = nc.dram_tensor("k2", [128, S], F32, kind="ExternalInput").ap()
    v_d = nc.dram_tensor("v", [128, 32, 64], F32, kind="ExternalInput").ap()
    gq_d = nc.dram_tensor("gq", [128, 1], F32, kind="ExternalInput").ap()
    gk_d = nc.dram_tensor("gk", [128, 1], F32, kind="ExternalInput").ap()
    es_d = nc.dram_tensor("esink", [1, 2, 512], F32, kind="ExternalInput").ap()
    cos_d = nc.dram_tensor("cosf", [128, S], F32, kind="ExternalInput").ap()
    sin_d = nc.dram_tensor("sinf", [128, S], F32, kind="ExternalInput").ap()
    R_d = nc.dram_tensor("Rm", [128, 128], F32, kind="ExternalInput").ap()
    id_d = nc.dram_tensor("ident", [128, 128], F32, kind="ExternalInput").ap()
    bd_d = nc.dram_tensor("bdiag", [128, 512], F32, kind="ExternalInput").ap()
    bp_d = nc.dram_tensor("bprev", [128, 512], F32, kind="ExternalInput").ap()
    ob_d = nc.dram_tensor("onesblk", [128, 128], F32, kind="ExternalInput").ap()
    out_d = nc.dram_tensor("oT", [512, S], F32, kind="ExternalOutput").ap()
    with ExitStack() as stack:
        P = Prog(nc, stack)
        work = [P.sb([128, S], F32, f"work{i}") for i in range(2)]
        qbf = P.sb([128, 4, S], BF16, "qbf")
        kbf = P.sb([128, S], BF16, "kbf")
        vbf = P.sb([128, 32, 64], BF16, "vbf")
        cosf = P.sb([128, S], F32, "cosf")
        sinf = P.sb([128, S], F32, "sinf")
        rr = P.sb([128, S], F32, "rr")
        tmpf = P.sb([128, S], F32, "tmpf")
        sq = P.sb([128, S], BF16, "sq")
        R_sb = P.sb([128, 128], F32, "R_sb")
        ident = P.sb([128, 128], BF16, "ident")
        bdiag = P.sb([128, 512], BF16, "bdiag")
        bprev = P.sb([128, 512], BF16, "bprev")
        onesblk = P.sb([128, 128], BF16, "onesblk")
        ones64 = P.sb([128, 64], BF16, "ones64")
        ones1 = P.sb([1, 64], BF16, "ones1")
        esink = P.sb([1, 2, 512], BF16, "esink")
        gq = P.sb([128, 1], F32, "gq")
        gk = P.sb([128, 1], F32, "gk")
        eps_col = P.sb([128, 1], F32, "eps_col")
        pt = [P.sb([128, 512], BF16, f"pt{i}") for i in range(4)]
        rden = [P.sb([64, 512], F32, f"rden{i}") for i in range(2)]
        ost = [P.sb([64, 4, 512], F32, f"ost{i}") for i in range(2)]
        psbig = [P.ps([128, 2048], F32, f"psbig{i}") for i in range(2)]

        P.add("pool", lambda e: e.memset(eps_col[:], float(NORM_EPS)), w=["epsc"])
        P.add("pool", lambda e: e.memset(ones64[:], 1.0), w=["ones64"])
        P.add("pool", lambda e: e.memset(ones1[:], 1.0), w=["ones1"])
        P.dma("sp", cosf[:], cos_d, w=["tabs"])
        P.dma("sp", sinf[:], sin_d, w=["tabs"])
        P.dma("sp", R_sb[:], R_d, w=["Rm"])
        P.dma("sp", gq[:], gq_d, w=["gains"])
        P.dma("sp", gk[:], gk_d, w=["gains"])
        P.dma("pool", ident[:], id_d, w=["ident"])
        P.dma("pool", bdiag[:], bd_d, w=["bias"])
        P.dma("pool", bprev[:], bp_d, w=["bias"])
        P.dma("pool", onesblk[:], ob_d, w=["onesblk"])
        esf = P.sb([1, 2, 512], F32, "esf")
        P.dma("sp", esf[:], es_d, w=["esf"])
        P.add("act", lambda e: e.activation(out=esink[:], in_=esf[:], func=AF.Exp), r=["esf"], w=["esink"])
        P.dma("pool", vbf[:], v_d, w=["v"])
        P.dma("sp", work[0][:], k_d, w=["w0"])
        qk_norm_rope(P, work[0][:], kbf[:], gk[:, 0:1], onesblk, R_sb, cosf, sinf, sq, rr, tmpf, psbig, eps_col, 1.0 / 64, "w0", S)
        for p in range(4):
            wk = (p + 1) % 2
            P.dma("sp", work[wk][:], q_d[p], w=[f"w{wk}"])
            qk_norm_rope(P, work[wk][:], qbf[:, p, :], gq[:, 0:1], onesblk, R_sb, cosf, sinf, sq, rr, tmpf, psbig, eps_col, 1.0 / 64,
                         f"w{wk}", S)
        qkeys = ["w0_bf", "w1_bf"]
        u = 0
        outv = out_d.rearrange("(p e d) t -> e d p t", p=4, e=2, d=64)
        for qg in range(SEQ // 512):
            for e in range(2):
                os_ = ost[(qg * 2 + e) % 2]
                for qi in range(4):
                    qt = qg * 4 + qi
                    kts = [kt for kt in (qt - 1, qt) if kt >= 0]
                    oset = u % 2
                    u += 1
                    ops_ = psbig[1][0:64, oset * 1024:oset * 1024 + 512]
                    dps_ = psbig[1][0:64, oset * 1024 + 512:oset * 1024 + 1024]
                    okey = ("ops", oset)
                    for i, kt in enumerate(kts):
                        sb_ = (u * 2 + i) % 4
                        sps = psbig[0][:, sb_ * 512:(sb_ + 1) * 512]
                        bias = bdiag if kt == qt else bprev
                        P.add("pe", lambda e_, e=e, kt=kt, qt=qt, sps=sps: e_.matmul(
                            sps, lhsT=kbf[64 * e:64 * e + 64, kt * 128:(kt + 1) * 128],
                            rhs=qbf[64 * e:64 * e + 64, :, qt * 128:(qt + 1) * 128], start=True, stop=False),
                            r=qkeys, w=[("sps", sb_)])
                        P.add("pe", lambda e_, sps=sps, bias=bias: e_.matmul(sps, lhsT=ident[:], rhs=bias[:], start=False, stop=True),
                              r=["ident", "bias"], w=[("sps", sb_)])
                        P.add("act", lambda e_, sps=sps, sb_=sb_: e_.activation(out=pt[sb_][:], in_=sps, func=AF.Exp, scale=0.125),
                              r=[("sps", sb_)], w=[("pt", sb_)])
                        P.add("pe", lambda e_, kt=kt, sb_=sb_, ops_=ops_, i=i: e_.matmul(
                            ops_, lhsT=vbf[:, kt, :], rhs=pt[sb_][:], start=(i == 0), stop=(i == len(kts) - 1)),
                            r=[("pt", sb_), "v"], w=[okey])
                        P.add("pe", lambda e_, sb_=sb_, dps_=dps_, i=i: e_.matmul(
                            dps_, lhsT=ones64[:], rhs=pt[sb_][:], start=(i == 0), stop=False),
                            r=[("pt", sb_), "ones64"], w=[okey])
                    P.add("pe", lambda e_, dps_=dps_, e=e: e_.matmul(dps_, lhsT=ones1[:], rhs=esink[:, e, :], start=False, stop=True),
                          r=["ones1", "esink"], w=[okey])
                    P.add("dve", lambda e_, dps_=dps_, oset=oset: e_.reciprocal(out=rden[oset][:], in_=dps_), r=[okey], w=[("rden", oset)])
                    P.add("dve", lambda e_, ops_=ops_, oset=oset, os_=os_, qi=qi: e_.tensor_tensor(
                        out=os_[:, :, qi * 128:(qi + 1) * 128], in0=ops_.rearrange("d (p q) -> d p q", p=4),
                        in1=rden[oset][:].rearrange("d (p q) -> d p q", p=4), op=ALU.mult),
                        r=[okey, ("rden", oset)], w=[("ost", (qg * 2 + e) % 2)])
                P.dma("sp", outv[e, :, :, qg * 512:(qg + 1) * 512], os_[:], r=[("ost", (qg * 2 + e) % 2)], w=[("out", qg, e)])
        P.finish([("out", qg, e) for qg in range(SEQ // 512) for e in range(2)])
        P.emit()
    return nc


def run_swaB(projT, q_norm, k_norm, sinks, trace=False):
    S = SEQ
    nc = get_prog("swaB", build_swaB)
    cosf, sinf = rope_tables(64, np.arange(S), 2)
    Rm = rope_matrix(64, 2)
    ident = np.eye(128, dtype=np.float32)
    bdiag, bprev = mask_biases()
    onesblk = np.kron(np.eye(2, dtype=np.float32), np.ones((64, 64), np.float32))
    gq = np.tile(q_norm.astype(np.float32), 2).reshape(128, 1)
    gk = np.tile(k_norm.astype(np.float32), 2).reshape(128, 1)
    in_maps = []
    for c in range(NCORES):
        b, g = c // 4, c % 4
        t0 = b * S
        q = np.ascontiguousarray(projT[g * 512:(g + 1) * 512, t0:t0 + S].reshape(4, 128, S))
        k = projT[2048 + g * 64:2048 + (g + 1) * 64, t0:t0 + S]
        k2 = np.ascontiguousarray(np.concatenate([k, k], axis=0))
        v = projT[2304 + g * 64:2304 + (g + 1) * 64, t0:t0 + S].T
        v = np.ascontiguousarray(v.reshape(32, 128, 64).transpose(1, 0, 2))
        sk = sinks[g * 8:(g + 1) * 8].astype(np.float32).reshape(4, 2)
        es = np.ascontiguousarray(np.repeat(sk.T[:, :, None], 128, axis=2).reshape(1, 2, 512))
        in_maps.append(dict(q=q, k2=k2, v=v, gq=gq, gk=gk, esink=es, cosf=cosf, sinf=sinf, Rm=Rm, ident=ident,
                            bdiag=bdiag, bprev=bprev, onesblk=onesblk))
    res = run_bass_kernel_spmd(nc, in_maps, core_ids=list(range(NCORES)), trace=trace)
    if trace:
        print("swaB exec_time_ns", res.exec_time_ns)
    oT = np.zeros((D_MODEL, BATCH * S), np.float32)
    for c in range(NCORES):
        b, g = c // 4, c % 4
        oT[g * 512:(g + 1) * 512, b * S:(b + 1) * S] = res.results[c]["oT"]
    return oT


def gla_consts():
    j = np.arange(128)[:, None]
    i = np.arange(128)[None, :]
    same = (j // 64) == (i // 64)
    T2 = np.where(same & (j <= i), -1.0 / 16.0, 0.0).astype(np.float32)
    U2 = np.where(same & (j > i), -1.0 / 16.0, 0.0).astype(np.float32)
    M2 = np.where(same & (j <= i), 1.0, 0.0).astype(np.float32)
    return T2, U2, M2


def build_glaB(dbg=False):
    S = SEQ
    NT = S // 128
    nc = bass.Bass("TRN2", target_bir_lowering=False)
    glr_d = nc.dram_tensor("glrT", [16, S], F32, kind="ExternalInput").ap()
    q_d = nc.dram_tensor("qT", [128, 2, S], F32, kind="ExternalInput").ap()
    k_d = nc.dram_tensor("kT", [128, 2, S], F32, kind="ExternalInput").ap()
    ktm_d = nc.dram_tensor("ktm", [NT, 128, 256], F32, kind="ExternalInput").ap()
    v_d = nc.dram_tensor("vtm", [NT, 128, 512], F32, kind="ExternalInput").ap()
    r_d = nc.dram_tensor("rT", [128, 4, S], F32, kind="ExternalInput").ap()
    wg_d = nc.dram_tensor("wg", [16, 256], F32, kind="ExternalInput").ap()
    bg_d = nc.dram_tensor("bg", [1, 256], F32, kind="ExternalInput").ap()
    gn_d = nc.dram_tensor("gn", [128, 4], F32, kind="ExternalInput").ap()
    T2_d = nc.dram_tensor("T2", [128, 128], F32, kind="ExternalInput").ap()
    U2_d = nc.dram_tensor("U2", [128, 128], F32, kind="ExternalInput").ap()
    M2_d = nc.dram_tensor("M2", [128, 128], F32, kind="ExternalInput").ap()
    out_d = nc.dram_tensor("oT", [512, S], F32, kind="ExternalOutput").ap()
    with ExitStack() as stack:
        P = Prog(nc, stack)
        qp = P.sb([128, 2, S], BF16, "qp")
        kp = P.sb([128, 2, S], BF16, "kp")
        kpp = P.sb([128, NT, 256], BF16, "kpp")
        vbf = P.sb([128, NT, 512], BF16, "vbf")
        att = P.sb([128, NT, 128], BF16, "att")
        explast = P.sb([128, 2, 2 * NT], F32, "explast")
        glr = P.sb([16, S], F32, "glr")
        wg = P.sb([16, 256], F32, "wg")
        bg = P.sb([1, 256], F32, "bg")
        gn = P.sb([128, 4], F32, "gn")
        T2 = P.sb([128, 128], F32, "T2")
        U2 = P.sb([128, 128], F32, "U2")
        M2 = P.sb([128, 128], F32, "M2")
        ones1 = P.sb([1, 128], F32, "ones1")
        onesb = P.sb([128, 128], BF16, "onesb")
        eps_col = P.sb([128, 1], F32, "eps_col")
        qt_ = [P.sb([128, 2, 128], F32, f"qt{i}") for i in range(2)]
        kt_ = [P.sb([128, 2, 128], F32, f"kt{i}") for i in range(2)]
        ktm = [P.sb([128, 256], F32, f"ktm{i}") for i in range(2)]
        e1 = [P.sb([128, 256], F32, f"e1{i}") for i in range(2)]
        la = [P.sb([128, 256], F32, f"la{i}") for i in range(2)]
        ET = [P.sb([128, 2, 128], F32, f"ET{i}") for i in range(2)]
        EinvT = [P.sb([128, 2, 128], F32, f"EinvT{i}") for i in range(2)]
        Elmc = [P.sb([128, 256], F32, f"Elmc{i}") for i in range(2)]
        Sst = P.sb([128, 2, 512], F32, "Sst")
        Sbf = [P.sb([128, 2, 512], BF16, f"Sbf{i}") for i in range(2)]
        ot = [P.sb([128, 4, 128], F32, f"ot{i}") for i in range(2)]
        osq = [P.sb([128, 4, 128], BF16, f"osq{i}") for i in range(2)]
        rs = [P.sb([128, 128], F32, f"rs{i}") for i in range(2)]
        rt = [P.sb([128, 4, 128], F32, f"rt{i}") for i in range(2)]
        ost = [P.sb([128, 4, 512], F32, f"ost{i}") for i in range(2)]
        ps = [P.ps([128, 512], F32, f"psb{i}") for i in range(8)]

        P.add("pool", lambda e: e.memset(eps_col[:], float(NORM_EPS)), w=["epsc"])
        P.add("pool", lambda e: e.memset(ones1[:], 1.0), w=["ones1"])
        P.add("pool", lambda e: e.memset(onesb[:], 1.0), w=["onesb"])
        P.add("pool", lambda e: e.memset(Sst[:], 0.0), w=["S"])
        for t_, d_, k_ in ((glr, glr_d, "glr"), (wg, wg_d, "wg"), (bg, bg_d, "bg"), (gn, gn_d, "gn"), (T2, T2_d, "T2"),
                           (U2, U2_d, "U2"), (M2, M2_d, "M2")):
            P.dma("sp", t_[:], d_, w=[k_])

        def pass1(t):
            b = t % 2
            tok = slice(t * 128, (t + 1) * 128)
            P.dma("sp", qt_[b][:], q_d[:, :, tok], w=[("qt", b)])
            P.dma("sp", kt_[b][:], k_d[:, :, tok], w=[("kt", b)])
            P.dma("sp", ktm[b][:], ktm_d[t], w=[("ktm", b)])
            P.dma("pool", vbf[:, t, :], v_d[t], w=[("v", t)])
            pA = ps[2 * b]
            pB = ps[2 * b + 1]
            P.add("pe", lambda e: e.matmul(pA[:, 0:256], lhsT=glr[:, tok], rhs=wg[:], start=True, stop=False),
                  r=["glr", "wg"], w=[("pA", b)])
            P.add("pe", lambda e: e.matmul(pA[:, 0:256], lhsT=ones1[:], rhs=bg[:], start=False, stop=True),
                  r=["ones1", "bg"], w=[("pA", b)])
            P.add("act", lambda e: e.activation(out=e1[b][:], in_=pA[:, 0:256], func=AF.Exp, scale=-1.0), r=[("pA", b)], w=[("e1", b)])
            P.add("act", lambda e: e.activation(out=la[b][:], in_=e1[b][:], func=AF.Ln, bias=1.0), r=[("e1", b)], w=[("la", b)])
            for dc in range(2):
                P.add("pe", lambda e, dc=dc: e.matmul(pA[:, 256 + dc * 128:256 + (dc + 1) * 128], lhsT=la[b][:, dc * 128:(dc + 1) * 128],
                                                      rhs=T2[:], start=True, stop=True), r=[("la", b), "T2"], w=[("pA2", b)])
            P.add("pe", lambda e: e.matmul(pB[:, 0:256], lhsT=U2[:], rhs=la[b][:], start=True, stop=True), r=[("la", b), "U2"], w=[("pB", b)])
            cumT = pA[:, 256:512].rearrange("p (c i) -> p c i", c=2)
            P.add("act", lambda e: e.activation(out=ET[b][:], in_=cumT, func=AF.Exp), r=[("pA2", b)], w=[("ET", b)])
            P.add("act", lambda e: e.activation(out=EinvT[b][:], in_=cumT, func=AF.Exp, scale=-1.0), r=[("pA2", b)], w=[("EinvT", b)])
            P.add("act", lambda e: e.activation(out=Elmc[b][:], in_=pB[:, 0:256], func=AF.Exp), w=[("pB", b), ("Elmc", b)])
            P.add("dve", lambda e: e.scalar_tensor_tensor(out=qp[:, :, tok], in0=qt_[b][:], scalar=float(256 ** -0.5), in1=ET[b][:],
                                                          op0=ALU.mult, op1=ALU.mult), r=[("qt", b), ("ET", b)], w=[("qp", t)])
            P.add("dve", lambda e: e.tensor_tensor(out=kp[:, :, tok], in0=kt_[b][:], in1=EinvT[b][:], op=ALU.mult),
                  r=[("kt", b), ("EinvT", b)], w=[("kp", t)])
            P.add("pool", lambda e: e.tensor_tensor(out=kpp[:, t, :], in0=ktm[b][:], in1=Elmc[b][:], op=ALU.mult),
                  r=[("ktm", b), ("Elmc", b)], w=[("kpp", t)])
            P.add("pool", lambda e: e.tensor_copy(out=explast[:, :, 2 * t:2 * t + 2], in_=ET[b][:, :, 63:128:64]),
                  r=[("ET", b)], w=[("explast", t)])
            for dc in range(2):
                P.add("pe", lambda e, dc=dc: e.matmul(pB[:, 256:384], lhsT=kp[:, dc, tok], rhs=qp[:, dc, tok], start=(dc == 0), stop=(dc == 1)),
                      r=[("kp", t), ("qp", t)], w=[("pB", b)])
            P.add("dve", lambda e: e.tensor_tensor(out=att[:, t, :], in0=pB[:, 256:384], in1=M2[:], op=ALU.mult),
                  r=["M2"], w=[("pB", b), ("att", t)])

        sidx = [0]

        def pass2(t):
            b = t % 2
            tok0 = t * 128
            pO = ps[6]
            pN = ps[7]
            P.dma("sp", rt[b][:], r_d[:, :, tok0:tok0 + 128], w=[("rt", b)])
            for dvc in range(4):
                P.add("pe", lambda e, dvc=dvc: e.matmul(pO[:, dvc * 128:(dvc + 1) * 128], lhsT=vbf[:, t, dvc * 128:(dvc + 1) * 128],
                                                        rhs=att[:, t, :], start=(dvc == 0), stop=False),
                      r=[("v", t), ("att", t)], w=["pO"])
            for c in range(2):
                ch = 2 * t + c
                cs = slice(tok0 + 64 * c, tok0 + 64 * c + 64)
                if ch > 0:
                    sb_ = Sbf[sidx[0] % 2]
                    sk = ("Sbf", sidx[0] % 2)
                    for dvc in range(4):
                        for dc in range(2):
                            P.add("pe", lambda e, dvc=dvc, dc=dc, sb_=sb_, c=c, cs=cs: e.matmul(
                                pO[:, dvc * 128 + 64 * c:dvc * 128 + 64 * c + 64], lhsT=sb_[:, dc, dvc * 128:(dvc + 1) * 128],
                                rhs=qp[:, dc, cs], start=False, stop=(dc == 1 and c == 1)),
                                r=[sk, ("qp", t)], w=["pO"])
                for dc in range(2):
                    pk = ps[4 + dc]
                    P.add("pe", lambda e, dc=dc, pk=pk, c=c: e.matmul(pk[:], lhsT=kpp[64 * c:64 * c + 64, t, dc * 128:(dc + 1) * 128],
                                                                 rhs=vbf[64 * c:64 * c + 64, t, :], start=True, stop=True),
                          r=[("kpp", t), ("v", t)], w=[("pk", dc)])
                sidx[0] += 1
                sb_ = Sbf[sidx[0] % 2]
                sk = ("Sbf", sidx[0] % 2)
                for dc in range(2):
                    pk = ps[4 + dc]
                    P.add("dve", lambda e, dc=dc, pk=pk, ch=ch: e.scalar_tensor_tensor(
                        out=Sst[:, dc, :], in0=Sst[:, dc, :], scalar=explast[:, dc, ch:ch + 1], in1=pk[:], op0=ALU.mult, op1=ALU.add),
                        r=[("pk", dc), ("explast", t), "S"], w=["S"])
                P.add("act", lambda e, sb_=sb_: e.copy(out=sb_[:], in_=Sst[:]), r=["S"], w=[sk])
                if dbg and t == 0 and c == 0:
                    P.dump("d_S0", Sst[:], ["S"])
                    P.dump("d_Sbf0", sb_[:], [sk])
            P.add("act", lambda e: e.copy(out=ot[b][:], in_=pO[:].rearrange("p (c i) -> p c i", c=4)), r=["pO"], w=[("ot", b)])
            if dbg and t == 0:
                P.dump("d_ot", ot[0][:], [("ot", 0)])
                P.dump("d_S", Sst[:], ["S"])
            P.add("act", lambda e: e.activation(out=osq[b][:], in_=ot[b][:], func=AF.Square), r=[("ot", b)], w=[("osq", b)])
            for dvc in range(4):
                P.add("pe", lambda e, dvc=dvc: e.matmul(pN[:, 0:128], lhsT=onesb[:], rhs=osq[b][:, dvc, :], start=(dvc == 0), stop=(dvc == 3)),
                      r=[("osq", b), "onesb"], w=["pN"])
            P.add("act", lambda e: e.activation(out=rs[b][:], in_=pN[:, 0:128], func=AF.Sqrt, bias=eps_col[:], scale=1.0 / 512),
                  r=["pN", "epsc"], w=[("rs", b)])
            P.add("dve", lambda e: e.reciprocal(out=rs[b][:], in_=rs[b][:]), r=[("rs", b)], w=[("rs", b)])
            P.add("act", lambda e: e.activation(out=rt[b][:], in_=rt[b][:], func=AF.Silu), r=[("rt", b)], w=[("rt", b)])
            o4 = ost[(t // 4) % 2]
            for dvc in range(4):
                P.add("dve", lambda e, dvc=dvc: e.scalar_tensor_tensor(out=ot[b][:, dvc, :], in0=ot[b][:, dvc, :], scalar=gn[:, dvc:dvc + 1],
                                                                       in1=rs[b][:], op0=ALU.mult, op1=ALU.mult),
                      r=[("ot", b), ("rs", b), "gn"], w=[("ot", b)])
            P.add("pool", lambda e: e.tensor_tensor(out=o4[:, :, (t % 4) * 128:(t % 4 + 1) * 128], in0=ot[b][:], in1=rt[b][:], op=ALU.mult),
                  r=[("ot", b), ("rt", b)], w=[("ost", (t // 4) % 2)])
            if t % 4 == 3:
                g4 = t // 4
                P.dma("sp", out_d.rearrange("(c p) t -> p c t", p=128)[:, :, g4 * 512:(g4 + 1) * 512], o4[:],
                      r=[("ost", g4 % 2)], w=[("out", g4)])

        pass1(0)
        if dbg:
            P.dump("d_la", la[0][:], [("la", 0)])
            P.dump("d_ET", ET[0][:], [("ET", 0)])
            P.dump("d_Elmc", Elmc[0][:], [("Elmc", 0)])
            P.dump("d_att", att[:, 0, :], [("att", 0)])
            P.dump("d_qp", qp[:, :, 0:128], [("qp", 0)])
            P.dump("d_kp", kp[:, :, 0:128], [("kp", 0)])
            P.dump("d_kpp", kpp[:, 0, :], [("kpp", 0)])
            P.dump("d_explast", explast[:, :, 0:2], [("explast", 0)])
        pass1(1)
        for t in range(NT):
            if t + 2 < NT:
                pass1(t + 2)
            pass2(t)
        P.finish([("out", g4) for g4 in range(NT // 4)])
        P.emit()
    return nc


def run_glaB(projT, w_gate_up, b_gate, o_norm, trace=False):
    S = SEQ
    nc = get_prog("glaB", build_glaB)
    T2, U2, M2 = gla_consts()
    gn = np.ascontiguousarray(o_norm.astype(np.float32).reshape(4, 128).T)
    in_maps = []
    for c in range(NCORES):
        b, hd = c // 4, c % 4
        ts = slice(b * S, (b + 1) * S)
        qT = to_fm(projT[hd * 256:(hd + 1) * 256, ts])
        kT_ = projT[1024 + hd * 256:1024 + (hd + 1) * 256, ts]
        kT = to_fm(kT_)
        ktm = np.ascontiguousarray(kT_.T.reshape(S // 128, 128, 256))
        vtm = np.ascontiguousarray(projT[2048 + hd * 512:2048 + (hd + 1) * 512, ts].T.reshape(S // 128, 128, 512))
        glrT = np.ascontiguousarray(projT[4096:4112, ts])
        rT = to_fm(projT[4112 + hd * 512:4112 + (hd + 1) * 512, ts])
        wg = np.ascontiguousarray(w_gate_up[:, hd * 256:(hd + 1) * 256])
        bg = np.ascontiguousarray(b_gate[hd * 256:(hd + 1) * 256].reshape(1, 256))
        in_maps.append(dict(glrT=glrT, qT=qT, kT=kT, ktm=ktm, vtm=vtm, rT=rT, wg=wg, bg=bg, gn=gn, T2=T2, U2=U2, M2=M2))
    res = run_bass_kernel_spmd(nc, in_maps, core_ids=list(range(NCORES)), trace=trace)
    if trace:
        print("glaB exec_time_ns", res.exec_time_ns)
    oT = np.zeros((D_MODEL, BATCH * S), np.float32)
    for c in range(NCORES):
        b, hd = c // 4, c % 4
        oT[hd * 512:(hd + 1) * 512, b * S:(b + 1) * S] = res.results[c]["oT"]
    return oT


NSA_NT = SEQ // 128
NSA_NCMP = (SEQ - 32) // 16 + 1


def nsa_consts():
    S = SEQ
    NT = NSA_NT
    cm = np.zeros((128, 48, 128), np.float32)
    ql = np.arange(128)[None, :]
    cl = np.arange(128)[:, None]
    for qt in range(NT):
        for ct in range(2):
            if ct == 1 and qt < 16:
                continue
            idx = qt if ct == 0 else 32 + qt - 16
            c = cl + 128 * ct
            vis = (16 * c + 31 <= 128 * qt + ql) & (c < NSA_NCMP)
            cm[:, idx, :] = np.where(vis, 0.0, NEGB)
    bonus = np.zeros((128, NT, 64), np.float32)
    j = np.arange(64)[None, :]
    for qt in range(NT):
        pos = 128 * qt + np.arange(128)[:, None]
        bq = pos // 64
        forced = (j == 0) | (j == bq) | (j == bq - 1)
        bonus[:, qt, :] = np.where(j <= bq, np.where(forced, 1e4, 0.0), -1e30)
    E = np.zeros((64, NT, 128), np.float32)
    for kt in range(NT):
        E[2 * kt, kt, :64] = 1.0
        E[2 * kt + 1, kt, 64:] = 1.0
    c0 = np.arange(256) * 16
    s0 = np.arange(64) * 64
    ov = np.minimum(c0[:, None] + 32, s0[None, :] + 64) - np.maximum(c0[:, None], s0[None, :])
    ov = (np.clip(ov, 0, None) / 32.0).astype(np.float32)
    ov[NSA_NCMP:] = 0.0
    ov = np.ascontiguousarray(ov.reshape(2, 128, 64).transpose(1, 0, 2))
    return cm, bonus, E, ov


def build_nsaB(dbg=False):
    S = SEQ
    NT = NSA_NT
    SCALE = float(128 ** -0.5)
    nc = bass.Bass("TRN2", target_bir_lowering=False)

    def din(name, shape):
        return nc.dram_tensor(name, list(shape), F32, kind="ExternalInput").ap()

    q_d = din("q", [4, 128, S])
    kc_d = din("kc", [128, S])
    vc_d = din("vc", [128, S])
    ks_d = din("ks", [128, S])
    kw_d = din("kw", [128, S])
    vs_d = din("vs", [128, NT, 128])
    vw_d = din("vw", [128, NT, 128])
    gate_d = din("gate", [12, S])
    gq_d = din("gq", [128, 1])
    gk_d = din("gk", [128, 3])
    pe_d = din("peT", [2, 128, 32])
    w1_d = din("w1", [2, 128, 32, 128])
    w2_d = din("w2", [2, 128, 128])
    cos_d = din("cosf", [128, S])
    sin_d = din("sinf", [128, S])
    cosc_d = din("cosc", [128, 256])
    sinc_d = din("sinc", [128, 256])
    R_d = din("Rm", [128, 128])
    id_d = din("ident", [128, 128])
    bd_d = din("bdiag", [128, 512])
    bp_d = din("bprev", [128, 512])
    cm_d = din("cmask", [128, 48, 128])
    bon_d = din("bonus", [128, NT, 64])
    E_d = din("Emat", [64, NT, 128])
    ov_d = din("ov", [128, 2, 64])
    out_d = nc.dram_tensor("oT", [512, S], F32, kind="ExternalOutput").ap()
    with ExitStack() as stack:
        P = Prog(nc, stack)
        qbf = P.sb([128, 4, S], BF16, "qbf")
        ksbf = P.sb([128, S], BF16, "ksbf")
        kwbf = P.sb([128, S], BF16, "kwbf")
        vsb = P.sb([128, NT, 128], BF16, "vsb")
        vwb = P.sb([128, NT, 128], BF16, "vwb")
        kcm = P.sb([128, 256], BF16, "kcm")
        vcm = P.sb([128, 2, 128], BF16, "vcm")
        R_sb = P.sb([128, 128], F32, "R_sb")
        identf = P.sb([128, 128], F32, "identf")
        ident = P.sb([128, 128], BF16, "identb")
        bdiag = P.sb([128, 512], BF16, "bdiag")
        bprev = P.sb([128, 512], BF16, "bprev")
        cmask = P.sb([128, 48, 128], BF16, "cmask")
        bonus = P.sb([128, NT, 64], F32, "bonus")
        Emat = P.sb([64, NT, 128], BF16, "Emat")
        ov = P.sb([128, 2, 64], BF16, "ov")
        onesb = P.sb([128, 128], BF16, "onesb")
        gq = P.sb([128, 1], F32, "gq")
        gk = P.sb([128, 3], F32, "gk")
        eps_col = P.sb([128, 1], F32, "eps_col")
        region = P.sb([128, 12288], F32, "region")
        psbig = [P.ps([128, 2048], F32, f"psbig{i}") for i in range(2)]

        def bank(i):
            return psbig[i // 4][:, (i % 4) * 512:(i % 4 + 1) * 512]

        roff = [0]

        def rsb(shape, dtype):
            exact = int(np.prod(shape[1:])) * (2 if dtype == BF16 else 4)
            assert exact % 4 == 0
            nbytes = (exact + 31) // 32 * 32
            a = roff[0] // 4
            v = region[0:shape[0], a:a + exact // 4]
            roff[0] += nbytes
            assert roff[0] <= 12288 * 4, roff[0]
            if dtype == BF16:
                v = v.bitcast(BF16)
            if len(shape) == 3:
                v = v.rearrange("p (a b) -> p a b", a=shape[1])
            return v

        P.add("pool", lambda e: e.memset(eps_col[:], float(NORM_EPS)), w=["epsc"])
        P.add("pool", lambda e: e.memset(onesb[:], 1.0), w=["onesblk"])
        for t_, d_, k_ in ((R_sb, R_d, "Rm"), (identf, id_d, "identf"), (gq, gq_d, "gains"), (gk, gk_d, "gains"), (bonus, bon_d, "bonus")):
            P.dma("sp", t_[:], d_, w=[k_])
        for t_, d_, k_ in ((ident, id_d, "ident"), (bdiag, bd_d, "bias"), (bprev, bp_d, "bias"), (cmask, cm_d, "cmask"),
                           (Emat, E_d, "Emat"), (ov, ov_d, "ov"), (vsb, vs_d, "vs"), (vwb, vw_d, "vw")):
            P.dma("pool", t_[:], d_, w=[k_])

        CH = 1024
        work = [rsb([128, CH], F32) for _ in range(2)]
        tcos = [rsb([128, CH], F32) for _ in range(2)]
        tsin = [rsb([128, CH], F32) for _ in range(2)]
        rr = rsb([128, CH], F32)
        tmpf = rsb([128, CH], F32)
        sq = rsb([128, CH], BF16)
        jobs = [(q_d[h], gq[:, 0:1], lambda c0, h=h: qbf[:, h, c0:c0 + CH]) for h in range(4)]
        jobs.append((ks_d, gk[:, 1:2], lambda c0: ksbf[:, c0:c0 + CH]))
        jobs.append((kw_d, gk[:, 2:3], lambda c0: kwbf[:, c0:c0 + CH]))
        n1 = 0
        for (src_d, gcol, dstf) in jobs:
            for ci in range(S // CH):
                w_ = n1 % 2
                n1 += 1
                c0 = ci * CH
                P.dma("sp", work[w_], src_d[:, c0:c0 + CH], w=[f"w{w_}"])
                P.dma("sp", tcos[w_], cos_d[:, c0:c0 + CH], w=["tabs"])
                P.dma("sp", tsin[w_], sin_d[:, c0:c0 + CH], w=["tabs"])
                qk_norm_rope(P, work[w_], dstf(c0), gcol, onesb, R_sb, tcos[w_], tsin[w_], sq, rr, tmpf, psbig, eps_col, 1.0 / 128,
                             f"w{w_}", CH)
        P.barrier()

        roff[0] = 0
        kcbf = rsb([128, S], BF16)
        vcbf = rsb([128, S], BF16)
        w1b = [rsb([128, 32, 128], BF16) for _ in range(2)]
        w2b = [rsb([128, 128], BF16) for _ in range(2)]
        peb = [rsb([128, 32], BF16) for _ in range(2)]
        ccol = [rsb([128, 1], F32) for _ in range(2)]
        xg = rsb([128, 256], F32)
        x2 = rsb([128, 256], F32)
        th = rsb([128, 256], F32)
        gel = rsb([128, 256], BF16)
        kcmf = rsb([128, 256], F32)
        tcc = rsb([128, 256], F32)
        tsc = rsb([128, 256], F32)
        rr2 = rsb([128, 256], F32)
        tmp2 = rsb([128, 256], F32)
        sq2 = rsb([128, 256], BF16)
        P.dma("pool", kcbf, kc_d, w=["kcbf"])
        P.dma("pool", vcbf, vc_d, w=["vcbf"])
        P.dma("sp", tcc, cosc_d, w=["tabs2"])
        P.dma("sp", tsc, sinc_d, w=["tabs2"])
        for i in range(2):
            P.dma("pool", w1b[i], w1_d[i], w=[("w1", i)])
            P.dma("pool", w2b[i], w2_d[i], w=[("w2", i)])
            P.dma("pool", peb[i], pe_d[i], w=[("pe", i)])
        for i, srcbf, skey in ((0, kcbf, "kcbf"), (1, vcbf, "vcbf")):
            pc = bank(0)
            pv = bank(1)
            for l in range(32):
                P.add("pe", lambda e, i=i, l=l, pc=pc: e.matmul(pc[:, 0:1], lhsT=w1b[i][:, l, :], rhs=peb[i][:, l:l + 1],
                                                             start=(l == 0), stop=(l == 31)),
                      r=[("w1", i), ("pe", i)], w=["pc"])
            P.add("act", lambda e, i=i, pc=pc: e.copy(out=ccol[i], in_=pc[:, 0:1]), r=[], w=["pc", ("ccol", i)])
            for l in range(32):
                P.add("pe", lambda e, i=i, l=l, pv=pv, srcbf=srcbf: e.matmul(
                    pv[:, 0:NSA_NCMP], lhsT=w1b[i][:, l, :], rhs=srcbf[:, l:l + 16 * (NSA_NCMP - 1) + 1:16],
                    start=(l == 0), stop=(l == 31)), r=[("w1", i), skey], w=["pv"])
            P.add("pool", lambda e: e.memset(xg, 0.0), w=["xg"])
            P.add("act", lambda e, i=i, pv=pv: e.activation(out=xg[:, 0:NSA_NCMP], in_=pv[:, 0:NSA_NCMP], func=AF.Identity, bias=ccol[i]),
                  r=[("ccol", i)], w=["pv", "xg"])
            P.add("pool", lambda e: e.tensor_tensor(out=x2, in0=xg, in1=xg, op=ALU.mult), r=["xg"], w=["x2"])
            P.add("pool", lambda e: e.tensor_scalar(out=x2, in0=x2, scalar1=0.044715, scalar2=1.0, op0=ALU.mult, op1=ALU.add),
                  r=["x2"], w=["x2"])
            P.add("pool", lambda e: e.tensor_tensor(out=x2, in0=x2, in1=xg, op=ALU.mult), r=["x2", "xg"], w=["x2"])
            P.add("act", lambda e: e.activation(out=th, in_=x2, func=AF.Tanh, scale=float(np.sqrt(2.0 / np.pi))), r=["x2"], w=["th"])
            P.add("pool", lambda e: e.tensor_scalar(out=th, in0=th, scalar1=1.0, scalar2=0.5, op0=ALU.add, op1=ALU.mult), r=["th"], w=["th"])
            P.add("pool", lambda e: e.tensor_tensor(out=gel, in0=th, in1=xg, op=ALU.mult), r=["th", "xg"], w=["gel"])
            if i == 0:
                pk2 = bank(2)
                P.add("pe", lambda e, pk2=pk2: e.matmul(pk2[:, 0:256], lhsT=w2b[0], rhs=gel, start=True, stop=True),
                      r=[("w2", 0), "gel"], w=["pk2"])
                P.add("act", lambda e, pk2=pk2: e.copy(out=kcmf, in_=pk2[:, 0:256]), r=[], w=["pk2", "kcmf"])
                qk_norm_rope(P, kcmf, kcm[:], gk[:, 0:1], onesb, R_sb, tcc, tsc, sq2, rr2, tmp2, [psbig[1]], eps_col, 1.0 / 128, "kcmf", 256)
            else:
                pk2 = bank(3)
                for ct in range(2):
                    P.add("pe", lambda e, ct=ct, pk2=pk2: e.matmul(pk2[:, ct * 128:(ct + 1) * 128], lhsT=gel[:, ct * 128:(ct + 1) * 128],
                                                                   rhs=w2b[1], start=(ct == 0), stop=True),
                          r=[("w2", 1), "gel"], w=["pk3"])
                P.add("act", lambda e, pk2=pk2: e.copy(out=vcm[:], in_=pk2[:, 0:256].rearrange("p (a b) -> p a b", a=2)),
                      r=[], w=["pk3", "vcm"])
        if dbg:
            P.dump("d_kcm", kcm[:], ["kcmf_bf"])
            P.dump("d_vcm", vcm[:], ["vcm"])
            P.dump("d_qbf", qbf[:, :, 0:256], ["w0_bf", "w1_bf"])
        P.barrier()

        roff[0] = 0
        pt = [rsb([128, 512], BF16) for _ in range(4)]
        G = [rsb([128, 12, 128], F32) for _ in range(2)]
        rden = [rsb([128, 512], F32) for _ in range(2)]
        fac = [rsb([128, 512], F32) for _ in range(2)]
        oft = [rsb([128, 512], F32) for _ in range(2)]
        ostage = [rsb([128, 4, 512], F32) for _ in range(2)]
        tmpU = rsb([64, 512], F32)
        impT = rsb([64, 128], F32)
        score = rsb([128, 64], F32)
        wk = rsb([128, 64], F32)
        m1 = rsb([128, 8], F32)
        m2 = rsb([128, 8], F32)
        negb = rsb([128, 64], F32)
        selT = [rsb([64, 128], BF16) for _ in range(2)]
        tiny = rsb([128, 1], F32)
        P.add("pool", lambda e: e.memset(tiny, 1e-30), w=["tiny"])
        cnt = dict(s=0, pt=0, o=0)
        qall = ["w0_bf", "w1_bf"]

        pend = []

        def flush():
            while pend:
                pend.pop(0)()

        def branch(qt, kts, kfn, vfn, maskfn, extra=None, extra_key=None):
            oset = cnt["o"] % 2
            cnt["o"] += 1
            ob, db = 4 + 2 * oset, 5 + 2 * oset
            okey = ("oset", oset)
            nk = len(kts)
            for i, kt in enumerate(kts):
                sl = cnt["s"] % 2
                cnt["s"] += 1
                sps = bank(sl)
                skey = ("sps", sl)
                p_ = cnt["pt"] % 4
                cnt["pt"] += 1
                masks = maskfn(kt)
                kl, kr = kfn(kt)
                P.add("pe", lambda e, sps=sps, kl=kl, qt=qt, masks=masks: e.matmul(
                    sps, lhsT=kl, rhs=qbf[:, :, qt * 128:(qt + 1) * 128], start=True, stop=(len(masks) == 0)),
                    r=qall + kr, w=[skey])
                for mi, (ml, mr, mk) in enumerate(masks):
                    P.add("pe", lambda e, sps=sps, ml=ml, mr=mr, mi=mi, masks=masks: e.matmul(
                        sps, lhsT=ml, rhs=mr, start=False, stop=(mi == len(masks) - 1)), r=mk, w=[skey])
                P.add("act", lambda e, sps=sps, p_=p_: e.activation(out=pt[p_], in_=sps, func=AF.Exp, scale=SCALE),
                      r=[], w=[skey, ("pt", p_)])
                flush()
                vl, vr = vfn(kt)

                def pv(ob=ob, db=db, vl=vl, vr=vr, p_=p_, i=i, nk=nk, kt=kt, okey=okey):
                    P.add("pe", lambda e: e.matmul(bank(ob), lhsT=vl, rhs=pt[p_], start=(i == 0), stop=(i == nk - 1)),
                          r=[("pt", p_)] + vr, w=[okey])
                    P.add("pe", lambda e: e.matmul(bank(db), lhsT=onesb[:], rhs=pt[p_], start=(i == 0), stop=(i == nk - 1)),
                          r=[("pt", p_), "onesblk"], w=[okey])
                    if extra is not None:
                        extra(kt, i, nk, p_)
                pend.append(pv)
            return ob, db, okey, oset

        def combine(qt, br, ob, db, okey, oset, add_tiny):
            st = ostage[(qt // 4) % 2]
            stkey = ("ostage", (qt // 4) % 2)
            gb = qt % 2
            dst = st[:, :, (qt % 4) * 128:(qt % 4 + 1) * 128]
            if add_tiny:
                P.add("dve", lambda e, db=db, oset=oset: e.tensor_scalar(out=rden[oset], in0=bank(db), scalar1=tiny[:, 0:1], scalar2=None, op0=ALU.add),
                      r=["tiny"], w=[okey, ("rden", oset)])
                P.add("dve", lambda e, oset=oset: e.reciprocal(out=rden[oset], in_=rden[oset]), r=[], w=[("rden", oset)])
            else:
                P.add("dve", lambda e, db=db, oset=oset: e.reciprocal(out=rden[oset], in_=bank(db)), r=[], w=[okey, ("rden", oset)])
            P.add("pool", lambda e, oset=oset, gb=gb, br=br: e.tensor_tensor(
                out=fac[oset].rearrange("p (h q) -> p h q", h=4), in0=rden[oset].rearrange("p (h q) -> p h q", h=4),
                in1=G[gb].rearrange("p (h b) q -> p h b q", b=3)[:, :, br, :], op=ALU.mult),
                r=[("rden", oset), ("G", gb)], w=[("fac", oset)])
            if br == 0:
                P.add("dve", lambda e, ob=ob, oset=oset, dst=dst: e.tensor_tensor(
                    out=dst, in0=bank(ob).rearrange("p (h q) -> p h q", h=4), in1=fac[oset].rearrange("p (h q) -> p h q", h=4), op=ALU.mult),
                    r=[("fac", oset)], w=[okey, stkey])
            else:
                P.add("dve", lambda e, ob=ob, oset=oset: e.tensor_tensor(out=oft[oset], in0=bank(ob), in1=fac[oset], op=ALU.mult),
                      r=[("fac", oset)], w=[okey, ("oft", oset)])
                P.add("pool", lambda e, oset=oset, dst=dst: e.tensor_tensor(
                    out=dst, in0=dst, in1=oft[oset].rearrange("p (h q) -> p h q", h=4), op=ALU.add),
                    r=[("oft", oset)], w=[stkey])

        for qt in range(NT):
            gb = qt % 2
            tok = slice(qt * 128, (qt + 1) * 128)
            P.dma("sp", G[gb], gate_d[:, tok].partition_broadcast(128), w=[("G", gb)])
            P.add("act", lambda e, gb=gb: e.activation(out=G[gb], in_=G[gb], func=AF.Sigmoid), r=[], w=[("G", gb)])
            cts = [0] if qt < 16 else [0, 1]
            ub = bank(3)

            def cmask_fn(ct, qt=qt):
                idx = qt if ct == 0 else 32 + qt - 16
                return [(ident[:], cmask[:, idx, :].unsqueeze(1).to_broadcast([128, 4, 128]), ["ident", "cmask"])]

            def uextra(ct, i, nk, p_, ub=ub):
                P.add("pe", lambda e, ct=ct, i=i, nk=nk, p_=p_: e.matmul(ub[0:64, :], lhsT=ov[:, ct, :], rhs=pt[p_], start=(i == 0), stop=(i == nk - 1)),
                      r=[("pt", p_), "ov"], w=["ub"])

            ob, db, okey, oset = branch(qt, cts, lambda ct: (kcm[:, ct * 128:(ct + 1) * 128], ["kcmf_bf"]),
                                        lambda ct: (vcm[:, ct, :], ["vcm"]), cmask_fn, extra=uextra)
            flush()
            combine(qt, 0, ob, db, okey, oset, True)
            P.add("dve", lambda e, ub=ub, oset=oset: e.tensor_tensor(out=tmpU, in0=ub[0:64, :], in1=rden[oset][0:64, :], op=ALU.mult),
                  r=[("rden", oset)], w=["ub", "tmpU"])
            P.add("dve", lambda e: e.tensor_reduce(out=impT, in_=tmpU.rearrange("p (h q) -> p q h", h=4), axis=AX.X, op=ALU.add),
                  r=["tmpU"], w=["impT"])
            tb = bank(2)
            P.add("pe", lambda e, tb=tb: e.transpose(tb[:, 0:64], impT, identf[0:64, 0:64]), r=["impT", "identf"], w=["tb"])
            P.add("dve", lambda e, tb=tb, qt=qt: e.tensor_tensor(out=score, in0=tb[:, 0:64], in1=bonus[:, qt, :], op=ALU.add),
                  r=["bonus"], w=["tb", "score"])
            P.add("dve", lambda e: e.max(out=m1, in_=score), r=["score"], w=["m1"])
            P.add("dve", lambda e: e.match_replace(out=wk, in_to_replace=m1, in_values=score, imm_value=-3.0e38), r=["score", "m1"], w=["wk"])
            P.add("dve", lambda e: e.max(out=m2, in_=wk), r=["wk"], w=["m2"])
            P.add("dve", lambda e: e.tensor_scalar(out=negb, in0=score, scalar1=m2[:, 7:8], scalar2=None, op0=ALU.is_ge), r=["score", "m2"], w=["negb"])
            P.add("dve", lambda e: e.tensor_scalar(out=negb, in0=negb, scalar1=1.0, scalar2=-NEGB, op0=ALU.subtract, op1=ALU.mult),
                  r=["negb"], w=["negb"])
            P.add("pe", lambda e, tb=tb: e.transpose(tb[0:64, 128:256], negb, identf[:]), r=["negb", "identf"], w=["tb"])
            sT = selT[qt % 2]
            P.add("act", lambda e, tb=tb, sT=sT: e.copy(out=sT, in_=tb[0:64, 128:256]), r=[], w=["tb", ("selT", qt % 2)])
            if dbg and qt in (3, 20):
                P.dump(f"d_score{qt}", score, ["score"])
                P.dump(f"d_negb{qt}", negb, ["negb"])
                P.dump(f"d_selT{qt}", sT, [("selT", qt % 2)])

            def win_mask(kt, qt=qt):
                if kt == qt:
                    return [(ident[:], bdiag[:], ["ident", "bias"])]
                if kt == qt - 4:
                    return [(ident[:], bprev[:], ["ident", "bias"])]
                return []

            ob, db, okey, oset = branch(qt, [kt for kt in range(qt - 4, qt + 1) if kt >= 0],
                                        lambda kt: (kwbf[:, kt * 128:(kt + 1) * 128], ["w0_bf", "w1_bf"]),
                                        lambda kt: (vwb[:, kt, :], ["vw"]), win_mask)
            pend.append(lambda qt=qt, ob=ob, db=db, okey=okey, oset=oset: combine(qt, 2, ob, db, okey, oset, False))
            def sel_mask(kt, qt=qt, sT=sT):
                ms = [(Emat[:, kt, :], sT.unsqueeze(1).to_broadcast([64, 4, 128]), ["Emat", ("selT", qt % 2)])]
                if kt == qt:
                    ms.append((ident[:], bdiag[:], ["ident", "bias"]))
                return ms

            ob, db, okey, oset = branch(qt, list(range(qt + 1)), lambda kt: (ksbf[:, kt * 128:(kt + 1) * 128], ["w0_bf", "w1_bf"]),
                                        lambda kt: (vsb[:, kt, :], ["vs"]), sel_mask)

            def fin(qt=qt, ob=ob, db=db, okey=okey, oset=oset):
                combine(qt, 1, ob, db, okey, oset, False)
                if qt % 4 == 3:
                    g4 = qt // 4
                    P.dma("sp", out_d.rearrange("(h d) t -> d h t", h=4)[:, :, g4 * 512:(g4 + 1) * 512], ostage[g4 % 2],
                          r=[("ostage", g4 % 2)], w=[("out", g4)])
            pend.append(fin)
        flush()
        P.finish([("out", g4) for g4 in range(NT // 4)])
        P.emit()
    return nc


def run_nsaB(projT, q_norm, k_norm, cmp_pe, cmp_w1, cmp_w2, trace=False, dbg=False, cores=None):
    S = SEQ
    NT = NSA_NT
    nc = get_prog(("nsaB", dbg), lambda: build_nsaB(dbg))
    cosf, sinf = rope_tables(128, np.arange(S), 1)
    cend = np.arange(256) * 16 + 31
    cosc, sinc = rope_tables(128, cend, 1)
    Rm = rope_matrix(128, 1)
    ident = np.eye(128, dtype=np.float32)
    bdiag, bprev = mask_biases()
    cm, bonus, E, ov = nsa_consts()
    gq = q_norm.astype(np.float32).reshape(128, 1)
    gk = np.ascontiguousarray(k_norm.astype(np.float32).T)
    peT = np.ascontiguousarray(cmp_pe.transpose(0, 2, 1))
    w1 = np.ascontiguousarray(cmp_w1.reshape(2, 32, 128, 128).transpose(0, 2, 1, 3))
    w2 = np.ascontiguousarray(cmp_w2)
    in_maps = []
    clist = list(range(NCORES)) if cores is None else cores
    for c in clist:
        b, g = c // 4, c % 4
        ts = slice(b * S, (b + 1) * S)

        def rows(base):
            return np.ascontiguousarray(projT[base + g * 128:base + (g + 1) * 128, ts])

        def tm(base):
            return np.ascontiguousarray(projT[base + g * 128:base + (g + 1) * 128, ts].T.reshape(NT, 128, 128).transpose(1, 0, 2))

        q = np.ascontiguousarray(projT[g * 512:(g + 1) * 512, ts].reshape(4, 128, S))
        gate = np.ascontiguousarray(projT[5120 + g * 12:5120 + (g + 1) * 12, ts])
        in_maps.append(dict(q=q, kc=rows(2048), vc=rows(2560), ks=rows(3072), vs=tm(3584), kw=rows(4096), vw=tm(4608), gate=gate,
                            gq=gq, gk=gk, peT=peT, w1=w1, w2=w2, cosf=cosf, sinf=sinf, cosc=cosc, sinc=sinc, Rm=Rm, ident=ident,
                            bdiag=bdiag, bprev=bprev, cmask=cm, bonus=bonus, Emat=E, ov=ov))
    res = run_bass_kernel_spmd(nc, in_maps, core_ids=list(range(len(clist))), trace=trace)
    if trace:
        print("nsaB exec_time_ns", res.exec_time_ns)
    if dbg:
        return res.results
    oT = np.zeros((D_MODEL, BATCH * S), np.float32)
    for i, c in enumerate(clist):
        b, g = c // 4, c % 4
        oT[g * 512:(g + 1) * 512, b * S:(b + 1) * S] = res.results[i]["oT"]
    return oT


def kernel(x, norm_mix, norm_mlp, mlp_w_up, mlp_w_down,
           nsa_w_in, nsa_w_out, nsa_q_norm, nsa_k_norm, nsa_cmp_pe, nsa_cmp_w1, nsa_cmp_w2,
           gla_w_in, gla_w_gate_up, gla_b_gate, gla_o_norm, gla_w_out,
           swa_w_in, swa_w_out, swa_q_norm, swa_k_norm, swa_sinks):
    f = lambda a: np.asarray(a, dtype=np.float32)
    x = f(x)
    xT = np.ascontiguousarray(x.reshape(BATCH * SEQ, D_MODEL).T)
    idx = {0: 0, 1: 0, 2: 0}
    w_ins = []
    for i in range(DEPTH):
        kind = i % 3
        w_ins.append(f((nsa_w_in, gla_w_in, swa_w_in)[kind][idx[kind]]))
        idx[kind] += 1
    idx = {0: 0, 1: 0, 2: 0}
    projT = run_phaseA(xT, f(norm_mix[0]), w_ins[0])
    for i in range(DEPTH):
        kind = i % 3
        j = idx[kind]
        idx[kind] += 1
        if kind == 0:
            oT = run_nsaB(projT, f(nsa_q_norm[j]), f(nsa_k_norm[j]), f(nsa_cmp_pe[j]), f(nsa_cmp_w1[j]), f(nsa_cmp_w2[j]))
            w_out = f(nsa_w_out[j])
        elif kind == 1:
            oT = run_glaB(projT, f(gla_w_gate_up[j]), f(gla_b_gate[j]), f(gla_o_norm[j]))
            w_out = f(gla_w_out[j])
        else:
            oT = run_swaB(projT, f(swa_q_norm[j]), f(swa_k_norm[j]), f(swa_sinks[j]))
            w_out = f(swa_w_out[j])
        if i + 1 < DEPTH:
            xT, projT = run_phaseC(xT, oT, w_out, f(norm_mlp[i]), f(mlp_w_up[i]), f(mlp_w_down[i]),
                                   gain2=f(norm_mix[i + 1]), w_in2=w_ins[i + 1])
        else:
            xT = run_phaseC(xT, oT, w_out, f(norm_mlp[i]), f(mlp_w_up[i]), f(mlp_w_down[i]))
    return np.ascontiguousarray(xT.T).reshape(BATCH, SEQ, D_MODEL).astype(np.float32)
```

```python
import numpy as np
from contextlib import ExitStack
import concourse.bass as bass
import concourse.mybir as mybir
from concourse.bass_utils import run_bass_kernel_spmd

F32 = mybir.dt.float32
BF16 = mybir.dt.bfloat16
ALU = mybir.AluOpType
AF = mybir.ActivationFunctionType
AX = mybir.AxisListType

D_MODEL = 2048
BATCH = 2
SEQ = 4096
DEPTH = 4
D_FF = 4 * D_MODEL
NORM_EPS = 1e-6
NCORES = 8
TOK = BATCH * SEQ // NCORES
KC = D_MODEL // 128

SEM_LIM = 16000
NDMASEM = 8


class Prog:
    ENGS = ("pe", "act", "dve", "pool", "sp")

    def __init__(self, nc, stack):
        self.nc = nc
        self.stack = stack
        self.ops = []
        self.lastw = {}
        self.readers = {}
        self._n = 0
        self.fence = None

    def sb(self, shape, dtype, name=None):
        self._n += 1
        return self.stack.enter_context(self.nc.sbuf_tensor((name + "_s") if name else f"sb{self._n}", list(shape), dtype))

    def ps(self, shape, dtype=F32, name=None):
        self._n += 1
        return self.stack.enter_context(self.nc.psum_tensor(name or f"ps{self._n}", list(shape), dtype))

    def add(self, eng, fn, r=(), w=(), dma=False):
        oid = len(self.ops)
        deps = set()
        for k in r:
            if k in self.lastw:
                deps.add(self.lastw[k])
        for k in w:
            if k in self.lastw:
                deps.add(self.lastw[k])
            deps.update(self.readers.get(k, ()))
        if self.fence is not None:
            deps.add(self.fence)
        for k in r:
            self.readers.setdefault(k, []).append(oid)
        for k in w:
            self.lastw[k] = oid
            self.readers[k] = []
        self.ops.append(dict(eng=eng, fn=fn, deps=deps, dma=dma))
        return oid

    def barrier(self):
        if not hasattr(self, "_fdummy"):
            self._fdummy = self.sb([1, 8], F32, "fence_dummy")
        keys = list(set(self.lastw.keys()) | set(self.readers.keys()))
        d = self._fdummy
        self.fence = None
        self.fence = self.add("pool", lambda e: e.memset(d[:], 0.0), r=(), w=keys)

    def dma(self, eng, out, in_, r=(), w=()):
        return self.add(eng, lambda e: e.dma_start(out=out, in_=in_), r=r, w=w, dma=True)

    def dump(self, name, ap, r):
        d = self.nc.dram_tensor(name, list(ap.shape), ap.dtype, kind="ExternalOutput").ap()
        self.dma("sp", d, ap, r=r, w=[("dump", name)])
        self.dumps = getattr(self, "dumps", []) + [("dump", name)]

    def finish(self, r):
        r = list(r) + getattr(self, "dumps", [])
        self.add("sp", None, r=r, w=())

    def emit(self):
        nc = self.nc
        ops = self.ops
        has_dep = [False] * len(ops)
        for op in ops:
            for d in op["deps"]:
                has_dep[d] = True
        cnt = {e: 0 for e in self.ENGS}
        dcnt = {e: 0 for e in self.ENGS}
        nsem = {}
        for i, op in enumerate(ops):
            e = op["eng"]
            if op["dma"]:
                j = dcnt[e]
                dcnt[e] += 1
                op["sig"] = (("d", e, j % NDMASEM), 16 * (j // NDMASEM + 1), 16)
                op["prev"] = (("d", e, j % NDMASEM), 16 * (j // NDMASEM)) if j >= NDMASEM else None
            elif has_dep[i] and op["fn"] is not None:
                c = cnt[e]
                cnt[e] += 1
                op["sig"] = (("c", e, c // SEM_LIM), c % SEM_LIM + 1, 1)
            else:
                op["sig"] = None
        keys = set()
        for op in ops:
            if op["sig"]:
                keys.add(op["sig"][0])
        sems = {}
        for k in sorted(keys):
            sems[k] = self.stack.enter_context(nc.semaphore("s_" + "_".join(str(x) for x in k)))
        by_eng = {e: [] for e in self.ENGS}
        for i, op in enumerate(ops):
            by_eng[op["eng"]].append(i)
        bname = {"pe": "tensor", "act": "scalar", "dve": "vector", "pool": "gpsimd", "sp": "sync"}
        self.n_wait = 0
        with nc.Block() as block:
            for E in self.ENGS:
                if not by_eng[E]:
                    continue

                def body(eng, E=E):
                    seen = {}
                    for i in by_eng[E]:
                        op = ops[i]
                        waits = {}
                        for d in op["deps"]:
                            dop = ops[d]
                            if dop["sig"] is None:
                                continue
                            if dop["eng"] == E and E == "pe" and not dop["dma"]:
                                continue
                            k, v, _ = dop["sig"]
                            if waits.get(k, 0) < v:
                                waits[k] = v
                        if op["dma"] and op["prev"] is not None:
                            k, v = op["prev"]
                            if waits.get(k, 0) < v:
                                waits[k] = v
                        for k, v in waits.items():
                            if seen.get(k, 0) >= v:
                                continue
                            seen[k] = v
                            eng.wait_ge(sems[k], v)
                            self.n_wait += 1
                        if op["fn"] is None:
                            continue
                        ins = op["fn"](eng)
                        if op["sig"] is not None:
                            k, v, inc = op["sig"]
                            ins.then_inc(sems[k], inc)

                getattr(block, bname[E])(body)


def rms_to_hT(P, xT, gain_sb, hT, ones_bf, psA, psB, scratch_bf, rr, eps_col, tag, gkey="gain"):
    nc = P.nc
    pss = [psA, psB]
    for kc in range(KC):
        sl = kc % 2
        P.add("act", lambda e, kc=kc, sl=sl: e.activation(out=scratch_bf[sl][:], in_=xT[:, kc, :], func=AF.Square),
              r=[("xT", kc)], w=[("sq", sl)])
        for hf in range(2):
            P.add("pe", lambda e, kc=kc, sl=sl, hf=hf: e.matmul(
                pss[hf][:], lhsT=ones_bf[:], rhs=scratch_bf[sl][:, hf * 512:(hf + 1) * 512],
                start=(kc == 0), stop=(kc == KC - 1)),
                r=[("sq", sl), "ones"], w=[("ps", tag, hf)])
    for hf in range(2):
        P.add("act", lambda e, hf=hf: e.activation(
            out=rr[:, hf * 512:(hf + 1) * 512], in_=pss[hf][:], func=AF.Sqrt, bias=eps_col[:], scale=1.0),
            r=[("ps", tag, hf), "epsc"], w=[("rr", hf)])
        P.add("dve", lambda e, hf=hf: e.reciprocal(out=rr[:, hf * 512:(hf + 1) * 512], in_=rr[:, hf * 512:(hf + 1) * 512]),
              r=[("rr", hf)], w=[("rr", hf)])
    for kc in range(KC):
        P.add("dve", lambda e, kc=kc: e.scalar_tensor_tensor(
            out=hT[:, kc, :], in0=xT[:, kc, :], scalar=gain_sb[:, kc:kc + 1], in1=rr[:],
            op0=ALU.mult, op1=ALU.mult), r=[("xT", kc), ("rr", 0), ("rr", 1), gkey], w=[("hT", kc)])


def build_phaseA(nch):
    nc = bass.Bass("TRN2", target_bir_lowering=False)
    xT_d = nc.dram_tensor("xT", [128, KC, TOK], F32, kind="ExternalInput").ap()
    gain_d = nc.dram_tensor("gain", [128, KC], F32, kind="ExternalInput").ap()
    w_d = nc.dram_tensor("w", [nch, 128, KC, 128], F32, kind="ExternalInput").ap()
    out_d = nc.dram_tensor("projT", [nch * 128, TOK], F32, kind="ExternalOutput").ap()
    with ExitStack() as stack:
        P = Prog(nc, stack)
        xT = P.sb([128, KC, TOK], F32, "xT_sb")
        hT = P.sb([128, KC, TOK], BF16, "hT_sb")
        gain_sb = P.sb([128, KC], F32, "gain_sb")
        ones_bf = P.sb([128, 128], BF16, "ones_bf")
        sq = [P.sb([128, TOK], BF16, f"sq{i}") for i in range(2)]
        rr = P.sb([128, TOK], F32, "rr")
        NW = 4
        wt = [P.sb([128, KC, 128], BF16, f"wt{i}") for i in range(NW)]
        NO = 3
        ot = [P.sb([128, TOK], F32, f"ot{i}") for i in range(NO)]
        ps = [P.ps([128, 512], F32, f"psb{i}") for i in range(8)]

        P.add("pool", lambda e: e.memset(ones_bf[:], 1.0), w=["ones"])
        eps_col = P.sb([128, 1], F32, "eps_col")
        P.add("pool", lambda e: e.memset(eps_col[:], float(D_MODEL * NORM_EPS)), w=["epsc"])
        P.dma("sp", gain_sb[:], gain_d, w=["gain"])
        for kc in range(KC):
            P.dma("sp", xT[:, kc, :], xT_d[:, kc, :], w=[("xT", kc)])
        P.add("act", lambda e: e.mul(out=gain_sb[:], in_=gain_sb[:], mul=float(np.sqrt(D_MODEL))), r=["gain"], w=["gain"])
        rms_to_hT(P, xT, gain_sb, hT, ones_bf, ps[0], ps[1], sq, rr, eps_col, "n")
        allh = [("hT", kc) for kc in range(KC)]
        for j in range(nch):
            s = j % NW
            P.dma("pool", wt[s][:], w_d[j], w=[("wt", s)])
            pb = (j % 3) * 2 + 2
            for kc in range(KC):
                for hf in range(2):
                    P.add("pe", lambda e, s=s, kc=kc, hf=hf, pb=pb: e.matmul(
                        ps[pb + hf][:], lhsT=wt[s][:, kc, :], rhs=hT[:, kc, hf * 512:(hf + 1) * 512],
                        start=(kc == 0), stop=(kc == KC - 1)),
                        r=[("wt", s)] + (allh if kc == 0 else []), w=[("ps", pb + hf)])
            o = j % NO
            P.add("act", lambda e, o=o, pb=pb: e.copy(out=ot[o][:, 0:512], in_=ps[pb][:]), r=[("ps", pb)], w=[("ot", o, 0)])
            P.add("dve", lambda e, o=o, pb=pb: e.tensor_copy(out=ot[o][:, 512:1024], in_=ps[pb + 1][:]), r=[("ps", pb + 1)], w=[("ot", o, 1)])
            P.dma("sp", out_d[j * 128:(j + 1) * 128, :], ot[o][:], r=[("ot", o, 0), ("ot", o, 1)], w=[("out", j)])
        P.finish([("out", j) for j in range(nch)])
        P.emit()
    return nc


def to_fm(a2d):
    r, c = a2d.shape
    return np.ascontiguousarray(a2d.reshape(r // 128, 128, c).transpose(1, 0, 2))


def prep_w_in(w, nch):
    d, n = w.shape
    wp = np.zeros((d, nch * 128), np.float32)
    wp[:, :n] = w
    return np.ascontiguousarray(wp.reshape(KC, 128, nch, 128).transpose(2, 1, 0, 3))


def run_phaseA(xT_full, gain, w):
    n = w.shape[1]
    nch = (n + 127) // 128
    nc = get_prog(("A", nch), lambda: build_phaseA(nch))
    wl = prep_w_in(w, nch)
    g = np.ascontiguousarray(gain.reshape(KC, 128).T)
    in_maps = []
    for c in range(NCORES):
        in_maps.append({"xT": to_fm(xT_full[:, c * TOK:(c + 1) * TOK]), "gain": g, "w": wl})
    res = run_bass_kernel_spmd(nc, in_maps, core_ids=list(range(NCORES)))
    return np.concatenate([r["projT"] for r in res.results], axis=1)[:n]


FB = 4
NFC = D_FF // 128


def build_phaseC(nch2=0):
    nc = bass.Bass("TRN2", target_bir_lowering=False)
    xT_d = nc.dram_tensor("xT", [128, KC, TOK], F32, kind="ExternalInput").ap()
    oT_d = nc.dram_tensor("oT", [128, KC, TOK], F32, kind="ExternalInput").ap()
    gain_d = nc.dram_tensor("gain", [128, KC], F32, kind="ExternalInput").ap()
    wo_d = nc.dram_tensor("wo", [KC, 128, KC, 128], F32, kind="ExternalInput").ap()
    wu_d = nc.dram_tensor("wu", [NFC, 128, KC, 128], F32, kind="ExternalInput").ap()
    wd_d = nc.dram_tensor("wd", [NFC, 128, D_MODEL], F32, kind="ExternalInput").ap()
    out_d = nc.dram_tensor("xoT", [128, KC, TOK], F32, kind="ExternalOutput").ap()
    if nch2:
        gain2_d = nc.dram_tensor("gain2", [128, KC], F32, kind="ExternalInput").ap()
        w2_d = nc.dram_tensor("w2", [nch2, 128, KC, 128], F32, kind="ExternalInput").ap()
        proj_d = nc.dram_tensor("projT", [nch2 * 128, TOK], F32, kind="ExternalOutput").ap()
    with ExitStack() as stack:
        P = Prog(nc, stack)
        xT = P.sb([128, KC, TOK], F32, "xT_sb")
        hT = P.sb([128, KC, TOK], BF16, "hT_sb")
        gain_sb = P.sb([128, KC], F32, "gain_sb")
        ones_bf = P.sb([128, 128], BF16, "ones_bf")
        eps_col = P.sb([128, 1], F32, "eps_col")
        sq = [P.sb([128, TOK], BF16, f"sq{i}") for i in range(2)]
        rr = P.sb([128, TOK], F32, "rr")
        NW = 4
        wt = [P.sb([128, KC, 128], BF16, f"wt{i}") for i in range(NW)]
        wd = [P.sb([128, D_MODEL], BF16, f"wd{i}") for i in range(2 * FB)]
        if nch2:
            gain2_sb = P.sb([128, KC], F32, "gain2_sb")
            ot2 = [P.sb([128, TOK], F32, f"ot2{i}") for i in range(2)]
        act = [P.sb([128, TOK], BF16, f"act{i}") for i in range(2 * FB)]
        tmp = [P.sb([128, 512], F32, f"tmp{i}") for i in range(2)]
        ps = [P.ps([128, 512], F32, f"psb{i}") for i in range(8)]

        P.add("pool", lambda e: e.memset(ones_bf[:], 1.0), w=["ones"])
        P.add("pool", lambda e: e.memset(eps_col[:], float(D_MODEL * NORM_EPS)), w=["epsc"])
        P.dma("sp", gain_sb[:], gain_d, w=["gain"])
        for kc in range(KC):
            P.dma("sp", xT[:, kc, :], xT_d[:, kc, :], w=[("xT", kc)])
        for kc in range(KC):
            P.dma("pool", hT[:, kc, :], oT_d[:, kc, :], w=[("hT", kc)])
        P.add("act", lambda e: e.mul(out=gain_sb[:], in_=gain_sb[:], mul=float(np.sqrt(D_MODEL))), r=["gain"], w=["gain"])
        allh = [("hT", kc) for kc in range(KC)]
        wcnt = [0]
        pcnt = [0]

        def wtile(src):
            s = wcnt[0] % NW
            wcnt[0] += 1
            P.dma("pool", wt[s][:], src, w=[("wt", s)])
            return s

        def pbank():
            pb = 2 + (pcnt[0] % 3) * 2
            pcnt[0] += 1
            return pb

        for n in range(KC):
            s = wtile(wo_d[n])
            pb = pbank()
            for kc in range(KC):
                for hf in range(2):
                    P.add("pe", lambda e, s=s, kc=kc, hf=hf, pb=pb: e.matmul(
                        ps[pb + hf][:], lhsT=wt[s][:, kc, :], rhs=hT[:, kc, hf * 512:(hf + 1) * 512],
                        start=(kc == 0), stop=(kc == KC - 1)),
                        r=[("wt", s)] + (allh if kc == 0 else []), w=[("ps", pb + hf)])
            for hf in range(2):
                P.add("dve", lambda e, n=n, hf=hf, pb=pb: e.tensor_tensor(
                    out=xT[:, n, hf * 512:(hf + 1) * 512], in0=xT[:, n, hf * 512:(hf + 1) * 512], in1=ps[pb + hf][:], op=ALU.add),
                    r=[("ps", pb + hf), ("xT", n)], w=[("xT", n)])
        rms_to_hT(P, xT, gain_sb, hT, ones_bf, ps[0], ps[1], sq, rr, eps_col, "n")
        for blk in range(NFC // FB):
            par = blk % 2
            for fl in range(FB):
                f = blk * FB + fl
                s = wtile(wu_d[f])
                a = par * FB + fl
                P.dma("pool", wd[a][:], wd_d[f], w=[("wd", a)])
                pb = pbank()
                for kc in range(KC):
                    for hf in range(2):
                        P.add("pe", lambda e, s=s, kc=kc, hf=hf, pb=pb: e.matmul(
                            ps[pb + hf][:], lhsT=wt[s][:, kc, :], rhs=hT[:, kc, hf * 512:(hf + 1) * 512],
                            start=(kc == 0), stop=(kc == KC - 1)),
                            r=[("wt", s)] + (allh if kc == 0 else []), w=[("ps", pb + hf)])
                for hf in range(2):
                    P.add("act", lambda e, hf=hf, pb=pb: e.activation(out=tmp[hf][:], in_=ps[pb + hf][:], func=AF.Relu),
                          r=[("ps", pb + hf)], w=[("tmp", hf)])
                    P.add("act", lambda e, hf=hf, a=a: e.activation(
                        out=act[a][:, hf * 512:(hf + 1) * 512], in_=tmp[hf][:], func=AF.Square),
                        r=[("tmp", hf)], w=[("act", a, hf)])
            for n in range(KC):
                pb = pbank()
                for fl in range(FB):
                    a = par * FB + fl
                    for hf in range(2):
                        P.add("pe", lambda e, a=a, n=n, hf=hf, pb=pb, fl=fl: e.matmul(
                            ps[pb + hf][:], lhsT=wd[a][:, n * 128:(n + 1) * 128], rhs=act[a][:, hf * 512:(hf + 1) * 512],
                            start=(fl == 0), stop=(fl == FB - 1)),
                            r=[("wd", a), ("act", a, hf)], w=[("ps", pb + hf)])
                for hf in range(2):
                    P.add("dve", lambda e, n=n, hf=hf, pb=pb: e.tensor_tensor(
                        out=xT[:, n, hf * 512:(hf + 1) * 512], in0=xT[:, n, hf * 512:(hf + 1) * 512], in1=ps[pb + hf][:], op=ALU.add),
                        r=[("ps", pb + hf), ("xT", n)], w=[("xT", n)])
        for kc in range(KC):
            P.dma("sp", out_d[:, kc, :], xT[:, kc, :], r=[("xT", kc)], w=[("out", kc)])
        fin = [("out", kc) for kc in range(KC)]
        if nch2:
            P.dma("sp", gain2_sb[:], gain2_d, w=["gain2"])
            P.add("act", lambda e: e.mul(out=gain2_sb[:], in_=gain2_sb[:], mul=float(np.sqrt(D_MODEL))), r=["gain2"], w=["gain2"])
            rms_to_hT(P, xT, gain2_sb, hT, ones_bf, ps[0], ps[1], sq, rr, eps_col, "n", gkey="gain2")
            for j in range(nch2):
                s = wtile(w2_d[j])
                pb = pbank()
                for kc in range(KC):
                    for hf in range(2):
                        P.add("pe", lambda e, s=s, kc=kc, hf=hf, pb=pb: e.matmul(
                            ps[pb + hf][:], lhsT=wt[s][:, kc, :], rhs=hT[:, kc, hf * 512:(hf + 1) * 512],
                            start=(kc == 0), stop=(kc == KC - 1)),
                            r=[("wt", s)] + (allh if kc == 0 else []), w=[("ps", pb + hf)])
                o = j % 2
                P.add("act", lambda e, o=o, pb=pb: e.copy(out=ot2[o][:, 0:512], in_=ps[pb][:]), r=[("ps", pb)], w=[("ot2", o, 0)])
                P.add("dve", lambda e, o=o, pb=pb: e.tensor_copy(out=ot2[o][:, 512:1024], in_=ps[pb + 1][:]), r=[("ps", pb + 1)], w=[("ot2", o, 1)])
                P.dma("sp", proj_d[j * 128:(j + 1) * 128, :], ot2[o][:], r=[("ot2", o, 0), ("ot2", o, 1)], w=[("pout", j)])
            fin += [("pout", j) for j in range(nch2)]
        P.finish(fin)
        P.emit()
    return nc


_PROG_CACHE = {}


def get_prog(key, builder):
    if key not in _PROG_CACHE:
        _PROG_CACHE[key] = builder()
    return _PROG_CACHE[key]


def run_phaseC(xT_full, oT_full, w_out, gain, w_up, w_down, trace=False, gain2=None, w_in2=None):
    nch2 = 0 if w_in2 is None else (w_in2.shape[1] + 127) // 128
    nc = get_prog(("C", nch2), lambda: build_phaseC(nch2))
    wo = prep_w_in(w_out, KC)
    wu = prep_w_in(w_up, NFC)
    wdl = np.ascontiguousarray(w_down.reshape(NFC, 128, D_MODEL))
    g = np.ascontiguousarray(gain.reshape(KC, 128).T)
    extra = {}
    if nch2:
        extra = {"gain2": np.ascontiguousarray(gain2.reshape(KC, 128).T), "w2": prep_w_in(w_in2, nch2)}
    in_maps = []
    for c in range(NCORES):
        m = {"xT": to_fm(xT_full[:, c * TOK:(c + 1) * TOK]), "oT": to_fm(oT_full[:, c * TOK:(c + 1) * TOK]),
             "gain": g, "wo": wo, "wu": wu, "wd": wdl}
        m.update(extra)
        in_maps.append(m)
    res = run_bass_kernel_spmd(nc, in_maps, core_ids=list(range(NCORES)), trace=trace)
    if trace:
        print("phaseC exec_time_ns", res.exec_time_ns)
    outs = [r["xoT"].transpose(1, 0, 2).reshape(D_MODEL, TOK) for r in res.results]
    xo = np.concatenate(outs, axis=1)
    if nch2:
        pj = np.concatenate([r["projT"] for r in res.results], axis=1)[:w_in2.shape[1]]
        return xo, pj
    return xo


NEGB = -30000.0
ROPE_THETA = 500000.0


def rope_tables(d, pos, reps):
    rot = d // 4
    half = rot // 2
    inv = np.power(np.float32(ROPE_THETA), -np.arange(half, dtype=np.float32) / np.float32(half)).astype(np.float32)
    ang = pos.astype(np.float32)[None, :] * inv[:, None]
    c = np.ones((d, len(pos)), np.float32)
    s = np.zeros((d, len(pos)), np.float32)
    c[:half] = np.cos(ang)
    c[half:rot] = np.cos(ang)
    s[:half] = np.sin(ang)
    s[half:rot] = np.sin(ang)
    return np.tile(c, (reps, 1)), np.tile(s, (reps, 1))


def rope_matrix(d, reps):
    rot = d // 4
    half = rot // 2
    m = np.zeros((reps * d, reps * d), np.float32)
    for r in range(reps):
        o = r * d
        for i in range(half):
            m[o + i + half, o + i] = -1.0
            m[o + i, o + i + half] = 1.0
    return m


def mask_biases():
    kl = np.arange(128)[:, None]
    ql = np.arange(128)[None, :]
    diag = np.where(kl <= ql, 0.0, NEGB).astype(np.float32)
    prev = np.where(kl > ql, 0.0, NEGB).astype(np.float32)
    return np.tile(diag, (1, 4)), np.tile(prev, (1, 4))


def qk_norm_rope(P, src, dst_bf, gain_col, ones_blk, R_sb, cosf, sinf, sq, rr, tmpf, psbig, eps_col, inv_d, key, ntok, sfx=""):
    cols = [(c0, min(512, ntok - c0)) for c0 in range(0, ntok, 512)]
    ng = (len(cols) + 3) // 4
    gw = [sum(n for (_, n) in cols[4 * g:4 * g + 4]) for g in range(ng)]
    P.add("act", lambda e: e.activation(out=sq[:, :ntok], in_=src, func=AF.Square), r=[key], w=["sq" + sfx])
    for c, (c0, n) in enumerate(cols):
        P.add("pe", lambda e, c=c, c0=c0, n=n: e.matmul(psbig[c // 4][:, (c % 4) * 512:(c % 4) * 512 + n], lhsT=ones_blk[:],
                                                        rhs=sq[:, c0:c0 + n], start=True, stop=True),
              r=["sq" + sfx, "onesblk"], w=[("psbig" + sfx, c // 4)])
    for g in range(ng):
        n = gw[g]
        P.add("act", lambda e, g=g, n=n: e.activation(out=rr[:, g * 2048:g * 2048 + n], in_=psbig[g][:, :n], func=AF.Sqrt,
                                                      bias=eps_col[:], scale=inv_d),
              r=["epsc"], w=[("psbig" + sfx, g), ("rr" + sfx, g)])
        P.add("dve", lambda e, g=g, n=n: e.reciprocal(out=rr[:, g * 2048:g * 2048 + n], in_=rr[:, g * 2048:g * 2048 + n]),
              r=[("rr" + sfx, g)], w=[("rr" + sfx, g)])
    rrk = [("rr" + sfx, g) for g in range(ng)]
    P.add("dve", lambda e: e.scalar_tensor_tensor(out=src, in0=src, scalar=gain_col, in1=rr[:, :ntok], op0=ALU.mult, op1=ALU.mult),
          r=[key, "gains"] + rrk, w=[key])
    for c, (c0, n) in enumerate(cols):
        P.add("pe", lambda e, c=c, c0=c0, n=n: e.matmul(psbig[c // 4][:, (c % 4) * 512:(c % 4) * 512 + n], lhsT=R_sb[:],
                                                        rhs=src[:, c0:c0 + n], start=True, stop=True),
              r=[key, "Rm"], w=[("psbig" + sfx, c // 4)])
    for g in range(ng):
        n = gw[g]
        P.add("dve", lambda e, g=g, n=n: e.tensor_tensor(out=tmpf[:, g * 2048:g * 2048 + n], in0=psbig[g][:, :n],
                                                         in1=sinf[:, g * 2048:g * 2048 + n], op=ALU.mult),
              r=["tabs" + sfx], w=[("psbig" + sfx, g), ("tmpf" + sfx, g)])
    P.add("pool", lambda e: e.tensor_tensor(out=src, in0=src, in1=cosf[:, :ntok], op=ALU.mult), r=[key, "tabs" + sfx], w=[key])
    P.add("pool", lambda e: e.tensor_tensor(out=dst_bf, in0=src, in1=tmpf[:, :ntok], op=ALU.add),
          r=[key] + [("tmpf" + sfx, g) for g in range(ng)], w=[key + "_bf"])


def build_swaB():
    S = SEQ
    CH = 1024
    nc = bass.Bass("TRN2", target_bir_lowering=False)
    q_d = nc.dram_tensor("q", [4, 128, S], F32, kind="ExternalInput").ap()
    k_d = nc.dram_tensor("k2", [128, S], F32, kind="ExternalInput").ap()
    v_d = nc.dram_tensor("v", [128, 32, 64], F32, kind="ExternalInput").ap()
    gq_d = nc.dram_tensor("gq", [128, 1], F32, kind="ExternalInput").ap()
    gk_d = nc.dram_tensor("gk", [128, 1], F32, kind="ExternalInput").ap()
    es_d = nc.dram_tensor("esink", [1, 2, 512], F32, kind="ExternalInput").ap()
    cos_d = nc.dram_tensor("cosf", [128, S], F32, kind="ExternalInput").ap()
    sin_d = nc.dram_tensor("sinf", [128, S], F32, kind="ExternalInput").ap()
    R_d = nc.dram_tensor("Rm", [128, 128], F32, kind="ExternalInput").ap()
    id_d = nc.dram_tensor("ident", [128, 128], F32, kind="ExternalInput").ap()
    bd_d = nc.dram_tensor("bdiag", [128, 512], F32, kind="ExternalInput").ap()
    bp_d = nc.dram_tensor("bprev", [128, 512], F32, kind="ExternalInput").ap()
    ob_d = nc.dram_tensor("onesblk", [128, 128], F32, kind="ExternalInput").ap()
    out_d = nc.dram_tensor("oT", [512, S], F32, kind="ExternalOutput").ap()
    with ExitStack() as stack:
        P = Prog(nc, stack)
        qbf = P.sb([128, 4, S], BF16, "qbf")
        kbf = P.sb([128, S], BF16, "kbf")
        vbf = P.sb([128, 32, 64], BF16, "vbf")
        work = [P.sb([128, CH], F32, f"work{i}") for i in range(2)]
        tcos = [P.sb([128, CH], F32, f"tcos{i}") for i in range(2)]
        tsin = [P.sb([128, CH], F32, f"tsin{i}") for i in range(2)]
        rr = [P.sb([128, CH], F32, f"rr{i}") for i in range(2)]
        tmpf = [P.sb([128, CH], F32, f"tmpf{i}") for i in range(2)]
        sq = [P.sb([128, CH], BF16, f"sq{i}") for i in range(2)]
        R_sb = P.sb([128, 128], F32, "R_sb")
        ident = P.sb([128, 128], BF16, "ident")
        bdiag = P.sb([128, 512], BF16, "bdiag")
        bprev = P.sb([128, 512], BF16, "bprev")
        onesblk = P.sb([128, 128], BF16, "onesblk")
        ones64 = P.sb([128, 64], BF16, "ones64")
        ones1 = P.sb([1, 64], BF16, "ones1")
        esink = P.sb([1, 2, 512], BF16, "esink")
        gq = P.sb([128, 1], F32, "gq")
        gk = P.sb([128, 1], F32, "gk")
        eps_col = P.sb([128, 1], F32, "eps_col")
        pt = [P.sb([128, 512], BF16, f"pt{i}") for i in range(4)]
        rden = [P.sb([64, 512], F32, f"rden{i}") for i in range(2)]
        ost = [P.sb([64, 4, 512], F32, f"ost{i}") for i in range(2)]
        psbig = [P.ps([128, 2048], F32, f"psbig{i}") for i in range(2)]

        P.add("pool", lambda e: e.memset(eps_col[:], float(NORM_EPS)), w=["epsc"])
        P.add("pool", lambda e: e.memset(ones64[:], 1.0), w=["ones64"])
        P.add("pool", lambda e: e.memset(ones1[:], 1.0), w=["ones1"])
        P.dma("sp", R_sb[:], R_d, w=["Rm"])
        P.dma("sp", gq[:], gq_d, w=["gains"])
        P.dma("sp", gk[:], gk_d, w=["gains"])
        P.dma("pool", ident[:], id_d, w=["ident"])
        P.dma("pool", bdiag[:], bd_d, w=["bias"])
        P.dma("pool", bprev[:], bp_d, w=["bias"])
        P.dma("pool", onesblk[:], ob_d, w=["onesblk"])
        esf = P.sb([1, 2, 512], F32, "esf")
        P.dma("sp", esf[:], es_d, w=["esf"])
        P.add("act", lambda e: e.activation(out=esink[:], in_=esf[:], func=AF.Exp), r=["esf"], w=["esink"])
        P.dma("pool", vbf[:], v_d, w=["v"])
        jobs = [(k_d, gk[:, 0:1], lambda c0: kbf[:, c0:c0 + CH])]
        for p in range(4):
            jobs.append((q_d[p], gq[:, 0:1], lambda c0, p=p: qbf[:, p, c0:c0 + CH]))
        n1 = 0
        for (src_d, gcol, dstf) in jobs:
            for ci in range(S // CH):
                w_ = n1 % 2
                n1 += 1
                c0 = ci * CH
                P.dma("sp", work[w_][:], src_d[:, c0:c0 + CH], w=[f"w{w_}"])
                P.dma("sp", tcos[w_][:], cos_d[:, c0:c0 + CH], w=[f"tabs{w_}"])
                P.dma("sp", tsin[w_][:], sin_d[:, c0:c0 + CH], w=[f"tabs{w_}"])
                qk_norm_rope(P, work[w_][:], dstf(c0), gcol, onesblk, R_sb, tcos[w_], tsin[w_], sq[w_], rr[w_], tmpf[w_], [psbig[w_]],
                             eps_col, 1.0 / 64, f"w{w_}", CH, sfx=str(w_))
        qkeys = ["w0_bf", "w1_bf"]
        u = 0
        sc = 0
        pend = []

        def flush():
            while pend:
                pend.pop(0)()

        outv = out_d.rearrange("(p e d) t -> e d p t", p=4, e=2, d=64)
        for qg in range(SEQ // 512):
            for e in range(2):
                os_ = ost[(qg * 2 + e) % 2]
                oskey = ("ost", (qg * 2 + e) % 2)
                for qi in range(4):
                    qt = qg * 4 + qi
                    kts = [kt for kt in (qt - 1, qt) if kt >= 0]
                    oset = u % 2
                    u += 1
                    ops_ = psbig[1][0:64, oset * 1024:oset * 1024 + 512]
                    dps_ = psbig[1][0:64, oset * 1024 + 512:oset * 1024 + 1024]
                    okey = ("ops", oset)
                    for i, kt in enumerate(kts):
                        sb_ = sc % 4
                        sc += 1
                        sps = psbig[0][:, sb_ * 512:(sb_ + 1) * 512]
                        bias = bdiag if kt == qt else bprev
                        P.add("pe", lambda e_, e=e, kt=kt, qt=qt, sps=sps: e_.matmul(
                            sps, lhsT=kbf[64 * e:64 * e + 64, kt * 128:(kt + 1) * 128],
                            rhs=qbf[64 * e:64 * e + 64, :, qt * 128:(qt + 1) * 128], start=True, stop=False),
                            r=qkeys, w=[("sps", sb_)])
                        P.add("pe", lambda e_, sps=sps, bias=bias: e_.matmul(sps, lhsT=ident[:], rhs=bias[:], start=False, stop=True),
                              r=["ident", "bias"], w=[("sps", sb_)])
                        P.add("act", lambda e_, sps=sps, sb_=sb_: e_.activation(out=pt[sb_][:], in_=sps, func=AF.Exp, scale=0.125),
                              r=[], w=[("sps", sb_), ("pt", sb_)])
                        flush()

                        def pv(kt=kt, sb_=sb_, ops_=ops_, dps_=dps_, i=i, okey=okey, nk=len(kts)):
                            P.add("pe", lambda e_: e_.matmul(ops_, lhsT=vbf[:, kt, :], rhs=pt[sb_][:], start=(i == 0), stop=(i == nk - 1)),
                                  r=[("pt", sb_), "v"], w=[okey])
                            P.add("pe", lambda e_: e_.matmul(dps_, lhsT=ones64[:], rhs=pt[sb_][:], start=(i == 0), stop=False),
                                  r=[("pt", sb_), "ones64"], w=[okey])
                        pend.append(pv)

                    def fin(e=e, dps_=dps_, ops_=ops_, oset=oset, os_=os_, qi=qi, okey=okey, oskey=oskey, qg=qg):
                        P.add("pe", lambda e_: e_.matmul(dps_, lhsT=ones1[:], rhs=esink[:, e, :], start=False, stop=True),
                              r=["ones1", "esink"], w=[okey])
                        P.add("dve", lambda e_: e_.reciprocal(out=rden[oset][:], in_=dps_), r=[], w=[okey, ("rden", oset)])
                        P.add("dve", lambda e_: e_.tensor_tensor(
                            out=os_[:, :, qi * 128:(qi + 1) * 128], in0=ops_.rearrange("d (p q) -> d p q", p=4),
                            in1=rden[oset][:].rearrange("d (p q) -> d p q", p=4), op=ALU.mult),
                            r=[("rden", oset)], w=[okey, oskey])
                        if qi == 3:
                            P.dma("sp", outv[e, :, :, qg * 512:(qg + 1) * 512], os_[:], r=[oskey], w=[("out", qg, e)])
                    pend.append(fin)
        flush()
        P.finish([("out", qg, e) for qg in range(SEQ // 512) for e in range(2)])
        P.emit()
    return nc


def run_swaB(projT, q_norm, k_norm, sinks, trace=False):
    S = SEQ
    nc = get_prog("swaB", build_swaB)
    cosf, sinf = rope_tables(64, np.arange(S), 2)
    Rm = rope_matrix(64, 2)
    ident = np.eye(128, dtype=np.float32)
    bdiag, bprev = mask_biases()
    onesblk = np.kron(np.eye(2, dtype=np.float32), np.ones((64, 64), np.float32))
    gq = np.tile(q_norm.astype(np.float32), 2).reshape(128, 1)
    gk = np.tile(k_norm.astype(np.float32), 2).reshape(128, 1)
    in_maps = []
    for c in range(NCORES):
        b, g = c // 4, c % 4
        t0 = b * S
        q = np.ascontiguousarray(projT[g * 512:(g + 1) * 512, t0:t0 + S].reshape(4, 128, S))
        k = projT[2048 + g * 64:2048 + (g + 1) * 64, t0:t0 + S]
        k2 = np.ascontiguousarray(np.concatenate([k, k], axis=0))
        v = projT[2304 + g * 64:2304 + (g + 1) * 64, t0:t0 + S].T
        v = np.ascontiguousarray(v.reshape(32, 128, 64).transpose(1, 0, 2))
        sk = sinks[g * 8:(g + 1) * 8].astype(np.float32).reshape(4, 2)
        es = np.ascontiguousarray(np.repeat(sk.T[:, :, None], 128, axis=2).reshape(1, 2, 512))
        in_maps.append(dict(q=q, k2=k2, v=v, gq=gq, gk=gk, esink=es, cosf=cosf, sinf=sinf, Rm=Rm, ident=ident,
                            bdiag=bdiag, bprev=bprev, onesblk=onesblk))
    res = run_bass_kernel_spmd(nc, in_maps, core_ids=list(range(NCORES)), trace=trace)
    if trace:
        print("swaB exec_time_ns", res.exec_time_ns)
    oT = np.zeros((D_MODEL, BATCH * S), np.float32)
    for c in range(NCORES):
        b, g = c // 4, c % 4
        oT[g * 512:(g + 1) * 512, b * S:(b + 1) * S] = res.results[c]["oT"]
    return oT


def gla_consts():
    j = np.arange(128)[:, None]
    i = np.arange(128)[None, :]
    same = (j // 64) == (i // 64)
    T2 = np.where(same & (j <= i), -1.0 / 16.0, 0.0).astype(np.float32)
    U2 = np.where(same & (j > i), -1.0 / 16.0, 0.0).astype(np.float32)
    M2 = np.where(same & (j <= i), 1.0, 0.0).astype(np.float32)
    return T2, U2, M2


def build_glaB(dbg=False):
    S = SEQ
    NT = S // 128
    nc = bass.Bass("TRN2", target_bir_lowering=False)
    glr_d = nc.dram_tensor("glrT", [16, S], F32, kind="ExternalInput").ap()
    q_d = nc.dram_tensor("qT", [128, 2, S], F32, kind="ExternalInput").ap()
    k_d = nc.dram_tensor("kT", [128, 2, S], F32, kind="ExternalInput").ap()
    ktm_d = nc.dram_tensor("ktm", [NT, 128, 256], F32, kind="ExternalInput").ap()
    v_d = nc.dram_tensor("vtm", [NT, 128, 512], F32, kind="ExternalInput").ap()
    r_d = nc.dram_tensor("rT", [128, 4, S], F32, kind="ExternalInput").ap()
    wg_d = nc.dram_tensor("wg", [16, 256], F32, kind="ExternalInput").ap()
    bg_d = nc.dram_tensor("bg", [1, 256], F32, kind="ExternalInput").ap()
    gn_d = nc.dram_tensor("gn", [128, 4], F32, kind="ExternalInput").ap()
    T2_d = nc.dram_tensor("T2", [128, 128], F32, kind="ExternalInput").ap()
    U2_d = nc.dram_tensor("U2", [128, 128], F32, kind="ExternalInput").ap()
    M2_d = nc.dram_tensor("M2", [128, 128], F32, kind="ExternalInput").ap()
    out_d = nc.dram_tensor("oT", [512, S], F32, kind="ExternalOutput").ap()
    with ExitStack() as stack:
        P = Prog(nc, stack)
        qp = P.sb([128, 2, S], BF16, "qp")
        kp = P.sb([128, 2, S], BF16, "kp")
        kpp = P.sb([128, NT, 256], BF16, "kpp")
        vbf = P.sb([128, NT, 512], BF16, "vbf")
        att = P.sb([128, NT, 128], BF16, "att")
        explast = P.sb([128, 2, 2 * NT], F32, "explast")
        glr = P.sb([16, S], F32, "glr")
        wg = P.sb([16, 256], F32, "wg")
        bg = P.sb([1, 256], F32, "bg")
        gn = P.sb([128, 4], F32, "gn")
        T2 = P.sb([128, 128], F32, "T2")
        U2 = P.sb([128, 128], F32, "U2")
        M2 = P.sb([128, 128], F32, "M2")
        ones1 = P.sb([1, 128], F32, "ones1")
        onesb = P.sb([128, 128], BF16, "onesb")
        eps_col = P.sb([128, 1], F32, "eps_col")
        qt_ = [P.sb([128, 2, 128], F32, f"qt{i}") for i in range(2)]
        kt_ = [P.sb([128, 2, 128], F32, f"kt{i}") for i in range(2)]
        ktm = [P.sb([128, 256], F32, f"ktm{i}") for i in range(2)]
        e1 = [P.sb([128, 256], F32, f"e1{i}") for i in range(2)]
        la = [P.sb([128, 256], F32, f"la{i}") for i in range(2)]
        ET = [P.sb([128, 2, 128], F32, f"ET{i}") for i in range(2)]
        EinvT = [P.sb([128, 2, 128], F32, f"EinvT{i}") for i in range(2)]
        Elmc = [P.sb([128, 256], F32, f"Elmc{i}") for i in range(2)]
        Sst = P.sb([128, 2, 512], F32, "Sst")
        Sbf = [P.sb([128, 2, 512], BF16, f"Sbf{i}") for i in range(2)]
        ot = [P.sb([128, 4, 128], F32, f"ot{i}") for i in range(2)]
        osq = [P.sb([128, 4, 128], BF16, f"osq{i}") for i in range(2)]
        rs = [P.sb([128, 128], F32, f"rs{i}") for i in range(2)]
        rt = [P.sb([128, 4, 128], F32, f"rt{i}") for i in range(2)]
        ost = [P.sb([128, 4, 512], F32, f"ost{i}") for i in range(2)]
        ps = [P.ps([128, 512], F32, f"psb{i}") for i in range(8)]

        P.add("pool", lambda e: e.memset(eps_col[:], float(NORM_EPS)), w=["epsc"])
        P.add("pool", lambda e: e.memset(ones1[:], 1.0), w=["ones1"])
        P.add("pool", lambda e: e.memset(onesb[:], 1.0), w=["onesb"])
        P.add("pool", lambda e: e.memset(Sst[:], 0.0), w=["S"])
        for t_, d_, k_ in ((glr, glr_d, "glr"), (wg, wg_d, "wg"), (bg, bg_d, "bg"), (gn, gn_d, "gn"), (T2, T2_d, "T2"),
                           (U2, U2_d, "U2"), (M2, M2_d, "M2")):
            P.dma("sp", t_[:], d_, w=[k_])

        def pass1(t):
            b = t % 2
            tok = slice(t * 128, (t + 1) * 128)
            P.dma("sp", qt_[b][:], q_d[:, :, tok], w=[("qt", b)])
            P.dma("sp", kt_[b][:], k_d[:, :, tok], w=[("kt", b)])
            P.dma("sp", ktm[b][:], ktm_d[t], w=[("ktm", b)])
            P.dma("pool", vbf[:, t, :], v_d[t], w=[("v", t)])
            pA = ps[2 * b]
            pB = ps[2 * b + 1]
            P.add("pe", lambda e: e.matmul(pA[:, 0:256], lhsT=glr[:, tok], rhs=wg[:], start=True, stop=False),
                  r=["glr", "wg"], w=[("pA", b)])
            P.add("pe", lambda e: e.matmul(pA[:, 0:256], lhsT=ones1[:], rhs=bg[:], start=False, stop=True),
                  r=["ones1", "bg"], w=[("pA", b)])
            P.add("act", lambda e: e.activation(out=e1[b][:], in_=pA[:, 0:256], func=AF.Exp, scale=-1.0), r=[("pA", b)], w=[("e1", b)])
            P.add("act", lambda e: e.activation(out=la[b][:], in_=e1[b][:], func=AF.Ln, bias=1.0), r=[("e1", b)], w=[("la", b)])
            for dc in range(2):
                P.add("pe", lambda e, dc=dc: e.matmul(pA[:, 256 + dc * 128:256 + (dc + 1) * 128], lhsT=la[b][:, dc * 128:(dc + 1) * 128],
                                                      rhs=T2[:], start=True, stop=True), r=[("la", b), "T2"], w=[("pA2", b)])
            P.add("pe", lambda e: e.matmul(pB[:, 0:256], lhsT=U2[:], rhs=la[b][:], start=True, stop=True), r=[("la", b), "U2"], w=[("pB", b)])
            cumT = pA[:, 256:512].rearrange("p (c i) -> p c i", c=2)
            P.add("act", lambda e: e.activation(out=ET[b][:], in_=cumT, func=AF.Exp), r=[("pA2", b)], w=[("ET", b)])
            P.add("act", lambda e: e.activation(out=EinvT[b][:], in_=cumT, func=AF.Exp, scale=-1.0), r=[("pA2", b)], w=[("EinvT", b)])
            P.add("act", lambda e: e.activation(out=Elmc[b][:], in_=pB[:, 0:256], func=AF.Exp), w=[("pB", b), ("Elmc", b)])
            P.add("dve", lambda e: e.scalar_tensor_tensor(out=qp[:, :, tok], in0=qt_[b][:], scalar=float(256 ** -0.5), in1=ET[b][:],
                                                          op0=ALU.mult, op1=ALU.mult), r=[("qt", b), ("ET", b)], w=[("qp", t)])
            P.add("dve", lambda e: e.tensor_tensor(out=kp[:, :, tok], in0=kt_[b][:], in1=EinvT[b][:], op=ALU.mult),
                  r=[("kt", b), ("EinvT", b)], w=[("kp", t)])
            P.add("pool", lambda e: e.tensor_tensor(out=kpp[:, t, :], in0=ktm[b][:], in1=Elmc[b][:], op=ALU.mult),
                  r=[("ktm", b), ("Elmc", b)], w=[("kpp", t)])
            P.add("pool", lambda e: e.tensor_copy(out=explast[:, :, 2 * t:2 * t + 2], in_=ET[b][:, :, 63:128:64]),
                  r=[("ET", b)], w=[("explast", t)])
            for dc in range(2):
                P.add("pe", lambda e, dc=dc: e.matmul(pB[:, 256:384], lhsT=kp[:, dc, tok], rhs=qp[:, dc, tok], start=(dc == 0), stop=(dc == 1)),
                      r=[("kp", t), ("qp", t)], w=[("pB", b)])
            P.add("dve", lambda e: e.tensor_tensor(out=att[:, t, :], in0=pB[:, 256:384], in1=M2[:], op=ALU.mult),
                  r=["M2"], w=[("pB", b), ("att", t)])

        sidx = [0]

        def pass2(t):
            b = t % 2
            tok0 = t * 128
            pO = ps[6]
            pN = ps[7]
            P.dma("sp", rt[b][:], r_d[:, :, tok0:tok0 + 128], w=[("rt", b)])
            for dvc in range(4):
                P.add("pe", lambda e, dvc=dvc: e.matmul(pO[:, dvc * 128:(dvc + 1) * 128], lhsT=vbf[:, t, dvc * 128:(dvc + 1) * 128],
                                                        rhs=att[:, t, :], start=(dvc == 0), stop=False),
                      r=[("v", t), ("att", t)], w=["pO"])
            for c in range(2):
                ch = 2 * t + c
                cs = slice(tok0 + 64 * c, tok0 + 64 * c + 64)
                if ch > 0:
                    sb_ = Sbf[sidx[0] % 2]
                    sk = ("Sbf", sidx[0] % 2)
                    for dvc in range(4):
                        for dc in range(2):
                            P.add("pe", lambda e, dvc=dvc, dc=dc, sb_=sb_, c=c, cs=cs: e.matmul(
                                pO[:, dvc * 128 + 64 * c:dvc * 128 + 64 * c + 64], lhsT=sb_[:, dc, dvc * 128:(dvc + 1) * 128],
                                rhs=qp[:, dc, cs], start=False, stop=(dc == 1 and c == 1)),
                                r=[sk, ("qp", t)], w=["pO"])
                for dc in range(2):
                    pk = ps[4 + dc]
                    P.add("pe", lambda e, dc=dc, pk=pk, c=c: e.matmul(pk[:], lhsT=kpp[64 * c:64 * c + 64, t, dc * 128:(dc + 1) * 128],
                                                                 rhs=vbf[64 * c:64 * c + 64, t, :], start=True, stop=True),
                          r=[("kpp", t), ("v", t)], w=[("pk", dc)])
                sidx[0] += 1
                sb_ = Sbf[sidx[0] % 2]
                sk = ("Sbf", sidx[0] % 2)
                for dc in range(2):
                    pk = ps[4 + dc]
                    P.add("dve", lambda e, dc=dc, pk=pk, ch=ch: e.scalar_tensor_tensor(
                        out=Sst[:, dc, :], in0=Sst[:, dc, :], scalar=explast[:, dc, ch:ch + 1], in1=pk[:], op0=ALU.mult, op1=ALU.add),
                        r=[("pk", dc), ("explast", t), "S"], w=["S"])
                P.add("act", lambda e, sb_=sb_: e.copy(out=sb_[:], in_=Sst[:]), r=["S"], w=[sk])
                if dbg and t == 0 and c == 0:
                    P.dump("d_S0", Sst[:], ["S"])
                    P.dump("d_Sbf0", sb_[:], [sk])
            P.add("act", lambda e: e.copy(out=ot[b][:], in_=pO[:].rearrange("p (c i) -> p c i", c=4)), r=["pO"], w=[("ot", b)])
            if dbg and t == 0:
                P.dump("d_ot", ot[0][:], [("ot", 0)])
                P.dump("d_S", Sst[:], ["S"])
            P.add("act", lambda e: e.activation(out=osq[b][:], in_=ot[b][:], func=AF.Square), r=[("ot", b)], w=[("osq", b)])
            for dvc in range(4):
                P.add("pe", lambda e, dvc=dvc: e.matmul(pN[:, 0:128], lhsT=onesb[:], rhs=osq[b][:, dvc, :], start=(dvc == 0), stop=(dvc == 3)),
                      r=[("osq", b), "onesb"], w=["pN"])
            P.add("act", lambda e: e.activation(out=rs[b][:], in_=pN[:, 0:128], func=AF.Sqrt, bias=eps_col[:], scale=1.0 / 512),
                  r=["pN", "epsc"], w=[("rs", b)])
            P.add("dve", lambda e: e.reciprocal(out=rs[b][:], in_=rs[b][:]), r=[("rs", b)], w=[("rs", b)])
            P.add("act", lambda e: e.activation(out=rt[b][:], in_=rt[b][:], func=AF.Silu), r=[("rt", b)], w=[("rt", b)])
            o4 = ost[(t // 4) % 2]
            for dvc in range(4):
                P.add("dve", lambda e, dvc=dvc: e.scalar_tensor_tensor(out=ot[b][:, dvc, :], in0=ot[b][:, dvc, :], scalar=gn[:, dvc:dvc + 1],
                                                                       in1=rs[b][:], op0=ALU.mult, op1=ALU.mult),
                      r=[("ot", b), ("rs", b), "gn"], w=[("ot", b)])
            P.add("pool", lambda e: e.tensor_tensor(out=o4[:, :, (t % 4) * 128:(t % 4 + 1) * 128], in0=ot[b][:], in1=rt[b][:], op=ALU.mult),
                  r=[("ot", b), ("rt", b)], w=[("ost", (t // 4) % 2)])
            if t % 4 == 3:
                g4 = t // 4
                P.dma("sp", out_d.rearrange("(c p) t -> p c t", p=128)[:, :, g4 * 512:(g4 + 1) * 512], o4[:],
                      r=[("ost", g4 % 2)], w=[("out", g4)])

        pass1(0)
        if dbg:
            P.dump("d_la", la[0][:], [("la", 0)])
            P.dump("d_ET", ET[0][:], [("ET", 0)])
            P.dump("d_Elmc", Elmc[0][:], [("Elmc", 0)])
            P.dump("d_att", att[:, 0, :], [("att", 0)])
            P.dump("d_qp", qp[:, :, 0:128], [("qp", 0)])
            P.dump("d_kp", kp[:, :, 0:128], [("kp", 0)])
            P.dump("d_kpp", kpp[:, 0, :], [("kpp", 0)])
            P.dump("d_explast", explast[:, :, 0:2], [("explast", 0)])
        pass1(1)
        for t in range(NT):
            if t + 2 < NT:
                pass1(t + 2)
            pass2(t)
        P.finish([("out", g4) for g4 in range(NT // 4)])
        P.emit()
    return nc


def run_glaB(projT, w_gate_up, b_gate, o_norm, trace=False):
    S = SEQ
    nc = get_prog("glaB", build_glaB)
    T2, U2, M2 = gla_consts()
    gn = np.ascontiguousarray(o_norm.astype(np.float32).reshape(4, 128).T)
    in_maps = []
    for c in range(NCORES):
        b, hd = c // 4, c % 4
        ts = slice(b * S, (b + 1) * S)
        qT = to_fm(projT[hd * 256:(hd + 1) * 256, ts])
        kT_ = projT[1024 + hd * 256:1024 + (hd + 1) * 256, ts]
        kT = to_fm(kT_)
        ktm = np.ascontiguousarray(kT_.T.reshape(S // 128, 128, 256))
        vtm = np.ascontiguousarray(projT[2048 + hd * 512:2048 + (hd + 1) * 512, ts].T.reshape(S // 128, 128, 512))
        glrT = np.ascontiguousarray(projT[4096:4112, ts])
        rT = to_fm(projT[4112 + hd * 512:4112 + (hd + 1) * 512, ts])
        wg = np.ascontiguousarray(w_gate_up[:, hd * 256:(hd + 1) * 256])
        bg = np.ascontiguousarray(b_gate[hd * 256:(hd + 1) * 256].reshape(1, 256))
        in_maps.append(dict(glrT=glrT, qT=qT, kT=kT, ktm=ktm, vtm=vtm, rT=rT, wg=wg, bg=bg, gn=gn, T2=T2, U2=U2, M2=M2))
    res = run_bass_kernel_spmd(nc, in_maps, core_ids=list(range(NCORES)), trace=trace)
    if trace:
        print("glaB exec_time_ns", res.exec_time_ns)
    oT = np.zeros((D_MODEL, BATCH * S), np.float32)
    for c in range(NCORES):
        b, hd = c // 4, c % 4
        oT[hd * 512:(hd + 1) * 512, b * S:(b + 1) * S] = res.results[c]["oT"]
    return oT


NSA_NT = SEQ // 128
NSA_NCMP = (SEQ - 32) // 16 + 1


def nsa_consts():
    S = SEQ
    NT = NSA_NT
    cm = np.zeros((128, 48, 128), np.float32)
    ql = np.arange(128)[None, :]
    cl = np.arange(128)[:, None]
    for qt in range(NT):
        for ct in range(2):
            if ct == 1 and qt < 16:
                continue
            idx = qt if ct == 0 else 32 + qt - 16
            c = cl + 128 * ct
            vis = (16 * c + 31 <= 128 * qt + ql) & (c < NSA_NCMP)
            cm[:, idx, :] = np.where(vis, 0.0, NEGB)
    bonus = np.zeros((128, NT, 64), np.float32)
    j = np.arange(64)[None, :]
    for qt in range(NT):
        pos = 128 * qt + np.arange(128)[:, None]
        bq = pos // 64
        forced = (j == 0) | (j == bq) | (j == bq - 1)
        bonus[:, qt, :] = np.where(j <= bq, np.where(forced, 1e4, 0.0), -1e30)
    E = np.zeros((64, NT, 128), np.float32)
    for kt in range(NT):
        E[2 * kt, kt, :64] = 1.0
        E[2 * kt + 1, kt, 64:] = 1.0
    c0 = np.arange(256) * 16
    s0 = np.arange(64) * 64
    ov = np.minimum(c0[:, None] + 32, s0[None, :] + 64) - np.maximum(c0[:, None], s0[None, :])
    ov = (np.clip(ov, 0, None) / 32.0).astype(np.float32)
    ov[NSA_NCMP:] = 0.0
    ov = np.ascontiguousarray(ov.reshape(2, 128, 64).transpose(1, 0, 2))
    return cm, bonus, E, ov


def build_nsaB(dbg=False):
    S = SEQ
    NT = NSA_NT
    SCALE = float(128 ** -0.5)
    nc = bass.Bass("TRN2", target_bir_lowering=False)

    def din(name, shape):
        return nc.dram_tensor(name, list(shape), F32, kind="ExternalInput").ap()

    q_d = din("q", [4, 128, S])
    kc_d = din("kc", [128, S])
    vc_d = din("vc", [128, S])
    ks_d = din("ks", [128, S])
    kw_d = din("kw", [128, S])
    vs_d = din("vs", [128, NT, 128])
    vw_d = din("vw", [128, NT, 128])
    gate_d = din("gate", [12, S])
    gq_d = din("gq", [128, 1])
    gk_d = din("gk", [128, 3])
    pe_d = din("peT", [2, 128, 32])
    w1_d = din("w1", [2, 128, 32, 128])
    w2_d = din("w2", [2, 128, 128])
    cos_d = din("cosf", [128, S])
    sin_d = din("sinf", [128, S])
    cosc_d = din("cosc", [128, 256])
    sinc_d = din("sinc", [128, 256])
    R_d = din("Rm", [128, 128])
    id_d = din("ident", [128, 128])
    bd_d = din("bdiag", [128, 512])
    bp_d = din("bprev", [128, 512])
    cm_d = din("cmask", [128, 48, 128])
    bon_d = din("bonus", [128, NT, 64])
    E_d = din("Emat", [64, NT, 128])
    ov_d = din("ov", [128, 2, 64])
    out_d = nc.dram_tensor("oT", [512, S], F32, kind="ExternalOutput").ap()
    with ExitStack() as stack:
        P = Prog(nc, stack)
        qbf = P.sb([128, 4, S], BF16, "qbf")
        ksbf = P.sb([128, S], BF16, "ksbf")
        kwbf = P.sb([128, S], BF16, "kwbf")
        vsb = P.sb([128, NT, 128], BF16, "vsb")
        vwb = P.sb([128, NT, 128], BF16, "vwb")
        kcm = P.sb([128, 256], BF16, "kcm")
        vcm = P.sb([128, 2, 128], BF16, "vcm")
        R_sb = P.sb([128, 128], F32, "R_sb")
        identf = P.sb([128, 128], F32, "identf")
        ident = P.sb([128, 128], BF16, "identb")
        bdiag = P.sb([128, 512], BF16, "bdiag")
        bprev = P.sb([128, 512], BF16, "bprev")
        cmask = P.sb([128, 48, 128], BF16, "cmask")
        bonus = P.sb([128, NT, 64], F32, "bonus")
        Emat = P.sb([64, NT, 128], BF16, "Emat")
        ov = P.sb([128, 2, 64], BF16, "ov")
        onesb = P.sb([128, 128], BF16, "onesb")
        gq = P.sb([128, 1], F32, "gq")
        gk = P.sb([128, 3], F32, "gk")
        eps_col = P.sb([128, 1], F32, "eps_col")
        region = P.sb([128, 12288], F32, "region")
        psbig = [P.ps([128, 2048], F32, f"psbig{i}") for i in range(2)]

        def bank(i):
            return psbig[i // 4][:, (i % 4) * 512:(i % 4 + 1) * 512]

        roff = [0]

        def rsb(shape, dtype):
            exact = int(np.prod(shape[1:])) * (2 if dtype == BF16 else 4)
            assert exact % 4 == 0
            nbytes = (exact + 31) // 32 * 32
            a = roff[0] // 4
            v = region[0:shape[0], a:a + exact // 4]
            roff[0] += nbytes
            assert roff[0] <= 12288 * 4, roff[0]
            if dtype == BF16:
                v = v.bitcast(BF16)
            if len(shape) == 3:
                v = v.rearrange("p (a b) -> p a b", a=shape[1])
            return v

        P.add("pool", lambda e: e.memset(eps_col[:], float(NORM_EPS)), w=["epsc"])
        P.add("pool", lambda e: e.memset(onesb[:], 1.0), w=["onesblk"])
        for t_, d_, k_ in ((R_sb, R_d, "Rm"), (identf, id_d, "identf"), (gq, gq_d, "gains"), (gk, gk_d, "gains"), (bonus, bon_d, "bonus")):
            P.dma("sp", t_[:], d_, w=[k_])
        for t_, d_, k_ in ((ident, id_d, "ident"), (bdiag, bd_d, "bias"), (bprev, bp_d, "bias"), (cmask, cm_d, "cmask"),
                           (Emat, E_d, "Emat"), (ov, ov_d, "ov"), (vsb, vs_d, "vs"), (vwb, vw_d, "vw")):
            P.dma("pool", t_[:], d_, w=[k_])

        CH = 1024
        work = [rsb([128, CH], F32) for _ in range(2)]
        tcos = [rsb([128, CH], F32) for _ in range(2)]
        tsin = [rsb([128, CH], F32) for _ in range(2)]
        rr = [rsb([128, CH], F32) for _ in range(2)]
        tmpf = [rsb([128, CH], F32) for _ in range(2)]
        sq = [rsb([128, CH], BF16) for _ in range(2)]
        jobs = [(q_d[h], gq[:, 0:1], lambda c0, h=h: qbf[:, h, c0:c0 + CH]) for h in range(4)]
        jobs.append((ks_d, gk[:, 1:2], lambda c0: ksbf[:, c0:c0 + CH]))
        jobs.append((kw_d, gk[:, 2:3], lambda c0: kwbf[:, c0:c0 + CH]))
        n1 = 0
        for (src_d, gcol, dstf) in jobs:
            for ci in range(S // CH):
                w_ = n1 % 2
                n1 += 1
                c0 = ci * CH
                P.dma("sp", work[w_], src_d[:, c0:c0 + CH], w=[f"w{w_}"])
                P.dma("sp", tcos[w_], cos_d[:, c0:c0 + CH], w=[f"tabs{w_}"])
                P.dma("sp", tsin[w_], sin_d[:, c0:c0 + CH], w=[f"tabs{w_}"])
                qk_norm_rope(P, work[w_], dstf(c0), gcol, onesb, R_sb, tcos[w_], tsin[w_], sq[w_], rr[w_], tmpf[w_], [psbig[w_]], eps_col,
                             1.0 / 128, f"w{w_}", CH, sfx=str(w_))
        P.barrier()

        roff[0] = 0
        kcbf = rsb([128, S], BF16)
        vcbf = rsb([128, S], BF16)
        w1b = [rsb([128, 32, 128], BF16) for _ in range(2)]
        w2b = [rsb([128, 128], BF16) for _ in range(2)]
        peb = [rsb([128, 32], BF16) for _ in range(2)]
        ccol = [rsb([128, 1], F32) for _ in range(2)]
        xg = rsb([128, 256], F32)
        x2 = rsb([128, 256], F32)
        th = rsb([128, 256], F32)
        gel = rsb([128, 256], BF16)
        kcmf = rsb([128, 256], F32)
        tcc = rsb([128, 256], F32)
        tsc = rsb([128, 256], F32)
        rr2 = rsb([128, 256], F32)
        tmp2 = rsb([128, 256], F32)
        sq2 = rsb([128, 256], BF16)
        P.dma("pool", kcbf, kc_d, w=["kcbf"])
        P.dma("pool", vcbf, vc_d, w=["vcbf"])
        P.dma("sp", tcc, cosc_d, w=["tabs2"])
        P.dma("sp", tsc, sinc_d, w=["tabs2"])
        for i in range(2):
            P.dma("pool", w1b[i], w1_d[i], w=[("w1", i)])
            P.dma("pool", w2b[i], w2_d[i], w=[("w2", i)])
            P.dma("pool", peb[i], pe_d[i], w=[("pe", i)])
        for i, srcbf, skey in ((0, kcbf, "kcbf"), (1, vcbf, "vcbf")):
            pc = bank(0)
            pv = bank(1)
            for l in range(32):
                P.add("pe", lambda e, i=i, l=l, pc=pc: e.matmul(pc[:, 0:1], lhsT=w1b[i][:, l, :], rhs=peb[i][:, l:l + 1],
                                                             start=(l == 0), stop=(l == 31)),
                      r=[("w1", i), ("pe", i)], w=["pc"])
            P.add("act", lambda e, i=i, pc=pc: e.copy(out=ccol[i], in_=pc[:, 0:1]), r=[], w=["pc", ("ccol", i)])
            for l in range(32):
                P.add("pe", lambda e, i=i, l=l, pv=pv, srcbf=srcbf: e.matmul(
                    pv[:, 0:NSA_NCMP], lhsT=w1b[i][:, l, :], rhs=srcbf[:, l:l + 16 * (NSA_NCMP - 1) + 1:16],
                    start=(l == 0), stop=(l == 31)), r=[("w1", i), skey], w=["pv"])
            P.add("pool", lambda e: e.memset(xg, 0.0), w=["xg"])
            P.add("act", lambda e, i=i, pv=pv: e.activation(out=xg[:, 0:NSA_NCMP], in_=pv[:, 0:NSA_NCMP], func=AF.Identity, bias=ccol[i]),
                  r=[("ccol", i)], w=["pv", "xg"])
            P.add("pool", lambda e: e.tensor_tensor(out=x2, in0=xg, in1=xg, op=ALU.mult), r=["xg"], w=["x2"])
            P.add("pool", lambda e: e.tensor_scalar(out=x2, in0=x2, scalar1=0.044715, scalar2=1.0, op0=ALU.mult, op1=ALU.add),
                  r=["x2"], w=["x2"])
            P.add("pool", lambda e: e.tensor_tensor(out=x2, in0=x2, in1=xg, op=ALU.mult), r=["x2", "xg"], w=["x2"])
            P.add("act", lambda e: e.activation(out=th, in_=x2, func=AF.Tanh, scale=float(np.sqrt(2.0 / np.pi))), r=["x2"], w=["th"])
            P.add("pool", lambda e: e.tensor_scalar(out=th, in0=th, scalar1=1.0, scalar2=0.5, op0=ALU.add, op1=ALU.mult), r=["th"], w=["th"])
            P.add("pool", lambda e: e.tensor_tensor(out=gel, in0=th, in1=xg, op=ALU.mult), r=["th", "xg"], w=["gel"])
            if i == 0:
                pk2 = bank(2)
                P.add("pe", lambda e, pk2=pk2: e.matmul(pk2[:, 0:256], lhsT=w2b[0], rhs=gel, start=True, stop=True),
                      r=[("w2", 0), "gel"], w=["pk2"])
                P.add("act", lambda e, pk2=pk2: e.copy(out=kcmf, in_=pk2[:, 0:256]), r=[], w=["pk2", "kcmf"])
                qk_norm_rope(P, kcmf, kcm[:], gk[:, 0:1], onesb, R_sb, tcc, tsc, sq2, rr2, tmp2, [psbig[1]], eps_col, 1.0 / 128, "kcmf", 256)
            else:
                pk2 = bank(3)
                for ct in range(2):
                    P.add("pe", lambda e, ct=ct, pk2=pk2: e.matmul(pk2[:, ct * 128:(ct + 1) * 128], lhsT=gel[:, ct * 128:(ct + 1) * 128],
                                                                   rhs=w2b[1], start=(ct == 0), stop=True),
                          r=[("w2", 1), "gel"], w=["pk3"])
                P.add("act", lambda e, pk2=pk2: e.copy(out=vcm[:], in_=pk2[:, 0:256].rearrange("p (a b) -> p a b", a=2)),
                      r=[], w=["pk3", "vcm"])
        if dbg:
            P.dump("d_kcm", kcm[:], ["kcmf_bf"])
            P.dump("d_vcm", vcm[:], ["vcm"])
            P.dump("d_qbf", qbf[:, :, 0:256], ["w0_bf", "w1_bf"])
        P.barrier()

        roff[0] = 0
        pt = [rsb([128, 512], BF16) for _ in range(4)]
        G = [rsb([128, 12, 128], F32) for _ in range(2)]
        rden = [rsb([128, 512], F32) for _ in range(2)]
        fac = [rsb([128, 512], F32) for _ in range(2)]
        oft = [rsb([128, 512], F32) for _ in range(2)]
        ostage = [rsb([128, 4, 512], F32) for _ in range(2)]
        tmpU = rsb([64, 512], F32)
        impT = rsb([64, 128], F32)
        score = rsb([128, 64], F32)
        wk = rsb([128, 64], F32)
        m1 = rsb([128, 8], F32)
        m2 = rsb([128, 8], F32)
        negb = rsb([128, 64], F32)
        selT = [rsb([64, 128], BF16) for _ in range(2)]
        tiny = rsb([128, 1], F32)
        P.add("pool", lambda e: e.memset(tiny, 1e-30), w=["tiny"])
        cnt = dict(s=0, pt=0, o=0)
        qall = ["w0_bf", "w1_bf"]

        pend = []

        def flush(depth=0):
            while len(pend) > depth:
                pend.pop(0)()

        def branch(qt, kts, kfn, vfn, maskfn, extra=None, extra_key=None):
            oset = cnt["o"] % 2
            cnt["o"] += 1
            ob, db = 4 + 2 * oset, 5 + 2 * oset
            okey = ("oset", oset)
            nk = len(kts)
            for i, kt in enumerate(kts):
                sl = cnt["s"] % 3
                cnt["s"] += 1
                sps = bank(sl)
                skey = ("sps", sl)
                p_ = cnt["pt"] % 4
                cnt["pt"] += 1
                masks = maskfn(kt)
                kl, kr = kfn(kt)
                P.add("pe", lambda e, sps=sps, kl=kl, qt=qt, masks=masks: e.matmul(
                    sps, lhsT=kl, rhs=qbf[:, :, qt * 128:(qt + 1) * 128], start=True, stop=(len(masks) == 0)),
                    r=qall + kr, w=[skey])
                for mi, (ml, mr, mk) in enumerate(masks):
                    P.add("pe", lambda e, sps=sps, ml=ml, mr=mr, mi=mi, masks=masks: e.matmul(
                        sps, lhsT=ml, rhs=mr, start=False, stop=(mi == len(masks) - 1)), r=mk, w=[skey])
                P.add("act", lambda e, sps=sps, p_=p_: e.activation(out=pt[p_], in_=sps, func=AF.Exp, scale=SCALE),
                      r=[], w=[skey, ("pt", p_)])
                flush(1)
                vl, vr = vfn(kt)

                def pv(ob=ob, db=db, vl=vl, vr=vr, p_=p_, i=i, nk=nk, kt=kt, okey=okey):
                    P.add("pe", lambda e: e.matmul(bank(ob), lhsT=vl, rhs=pt[p_], start=(i == 0), stop=(i == nk - 1)),
                          r=[("pt", p_)] + vr, w=[okey])
                    P.add("pe", lambda e: e.matmul(bank(db), lhsT=onesb[:], rhs=pt[p_], start=(i == 0), stop=(i == nk - 1)),
                          r=[("pt", p_), "onesblk"], w=[okey])
                    if extra is not None:
                        extra(kt, i, nk, p_)
                pend.append(pv)
            return ob, db, okey, oset

        def combine(qt, br, ob, db, okey, oset, add_tiny):
            st = ostage[(qt // 4) % 2]
            stkey = ("ostage", (qt // 4) % 2)
            gb = qt % 2
            dst = st[:, :, (qt % 4) * 128:(qt % 4 + 1) * 128]
            if add_tiny:
                P.add("dve", lambda e, db=db, oset=oset: e.tensor_scalar(out=rden[oset], in0=bank(db), scalar1=tiny[:, 0:1], scalar2=None, op0=ALU.add),
                      r=["tiny"], w=[okey, ("rden", oset)])
                P.add("dve", lambda e, oset=oset: e.reciprocal(out=rden[oset], in_=rden[oset]), r=[], w=[("rden", oset)])
            else:
                P.add("dve", lambda e, db=db, oset=oset: e.reciprocal(out=rden[oset], in_=bank(db)), r=[], w=[okey, ("rden", oset)])
            P.add("pool", lambda e, oset=oset, gb=gb, br=br: e.tensor_tensor(
                out=fac[oset].rearrange("p (h q) -> p h q", h=4), in0=rden[oset].rearrange("p (h q) -> p h q", h=4),
                in1=G[gb].rearrange("p (h b) q -> p h b q", b=3)[:, :, br, :], op=ALU.mult),
                r=[("rden", oset), ("G", gb)], w=[("fac", oset)])
            if br == 0:
                P.add("dve", lambda e, ob=ob, oset=oset, dst=dst: e.tensor_tensor(
                    out=dst, in0=bank(ob).rearrange("p (h q) -> p h q", h=4), in1=fac[oset].rearrange("p (h q) -> p h q", h=4), op=ALU.mult),
                    r=[("fac", oset)], w=[okey, stkey])
            else:
                P.add("dve", lambda e, ob=ob, oset=oset: e.tensor_tensor(out=oft[oset], in0=bank(ob), in1=fac[oset], op=ALU.mult),
                      r=[("fac", oset)], w=[okey, ("oft", oset)])
                P.add("pool", lambda e, oset=oset, dst=dst: e.tensor_tensor(
                    out=dst, in0=dst, in1=oft[oset].rearrange("p (h q) -> p h q", h=4), op=ALU.add),
                    r=[("oft", oset)], w=[stkey])

        for qt in range(NT):
            gb = qt % 2
            tok = slice(qt * 128, (qt + 1) * 128)
            P.dma("sp", G[gb], gate_d[:, tok].partition_broadcast(128), w=[("G", gb)])
            P.add("act", lambda e, gb=gb: e.activation(out=G[gb], in_=G[gb], func=AF.Sigmoid), r=[], w=[("G", gb)])
            cts = [0] if qt < 16 else [0, 1]
            ub = bank(3)

            def cmask_fn(ct, qt=qt):
                idx = qt if ct == 0 else 32 + qt - 16
                return [(ident[:], cmask[:, idx, :].unsqueeze(1).to_broadcast([128, 4, 128]), ["ident", "cmask"])]

            def uextra(ct, i, nk, p_, ub=ub):
                P.add("pe", lambda e, ct=ct, i=i, nk=nk, p_=p_: e.matmul(ub[0:64, :], lhsT=ov[:, ct, :], rhs=pt[p_], start=(i == 0), stop=(i == nk - 1)),
                      r=[("pt", p_), "ov"], w=["ub"])

            ob, db, okey, oset = branch(qt, cts, lambda ct: (kcm[:, ct * 128:(ct + 1) * 128], ["kcmf_bf"]),
                                        lambda ct: (vcm[:, ct, :], ["vcm"]), cmask_fn, extra=uextra)
            flush()
            combine(qt, 0, ob, db, okey, oset, True)
            P.add("dve", lambda e, ub=ub, oset=oset: e.tensor_tensor(out=tmpU, in0=ub[0:64, :], in1=rden[oset][0:64, :], op=ALU.mult),
                  r=[("rden", oset)], w=["ub", "tmpU"])
            P.add("dve", lambda e: e.tensor_reduce(out=impT, in_=tmpU.rearrange("p (h q) -> p q h", h=4), axis=AX.X, op=ALU.add),
                  r=["tmpU"], w=["impT"])
            tb = bank(2)
            P.add("pe", lambda e, tb=tb: e.transpose(tb[:, 0:64], impT, identf[0:64, 0:64]), r=["impT", "identf"], w=[("sps", 2)])
            P.add("dve", lambda e, tb=tb, qt=qt: e.tensor_tensor(out=score, in0=tb[:, 0:64], in1=bonus[:, qt, :], op=ALU.add),
                  r=["bonus"], w=[("sps", 2), "score"])
            P.add("dve", lambda e: e.max(out=m1, in_=score), r=["score"], w=["m1"])
            P.add("dve", lambda e: e.match_replace(out=wk, in_to_replace=m1, in_values=score, imm_value=-3.0e38), r=["score", "m1"], w=["wk"])
            P.add("dve", lambda e: e.max(out=m2, in_=wk), r=["wk"], w=["m2"])
            P.add("dve", lambda e: e.tensor_scalar(out=negb, in0=score, scalar1=m2[:, 7:8], scalar2=None, op0=ALU.is_ge), r=["score", "m2"], w=["negb"])
            P.add("dve", lambda e: e.tensor_scalar(out=negb, in0=negb, scalar1=1.0, scalar2=-NEGB, op0=ALU.subtract, op1=ALU.mult),
                  r=["negb"], w=["negb"])
            P.add("pe", lambda e, tb=tb: e.transpose(tb[0:64, 128:256], negb, identf[:]), r=["negb", "identf"], w=[("sps", 2)])
            sT = selT[qt % 2]
            P.add("act", lambda e, tb=tb, sT=sT: e.copy(out=sT, in_=tb[0:64, 128:256]), r=[], w=[("sps", 2), ("selT", qt % 2)])
            if dbg and qt in (3, 20):
                P.dump(f"d_score{qt}", score, ["score"])
                P.dump(f"d_negb{qt}", negb, ["negb"])
                P.dump(f"d_selT{qt}", sT, [("selT", qt % 2)])

            def win_mask(kt, qt=qt):
                if kt == qt:
                    return [(ident[:], bdiag[:], ["ident", "bias"])]
                if kt == qt - 4:
                    return [(ident[:], bprev[:], ["ident", "bias"])]
                return []

            ob, db, okey, oset = branch(qt, [kt for kt in range(qt - 4, qt + 1) if kt >= 0],
                                        lambda kt: (kwbf[:, kt * 128:(kt + 1) * 128], ["w0_bf", "w1_bf"]),
                                        lambda kt: (vwb[:, kt, :], ["vw"]), win_mask)
            pend.append(lambda qt=qt, ob=ob, db=db, okey=okey, oset=oset: combine(qt, 2, ob, db, okey, oset, False))
            def sel_mask(kt, qt=qt, sT=sT):
                ms = [(Emat[:, kt, :], sT.unsqueeze(1).to_broadcast([64, 4, 128]), ["Emat", ("selT", qt % 2)])]
                if kt == qt:
                    ms.append((ident[:], bdiag[:], ["ident", "bias"]))
                return ms

            ob, db, okey, oset = branch(qt, list(range(qt + 1)), lambda kt: (ksbf[:, kt * 128:(kt + 1) * 128], ["w0_bf", "w1_bf"]),
                                        lambda kt: (vsb[:, kt, :], ["vs"]), sel_mask)

            def fin(qt=qt, ob=ob, db=db, okey=okey, oset=oset):
                combine(qt, 1, ob, db, okey, oset, False)
                if qt % 4 == 3:
                    g4 = qt // 4
                    P.dma("sp", out_d.rearrange("(h d) t -> d h t", h=4)[:, :, g4 * 512:(g4 + 1) * 512], ostage[g4 % 2],
                          r=[("ostage", g4 % 2)], w=[("out", g4)])
            pend.append(fin)
        flush()
        P.finish([("out", g4) for g4 in range(NT // 4)])
        P.emit()
    return nc


def run_nsaB(projT, q_norm, k_norm, cmp_pe, cmp_w1, cmp_w2, trace=False, dbg=False, cores=None):
    S = SEQ
    NT = NSA_NT
    nc = get_prog(("nsaB", dbg), lambda: build_nsaB(dbg))
    cosf, sinf = rope_tables(128, np.arange(S), 1)
    cend = np.arange(256) * 16 + 31
    cosc, sinc = rope_tables(128, cend, 1)
    Rm = rope_matrix(128, 1)
    ident = np.eye(128, dtype=np.float32)
    bdiag, bprev = mask_biases()
    cm, bonus, E, ov = nsa_consts()
    gq = q_norm.astype(np.float32).reshape(128, 1)
    gk = np.ascontiguousarray(k_norm.astype(np.float32).T)
    peT = np.ascontiguousarray(cmp_pe.transpose(0, 2, 1))
    w1 = np.ascontiguousarray(cmp_w1.reshape(2, 32, 128, 128).transpose(0, 2, 1, 3))
    w2 = np.ascontiguousarray(cmp_w2)
    in_maps = []
    clist = list(range(NCORES)) if cores is None else cores
    for c in clist:
        b, g = c // 4, c % 4
        ts = slice(b * S, (b + 1) * S)

        def rows(base):
            return np.ascontiguousarray(projT[base + g * 128:base + (g + 1) * 128, ts])

        def tm(base):
            return np.ascontiguousarray(projT[base + g * 128:base + (g + 1) * 128, ts].T.reshape(NT, 128, 128).transpose(1, 0, 2))

        q = np.ascontiguousarray(projT[g * 512:(g + 1) * 512, ts].reshape(4, 128, S))
        gate = np.ascontiguousarray(projT[5120 + g * 12:5120 + (g + 1) * 12, ts])
        in_maps.append(dict(q=q, kc=rows(2048), vc=rows(2560), ks=rows(3072), vs=tm(3584), kw=rows(4096), vw=tm(4608), gate=gate,
                            gq=gq, gk=gk, peT=peT, w1=w1, w2=w2, cosf=cosf, sinf=sinf, cosc=cosc, sinc=sinc, Rm=Rm, ident=ident,
                            bdiag=bdiag, bprev=bprev, cmask=cm, bonus=bonus, Emat=E, ov=ov))
    res = run_bass_kernel_spmd(nc, in_maps, core_ids=list(range(len(clist))), trace=trace)
    if trace:
        print("nsaB exec_time_ns", res.exec_time_ns)
    if dbg:
        return res.results
    oT = np.zeros((D_MODEL, BATCH * S), np.float32)
    for i, c in enumerate(clist):
        b, g = c // 4, c % 4
        oT[g * 512:(g + 1) * 512, b * S:(b + 1) * S] = res.results[i]["oT"]
    return oT


def kernel(x, norm_mix, norm_mlp, mlp_w_up, mlp_w_down,
           nsa_w_in, nsa_w_out, nsa_q_norm, nsa_k_norm, nsa_cmp_pe, nsa_cmp_w1, nsa_cmp_w2,
           gla_w_in, gla_w_gate_up, gla_b_gate, gla_o_norm, gla_w_out,
           swa_w_in, swa_w_out, swa_q_norm, swa_k_norm, swa_sinks):
    f = lambda a: np.asarray(a, dtype=np.float32)
    x = f(x)
    xT = np.ascontiguousarray(x.reshape(BATCH * SEQ, D_MODEL).T)
    idx = {0: 0, 1: 0, 2: 0}
    w_ins = []
    for i in range(DEPTH):
        kind = i % 3
        w_ins.append(f((nsa_w_in, gla_w_in, swa_w_in)[kind][idx[kind]]))
        idx[kind] += 1
    idx = {0: 0, 1: 0, 2: 0}
    projT = run_phaseA(xT, f(norm_mix[0]), w_ins[0])
    for i in range(DEPTH):
        kind = i % 3
        j = idx[kind]
        idx[kind] += 1
        if kind == 0:
            oT = run_nsaB(projT, f(nsa_q_norm[j]), f(nsa_k_norm[j]), f(nsa_cmp_pe[j]), f(nsa_cmp_w1[j]), f(nsa_cmp_w2[j]))
            w_out = f(nsa_w_out[j])
        elif kind == 1:
            oT = run_glaB(projT, f(gla_w_gate_up[j]), f(gla_b_gate[j]), f(gla_o_norm[j]))
            w_out = f(gla_w_out[j])
        else:
            oT = run_swaB(projT, f(swa_q_norm[j]), f(swa_k_norm[j]), f(swa_sinks[j]))
            w_out = f(swa_w_out[j])
        if i + 1 < DEPTH:
            xT, projT = run_phaseC(xT, oT, w_out, f(norm_mlp[i]), f(mlp_w_up[i]), f(mlp_w_down[i]),
                                   gain2=f(norm_mix[i + 1]), w_in2=w_ins[i + 1])
        else:
            xT = run_phaseC(xT, oT, w_out, f(norm_mlp[i]), f(mlp_w_up[i]), f(mlp_w_down[i]))
    return np.ascontiguousarray(xT.T).reshape(BATCH, SEQ, D_MODEL).astype(np.float32)
```

```python
import numpy as np
from contextlib import ExitStack
import concourse.bass as bass
import concourse.mybir as mybir
from concourse.bass_utils import run_bass_kernel_spmd

F32 = mybir.dt.float32
BF16 = mybir.dt.bfloat16
ALU = mybir.AluOpType
AF = mybir.ActivationFunctionType
AX = mybir.AxisListType

D_MODEL = 2048
BATCH = 2
SEQ = 4096
DEPTH = 4
D_FF = 4 * D_MODEL
NORM_EPS = 1e-6
NCORES = 8
TOK = BATCH * SEQ // NCORES
KC = D_MODEL // 128

SEM_LIM = 16000
NDMASEM = 8


class Prog:
    ENGS = ("pe", "act", "dve", "pool", "sp")

    def __init__(self, nc, stack):
        self.nc = nc
        self.stack = stack
        self.ops = []
        self.lastw = {}
        self.readers = {}
        self._n = 0
        self.fence = None

    def sb(self, shape, dtype, name=None):
        self._n += 1
        return self.stack.enter_context(self.nc.sbuf_tensor((name + "_s") if name else f"sb{self._n}", list(shape), dtype))

    def ps(self, shape, dtype=F32, name=None):
        self._n += 1
        return self.stack.enter_context(self.nc.psum_tensor(name or f"ps{self._n}", list(shape), dtype))

    def add(self, eng, fn, r=(), w=(), dma=False):
        oid = len(self.ops)
        deps = set()
        for k in r:
            if k in self.lastw:
                deps.add(self.lastw[k])
        for k in w:
            if k in self.lastw:
                deps.add(self.lastw[k])
            deps.update(self.readers.get(k, ()))
        if self.fence is not None:
            deps.add(self.fence)
        for k in r:
            self.readers.setdefault(k, []).append(oid)
        for k in w:
            self.lastw[k] = oid
            self.readers[k] = []
        self.ops.append(dict(eng=eng, fn=fn, deps=deps, dma=dma))
        return oid

    def barrier(self):
        if not hasattr(self, "_fdummy"):
            self._fdummy = self.sb([1, 8], F32, "fence_dummy")
        keys = list(set(self.lastw.keys()) | set(self.readers.keys()))
        d = self._fdummy
        self.fence = None
        self.fence = self.add("pool", lambda e: e.memset(d[:], 0.0), r=(), w=keys)

    def dma(self, eng, out, in_, r=(), w=()):
        return self.add(eng, lambda e: e.dma_start(out=out, in_=in_), r=r, w=w, dma=True)

    def dump(self, name, ap, r):
        d = self.nc.dram_tensor(name, list(ap.shape), ap.dtype, kind="ExternalOutput").ap()
        self.dma("sp", d, ap, r=r, w=[("dump", name)])
        self.dumps = getattr(self, "dumps", []) + [("dump", name)]

    def finish(self, r):
        r = list(r) + getattr(self, "dumps", [])
        self.add("sp", None, r=r, w=())

    def emit(self):
        nc = self.nc
        ops = self.ops
        has_dep = [False] * len(ops)
        for op in ops:
            for d in op["deps"]:
                has_dep[d] = True
        cnt = {e: 0 for e in self.ENGS}
        dcnt = {e: 0 for e in self.ENGS}
        nsem = {}
        for i, op in enumerate(ops):
            e = op["eng"]
            if op["dma"]:
                j = dcnt[e]
                dcnt[e] += 1
                op["sig"] = (("d", e, j % NDMASEM), 16 * (j // NDMASEM + 1), 16)
                op["prev"] = (("d", e, j % NDMASEM), 16 * (j // NDMASEM)) if j >= NDMASEM else None
            elif has_dep[i] and op["fn"] is not None:
                c = cnt[e]
                cnt[e] += 1
                op["sig"] = (("c", e, c // SEM_LIM), c % SEM_LIM + 1, 1)
            else:
                op["sig"] = None
        keys = set()
        for op in ops:
            if op["sig"]:
                keys.add(op["sig"][0])
        sems = {}
        for k in sorted(keys):
            sems[k] = self.stack.enter_context(nc.semaphore("s_" + "_".join(str(x) for x in k)))
        by_eng = {e: [] for e in self.ENGS}
        for i, op in enumerate(ops):
            by_eng[op["eng"]].append(i)
        bname = {"pe": "tensor", "act": "scalar", "dve": "vector", "pool": "gpsimd", "sp": "sync"}
        self.n_wait = 0
        with nc.Block() as block:
            for E in self.ENGS:
                if not by_eng[E]:
                    continue

                def body(eng, E=E):
                    seen = {}
                    for i in by_eng[E]:
                        op = ops[i]
                        waits = {}
                        for d in op["deps"]:
                            dop = ops[d]
                            if dop["sig"] is None:
                                continue
                            if dop["eng"] == E and E == "pe" and not dop["dma"]:
                                continue
                            k, v, _ = dop["sig"]
                            if waits.get(k, 0) < v:
                                waits[k] = v
                        if op["dma"] and op["prev"] is not None:
                            k, v = op["prev"]
                            if waits.get(k, 0) < v:
                                waits[k] = v
                        for k, v in waits.items():
                            if seen.get(k, 0) >= v:
                                continue
                            seen[k] = v
                            eng.wait_ge(sems[k], v)
                            self.n_wait += 1
                        if op["fn"] is None:
                            continue
                        ins = op["fn"](eng)
                        if op["sig"] is not None:
                            k, v, inc = op["sig"]
                            ins.then_inc(sems[k], inc)

                getattr(block, bname[E])(body)


def rms_to_hT(P, xT, gain_sb, hT, ones_bf, psA, psB, scratch_bf, rr, eps_col, tag, gkey="gain"):
    nc = P.nc
    pss = [psA, psB]
    for kc in range(KC):
        sl = kc % 2
        P.add("act", lambda e, kc=kc, sl=sl: e.activation(out=scratch_bf[sl][:], in_=xT[:, kc, :], func=AF.Square),
              r=[("xT", kc)], w=[("sq", sl)])
        for hf in range(2):
            P.add("pe", lambda e, kc=kc, sl=sl, hf=hf: e.matmul(
                pss[hf][:], lhsT=ones_bf[:], rhs=scratch_bf[sl][:, hf * 512:(hf + 1) * 512],
                start=(kc == 0), stop=(kc == KC - 1)),
                r=[("sq", sl), "ones"], w=[("ps", tag, hf)])
    for hf in range(2):
        P.add("act", lambda e, hf=hf: e.activation(
            out=rr[:, hf * 512:(hf + 1) * 512], in_=pss[hf][:], func=AF.Sqrt, bias=eps_col[:], scale=1.0),
            r=[("ps", tag, hf), "epsc"], w=[("rr", hf)])
        P.add("dve", lambda e, hf=hf: e.reciprocal(out=rr[:, hf * 512:(hf + 1) * 512], in_=rr[:, hf * 512:(hf + 1) * 512]),
              r=[("rr", hf)], w=[("rr", hf)])
    for kc in range(KC):
        P.add("dve", lambda e, kc=kc: e.scalar_tensor_tensor(
            out=hT[:, kc, :], in0=xT[:, kc, :], scalar=gain_sb[:, kc:kc + 1], in1=rr[:],
            op0=ALU.mult, op1=ALU.mult), r=[("xT", kc), ("rr", 0), ("rr", 1), gkey], w=[("hT", kc)])


def build_phaseA(nch):
    nc = bass.Bass("TRN2", target_bir_lowering=False)
    xT_d = nc.dram_tensor("xT", [128, KC, TOK], F32, kind="ExternalInput").ap()
    gain_d = nc.dram_tensor("gain", [128, KC], F32, kind="ExternalInput").ap()
    w_d = nc.dram_tensor("w", [nch, 128, KC, 128], F32, kind="ExternalInput").ap()
    out_d = nc.dram_tensor("projT", [nch * 128, TOK], F32, kind="ExternalOutput").ap()
    with ExitStack() as stack:
        P = Prog(nc, stack)
        xT = P.sb([128, KC, TOK], F32, "xT_sb")
        hT = P.sb([128, KC, TOK], BF16, "hT_sb")
        gain_sb = P.sb([128, KC], F32, "gain_sb")
        ones_bf = P.sb([128, 128], BF16, "ones_bf")
        sq = [P.sb([128, TOK], BF16, f"sq{i}") for i in range(2)]
        rr = P.sb([128, TOK], F32, "rr")
        NW = 4
        wt = [P.sb([128, KC, 128], BF16, f"wt{i}") for i in range(NW)]
        NO = 3
        ot = [P.sb([128, TOK], F32, f"ot{i}") for i in range(NO)]
        ps = [P.ps([128, 512], F32, f"psb{i}") for i in range(8)]

        P.add("pool", lambda e: e.memset(ones_bf[:], 1.0), w=["ones"])
        eps_col = P.sb([128, 1], F32, "eps_col")
        P.add("pool", lambda e: e.memset(eps_col[:], float(D_MODEL * NORM_EPS)), w=["epsc"])
        P.dma("sp", gain_sb[:], gain_d, w=["gain"])
        for kc in range(KC):
            P.dma("sp", xT[:, kc, :], xT_d[:, kc, :], w=[("xT", kc)])
        P.add("act", lambda e: e.mul(out=gain_sb[:], in_=gain_sb[:], mul=float(np.sqrt(D_MODEL))), r=["gain"], w=["gain"])
        rms_to_hT(P, xT, gain_sb, hT, ones_bf, ps[0], ps[1], sq, rr, eps_col, "n")
        allh = [("hT", kc) for kc in range(KC)]
        for j in range(nch):
            s = j % NW
            P.dma("pool", wt[s][:], w_d[j], w=[("wt", s)])
            pb = (j % 3) * 2 + 2
            for kc in range(KC):
                for hf in range(2):
                    P.add("pe", lambda e, s=s, kc=kc, hf=hf, pb=pb: e.matmul(
                        ps[pb + hf][:], lhsT=wt[s][:, kc, :], rhs=hT[:, kc, hf * 512:(hf + 1) * 512],
                        start=(kc == 0), stop=(kc == KC - 1)),
                        r=[("wt", s)] + (allh if kc == 0 else []), w=[("ps", pb + hf)])
            o = j % NO
            P.add("act", lambda e, o=o, pb=pb: e.copy(out=ot[o][:, 0:512], in_=ps[pb][:]), r=[("ps", pb)], w=[("ot", o, 0)])
            P.add("dve", lambda e, o=o, pb=pb: e.tensor_copy(out=ot[o][:, 512:1024], in_=ps[pb + 1][:]), r=[("ps", pb + 1)], w=[("ot", o, 1)])
            P.dma("sp", out_d[j * 128:(j + 1) * 128, :], ot[o][:], r=[("ot", o, 0), ("ot", o, 1)], w=[("out", j)])
        P.finish([("out", j) for j in range(nch)])
        P.emit()
    return nc


def to_fm(a2d):
    r, c = a2d.shape
    return np.ascontiguousarray(a2d.reshape(r // 128, 128, c).transpose(1, 0, 2))


def prep_w_in(w, nch):
    d, n = w.shape
    wp = np.zeros((d, nch * 128), np.float32)
    wp[:, :n] = w
    return np.ascontiguousarray(wp.reshape(KC, 128, nch, 128).transpose(2, 1, 0, 3))


def run_phaseA(xT_full, gain, w):
    n = w.shape[1]
    nch = (n + 127) // 128
    nc = get_prog(("A", nch), lambda: build_phaseA(nch))
    wl = prep_w_in(w, nch)
    g = np.ascontiguousarray(gain.reshape(KC, 128).T)
    in_maps = []
    for c in range(NCORES):
        in_maps.append({"xT": to_fm(xT_full[:, c * TOK:(c + 1) * TOK]), "gain": g, "w": wl})
    res = run_bass_kernel_spmd(nc, in_maps, core_ids=list(range(NCORES)))
    return np.concatenate([r["projT"] for r in res.results], axis=1)[:n]


FB = 4
NFC = D_FF // 128


def build_phaseC(nch2=0):
    nc = bass.Bass("TRN2", target_bir_lowering=False)
    xT_d = nc.dram_tensor("xT", [128, KC, TOK], F32, kind="ExternalInput").ap()
    oT_d = nc.dram_tensor("oT", [128, KC, TOK], F32, kind="ExternalInput").ap()
    gain_d = nc.dram_tensor("gain", [128, KC], F32, kind="ExternalInput").ap()
    wo_d = nc.dram_tensor("wo", [KC, 128, KC, 128], F32, kind="ExternalInput").ap()
    wu_d = nc.dram_tensor("wu", [NFC, 128, KC, 128], F32, kind="ExternalInput").ap()
    wd_d = nc.dram_tensor("wd", [NFC, 128, D_MODEL], F32, kind="ExternalInput").ap()
    out_d = nc.dram_tensor("xoT", [128, KC, TOK], F32, kind="ExternalOutput").ap()
    if nch2:
        gain2_d = nc.dram_tensor("gain2", [128, KC], F32, kind="ExternalInput").ap()
        w2_d = nc.dram_tensor("w2", [nch2, 128, KC, 128], F32, kind="ExternalInput").ap()
        proj_d = nc.dram_tensor("projT", [nch2 * 128, TOK], F32, kind="ExternalOutput").ap()
    with ExitStack() as stack:
        P = Prog(nc, stack)
        xT = P.sb([128, KC, TOK], F32, "xT_sb")
        hT = P.sb([128, KC, TOK], BF16, "hT_sb")
        gain_sb = P.sb([128, KC], F32, "gain_sb")
        ones_bf = P.sb([128, 128], BF16, "ones_bf")
        eps_col = P.sb([128, 1], F32, "eps_col")
        sq = [P.sb([128, TOK], BF16, f"sq{i}") for i in range(2)]
        rr = P.sb([128, TOK], F32, "rr")
        NW = 4
        wt = [P.sb([128, KC, 128], BF16, f"wt{i}") for i in range(NW)]
        wd = [P.sb([128, D_MODEL], BF16, f"wd{i}") for i in range(2 * FB)]
        if nch2:
            gain2_sb = P.sb([128, KC], F32, "gain2_sb")
            ot2 = [P.sb([128, TOK], F32, f"ot2{i}") for i in range(2)]
        act = [P.sb([128, TOK], BF16, f"act{i}") for i in range(2 * FB)]
        tmp = [P.sb([128, 512], F32, f"tmp{i}") for i in range(2)]
        ps = [P.ps([128, 512], F32, f"psb{i}") for i in range(8)]

        P.add("pool", lambda e: e.memset(ones_bf[:], 1.0), w=["ones"])
        P.add("pool", lambda e: e.memset(eps_col[:], float(D_MODEL * NORM_EPS)), w=["epsc"])
        P.dma("sp", gain_sb[:], gain_d, w=["gain"])
        for kc in range(KC):
            P.dma("sp", xT[:, kc, :], xT_d[:, kc, :], w=[("xT", kc)])
        for kc in range(KC):
            P.dma("pool", hT[:, kc, :], oT_d[:, kc, :], w=[("hT", kc)])
        P.add("act", lambda e: e.mul(out=gain_sb[:], in_=gain_sb[:], mul=float(np.sqrt(D_MODEL))), r=["gain"], w=["gain"])
        allh = [("hT", kc) for kc in range(KC)]
        wcnt = [0]
        pcnt = [0]

        def wtile(src):
            s = wcnt[0] % NW
            wcnt[0] += 1
            P.dma("pool", wt[s][:], src, w=[("wt", s)])
            return s

        def pbank():
            pb = 2 + (pcnt[0] % 3) * 2
            pcnt[0] += 1
            return pb

        for n in range(KC):
            s = wtile(wo_d[n])
            pb = pbank()
            for kc in range(KC):
                for hf in range(2):
                    P.add("pe", lambda e, s=s, kc=kc, hf=hf, pb=pb: e.matmul(
                        ps[pb + hf][:], lhsT=wt[s][:, kc, :], rhs=hT[:, kc, hf * 512:(hf + 1) * 512],
                        start=(kc == 0), stop=(kc == KC - 1)),
                        r=[("wt", s)] + (allh if kc == 0 else []), w=[("ps", pb + hf)])
            for hf in range(2):
                P.add("dve", lambda e, n=n, hf=hf, pb=pb: e.tensor_tensor(
                    out=xT[:, n, hf * 512:(hf + 1) * 512], in0=xT[:, n, hf * 512:(hf + 1) * 512], in1=ps[pb + hf][:], op=ALU.add),
                    r=[("ps", pb + hf), ("xT", n)], w=[("xT", n)])
        rms_to_hT(P, xT, gain_sb, hT, ones_bf, ps[0], ps[1], sq, rr, eps_col, "n")
        for blk in range(NFC // FB):
            par = blk % 2
            for fl in range(FB):
                f = blk * FB + fl
                s = wtile(wu_d[f])
                a = par * FB + fl
                P.dma("pool", wd[a][:], wd_d[f], w=[("wd", a)])
                pb = pbank()
                for kc in range(KC):
                    for hf in range(2):
                        P.add("pe", lambda e, s=s, kc=kc, hf=hf, pb=pb: e.matmul(
                            ps[pb + hf][:], lhsT=wt[s][:, kc, :], rhs=hT[:, kc, hf * 512:(hf + 1) * 512],
                            start=(kc == 0), stop=(kc == KC - 1)),
                            r=[("wt", s)] + (allh if kc == 0 else []), w=[("ps", pb + hf)])
                for hf in range(2):
                    P.add("act", lambda e, hf=hf, pb=pb: e.activation(out=tmp[hf][:], in_=ps[pb + hf][:], func=AF.Relu),
                          r=[("ps", pb + hf)], w=[("tmp", hf)])
                    P.add("act", lambda e, hf=hf, a=a: e.activation(
                        out=act[a][:, hf * 512:(hf + 1) * 512], in_=tmp[hf][:], func=AF.Square),
                        r=[("tmp", hf)], w=[("act", a, hf)])
            for n in range(KC):
                pb = pbank()
                for fl in range(FB):
                    a = par * FB + fl
                    for hf in range(2):
                        P.add("pe", lambda e, a=a, n=n, hf=hf, pb=pb, fl=fl: e.matmul(
                            ps[pb + hf][:], lhsT=wd[a][:, n * 128:(n + 1) * 128], rhs=act[a][:, hf * 512:(hf + 1) * 512],
                            start=(fl == 0), stop=(fl == FB - 1)),
                            r=[("wd", a), ("act", a, hf)], w=[("ps", pb + hf)])
                for hf in range(2):
                    P.add("dve", lambda e, n=n, hf=hf, pb=pb: e.tensor_tensor(
                        out=xT[:, n, hf * 512:(hf + 1) * 512], in0=xT[:, n, hf * 512:(hf + 1) * 512], in1=ps[pb + hf][:], op=ALU.add),
                        r=[("ps", pb + hf), ("xT", n)], w=[("xT", n)])
        for kc in range(KC):
            P.dma("sp", out_d[:, kc, :], xT[:, kc, :], r=[("xT", kc)], w=[("out", kc)])
        fin = [("out", kc) for kc in range(KC)]
        if nch2:
            P.dma("sp", gain2_sb[:], gain2_d, w=["gain2"])
            P.add("act", lambda e: e.mul(out=gain2_sb[:], in_=gain2_sb[:], mul=float(np.sqrt(D_MODEL))), r=["gain2"], w=["gain2"])
            rms_to_hT(P, xT, gain2_sb, hT, ones_bf, ps[0], ps[1], sq, rr, eps_col, "n", gkey="gain2")
            for j in range(nch2):
                s = wtile(w2_d[j])
                pb = pbank()
                for kc in range(KC):
                    for hf in range(2):
                        P.add("pe", lambda e, s=s, kc=kc, hf=hf, pb=pb: e.matmul(
                            ps[pb + hf][:], lhsT=wt[s][:, kc, :], rhs=hT[:, kc, hf * 512:(hf + 1) * 512],
                            start=(kc == 0), stop=(kc == KC - 1)),
                            r=[("wt", s)] + (allh if kc == 0 else []), w=[("ps", pb + hf)])
                o = j % 2
                P.add("act", lambda e, o=o, pb=pb: e.copy(out=ot2[o][:, 0:512], in_=ps[pb][:]), r=[("ps", pb)], w=[("ot2", o, 0)])
                P.add("dve", lambda e, o=o, pb=pb: e.tensor_copy(out=ot2[o][:, 512:1024], in_=ps[pb + 1][:]), r=[("ps", pb + 1)], w=[("ot2", o, 1)])
                P.dma("sp", proj_d[j * 128:(j + 1) * 128, :], ot2[o][:], r=[("ot2", o, 0), ("ot2", o, 1)], w=[("pout", j)])
            fin += [("pout", j) for j in range(nch2)]
        P.finish(fin)
        P.emit()
    return nc


_PROG_CACHE = {}


def get_prog(key, builder):
    if key not in _PROG_CACHE:
        _PROG_CACHE[key] = builder()
    return _PROG_CACHE[key]


def run_phaseC(xT_full, oT_full, w_out, gain, w_up, w_down, trace=False, gain2=None, w_in2=None):
    nch2 = 0 if w_in2 is None else (w_in2.shape[1] + 127) // 128
    nc = get_prog(("C", nch2), lambda: build_phaseC(nch2))
    wo = prep_w_in(w_out, KC)
    wu = prep_w_in(w_up, NFC)
    wdl = np.ascontiguousarray(w_down.reshape(NFC, 128, D_MODEL))
    g = np.ascontiguousarray(gain.reshape(KC, 128).T)
    extra = {}
    if nch2:
        extra = {"gain2": np.ascontiguousarray(gain2.reshape(KC, 128).T), "w2": prep_w_in(w_in2, nch2)}
    in_maps = []
    for c in range(NCORES):
        m = {"xT": to_fm(xT_full[:, c * TOK:(c + 1) * TOK]), "oT": to_fm(oT_full[:, c * TOK:(c + 1) * TOK]),
             "gain": g, "wo": wo, "wu": wu, "wd": wdl}
        m.update(extra)
        in_maps.append(m)
    res = run_bass_kernel_spmd(nc, in_maps, core_ids=list(range(NCORES)), trace=trace)
    if trace:
        print("phaseC exec_time_ns", res.exec_time_ns)
    outs = [r["xoT"].transpose(1, 0, 2).reshape(D_MODEL, TOK) for r in res.results]
    xo = np.concatenate(outs, axis=1)
    if nch2:
        pj = np.concatenate([r["projT"] for r in res.results], axis=1)[:w_in2.shape[1]]
        return xo, pj
    return xo


NEGB = -30000.0
ROPE_THETA = 500000.0


def rope_tables(d, pos, reps):
    rot = d // 4
    half = rot // 2
    inv = np.power(np.float32(ROPE_THETA), -np.arange(half, dtype=np.float32) / np.float32(half)).astype(np.float32)
    ang = pos.astype(np.float32)[None, :] * inv[:, None]
    c = np.ones((d, len(pos)), np.float32)
    s = np.zeros((d, len(pos)), np.float32)
    c[:half] = np.cos(ang)
    c[half:rot] = np.cos(ang)
    s[:half] = np.sin(ang)
    s[half:rot] = np.sin(ang)
    return np.tile(c, (reps, 1)), np.tile(s, (reps, 1))


def rope_matrix(d, reps):
    rot = d // 4
    half = rot // 2
    m = np.zeros((reps * d, reps * d), np.float32)
    for r in range(reps):
        o = r * d
        for i in range(half):
            m[o + i + half, o + i] = -1.0
            m[o + i, o + i + half] = 1.0
    return m


def mask_biases():
    kl = np.arange(128)[:, None]
    ql = np.arange(128)[None, :]
    diag = np.where(kl <= ql, 0.0, NEGB).astype(np.float32)
    prev = np.where(kl > ql, 0.0, NEGB).astype(np.float32)
    return np.tile(diag, (1, 4)), np.tile(prev, (1, 4))


def qk_norm_rope(P, src, dst_bf, gain_col, ones_blk, R_sb, cosf, sinf, sq, rr, tmpf, psbig, eps_col, inv_d, key, ntok, sfx=""):
    cols = [(c0, min(512, ntok - c0)) for c0 in range(0, ntok, 512)]
    ng = (len(cols) + 3) // 4
    gw = [sum(n for (_, n) in cols[4 * g:4 * g + 4]) for g in range(ng)]
    P.add("act", lambda e: e.activation(out=sq[:, :ntok], in_=src, func=AF.Square), r=[key], w=["sq" + sfx])
    for c, (c0, n) in enumerate(cols):
        P.add("pe", lambda e, c=c, c0=c0, n=n: e.matmul(psbig[c // 4][:, (c % 4) * 512:(c % 4) * 512 + n], lhsT=ones_blk[:],
                                                        rhs=sq[:, c0:c0 + n], start=True, stop=True),
              r=["sq" + sfx, "onesblk"], w=[("psbig" + sfx, c // 4)])
    for g in range(ng):
        n = gw[g]
        P.add("act", lambda e, g=g, n=n: e.activation(out=rr[:, g * 2048:g * 2048 + n], in_=psbig[g][:, :n], func=AF.Sqrt,
                                                      bias=eps_col[:], scale=inv_d),
              r=["epsc"], w=[("psbig" + sfx, g), ("rr" + sfx, g)])
        P.add("dve", lambda e, g=g, n=n: e.reciprocal(out=rr[:, g * 2048:g * 2048 + n], in_=rr[:, g * 2048:g * 2048 + n]),
              r=[("rr" + sfx, g)], w=[("rr" + sfx, g)])
    rrk = [("rr" + sfx, g) for g in range(ng)]
    P.add("dve", lambda e: e.scalar_tensor_tensor(out=src, in0=src, scalar=gain_col, in1=rr[:, :ntok], op0=ALU.mult, op1=ALU.mult),
          r=[key, "gains"] + rrk, w=[key])
    for c, (c0, n) in enumerate(cols):
        P.add("pe", lambda e, c=c, c0=c0, n=n: e.matmul(psbig[c // 4][:, (c % 4) * 512:(c % 4) * 512 + n], lhsT=R_sb[:],
                                                        rhs=src[:, c0:c0 + n], start=True, stop=True),
              r=[key, "Rm"], w=[("psbig" + sfx, c // 4)])
    for g in range(ng):
        n = gw[g]
        P.add("dve", lambda e, g=g, n=n: e.tensor_tensor(out=tmpf[:, g * 2048:g * 2048 + n], in0=psbig[g][:, :n],
                                                         in1=sinf[:, g * 2048:g * 2048 + n], op=ALU.mult),
              r=["tabs" + sfx], w=[("psbig" + sfx, g), ("tmpf" + sfx, g)])
    P.add("pool", lambda e: e.tensor_tensor(out=src, in0=src, in1=cosf[:, :ntok], op=ALU.mult), r=[key, "tabs" + sfx], w=[key])
    P.add("pool", lambda e: e.tensor_tensor(out=dst_bf, in0=src, in1=tmpf[:, :ntok], op=ALU.add),
          r=[key] + [("tmpf" + sfx, g) for g in range(ng)], w=[key + "_bf"])


def build_swaB():
    S = SEQ
    CH = 1024
    nc = bass.Bass("TRN2", target_bir_lowering=False)
    q_d = nc.dram_tensor("q", [4, 128, S], F32, kind="ExternalInput").ap()
    k_d = nc.dram_tensor("k2", [128, S], F32, kind="ExternalInput").ap()
    v_d = nc.dram_tensor("v", [128, 32, 64], F32, kind="ExternalInput").ap()
    gq_d = nc.dram_tensor("gq", [128, 1], F32, kind="ExternalInput").ap()
    gk_d = nc.dram_tensor("gk", [128, 1], F32, kind="ExternalInput").ap()
    es_d = nc.dram_tensor("esink", [1, 2, 512], F32, kind="ExternalInput").ap()
    cos_d = nc.dram_tensor("cosf", [128, S], F32, kind="ExternalInput").ap()
    sin_d = nc.dram_tensor("sinf", [128, S], F32, kind="ExternalInput").ap()
    R_d = nc.dram_tensor("Rm", [128, 128], F32, kind="ExternalInput").ap()
    id_d = nc.dram_tensor("ident", [128, 128], F32, kind="ExternalInput").ap()
    bd_d = nc.dram_tensor("bdiag", [128, 512], F32, kind="ExternalInput").ap()
    bp_d = nc.dram_tensor("bprev", [128, 512], F32, kind="ExternalInput").ap()
    ob_d = nc.dram_tensor("onesblk", [128, 128], F32, kind="ExternalInput").ap()
    out_d = nc.dram_tensor("oT", [512, S], F32, kind="ExternalOutput").ap()
    with ExitStack() as stack:
        P = Prog(nc, stack)
        qbf = P.sb([128, 4, S], BF16, "qbf")
        kbf = P.sb([128, S], BF16, "kbf")
        vbf = P.sb([128, 32, 64], BF16, "vbf")
        work = [P.sb([128, CH], F32, f"work{i}") for i in range(2)]
        tcos = [P.sb([128, CH], F32, f"tcos{i}") for i in range(2)]
        tsin = [P.sb([128, CH], F32, f"tsin{i}") for i in range(2)]
        rr = [P.sb([128, CH], F32, f"rr{i}") for i in range(2)]
        tmpf = [P.sb([128, CH], F32, f"tmpf{i}") for i in range(2)]
        sq = [P.sb([128, CH], BF16, f"sq{i}") for i in range(2)]
        R_sb = P.sb([128, 128], F32, "R_sb")
        ident = P.sb([128, 128], BF16, "ident")
        bdiag = P.sb([128, 512], BF16, "bdiag")
        bprev = P.sb([128, 512], BF16, "bprev")
        onesblk = P.sb([128, 128], BF16, "onesblk")
        ones64 = P.sb([128, 64], BF16, "ones64")
        ones1 = P.sb([1, 64], BF16, "ones1")
        esink = P.sb([1, 2, 512], BF16, "esink")
        gq = P.sb([128, 1], F32, "gq")
        gk = P.sb([128, 1], F32, "gk")
        eps_col = P.sb([128, 1], F32, "eps_col")
        pt = [P.sb([128, 512], BF16, f"pt{i}") for i in range(4)]
        rden = [P.sb([64, 512], F32, f"rden{i}") for i in range(2)]
        ost = [P.sb([64, 4, 512], F32, f"ost{i}") for i in range(2)]
        psbig = [P.ps([128, 2048], F32, f"psbig{i}") for i in range(2)]

        P.add("pool", lambda e: e.memset(eps_col[:], float(NORM_EPS)), w=["epsc"])
        P.add("pool", lambda e: e.memset(ones64[:], 1.0), w=["ones64"])
        P.add("pool", lambda e: e.memset(ones1[:], 1.0), w=["ones1"])
        P.dma("sp", R_sb[:], R_d, w=["Rm"])
        P.dma("sp", gq[:], gq_d, w=["gains"])
        P.dma("sp", gk[:], gk_d, w=["gains"])
        P.dma("pool", ident[:], id_d, w=["ident"])
        P.dma("pool", bdiag[:], bd_d, w=["bias"])
        P.dma("pool", bprev[:], bp_d, w=["bias"])
        P.dma("pool", onesblk[:], ob_d, w=["onesblk"])
        esf = P.sb([1, 2, 512], F32, "esf")
        P.dma("sp", esf[:], es_d, w=["esf"])
        P.add("act", lambda e: e.activation(out=esink[:], in_=esf[:], func=AF.Exp), r=["esf"], w=["esink"])
        P.dma("pool", vbf[:], v_d, w=["v"])
        jobs = [(k_d, gk[:, 0:1], lambda c0: kbf[:, c0:c0 + CH])]
        for p in range(4):
            jobs.append((q_d[p], gq[:, 0:1], lambda c0, p=p: qbf[:, p, c0:c0 + CH]))
        n1 = 0
        for (src_d, gcol, dstf) in jobs:
            for ci in range(S // CH):
                w_ = n1 % 2
                n1 += 1
                c0 = ci * CH
                P.dma("sp", work[w_][:], src_d[:, c0:c0 + CH], w=[f"w{w_}"])
                P.dma("sp", tcos[w_][:], cos_d[:, c0:c0 + CH], w=[f"tabs{w_}"])
                P.dma("sp", tsin[w_][:], sin_d[:, c0:c0 + CH], w=[f"tabs{w_}"])
                qk_norm_rope(P, work[w_][:], dstf(c0), gcol, onesblk, R_sb, tcos[w_], tsin[w_], sq[w_], rr[w_], tmpf[w_], [psbig[w_]],
                             eps_col, 1.0 / 64, f"w{w_}", CH, sfx=str(w_))
        qkeys = ["w0_bf", "w1_bf"]
        u = 0
        sc = 0
        pend = []

        def flush():
            while pend:
                pend.pop(0)()

        outv = out_d.rearrange("(p e d) t -> e d p t", p=4, e=2, d=64)
        for qg in range(SEQ // 512):
            for e in range(2):
                os_ = ost[(qg * 2 + e) % 2]
                oskey = ("ost", (qg * 2 + e) % 2)
                for qi in range(4):
                    qt = qg * 4 + qi
                    kts = [kt for kt in (qt - 1, qt) if kt >= 0]
                    oset = u % 2
                    u += 1
                    ops_ = psbig[1][0:64, oset * 1024:oset * 1024 + 512]
                    dps_ = psbig[1][0:64, oset * 1024 + 512:oset * 1024 + 1024]
                    okey = ("ops", oset)
                    for i, kt in enumerate(kts):
                        sb_ = sc % 4
                        sc += 1
                        sps = psbig[0][:, sb_ * 512:(sb_ + 1) * 512]
                        bias = bdiag if kt == qt else bprev
                        P.add("pe", lambda e_, e=e, kt=kt, qt=qt, sps=sps: e_.matmul(
                            sps, lhsT=kbf[64 * e:64 * e + 64, kt * 128:(kt + 1) * 128],
                            rhs=qbf[64 * e:64 * e + 64, :, qt * 128:(qt + 1) * 128], start=True, stop=False),
                            r=qkeys, w=[("sps", sb_)])
                        P.add("pe", lambda e_, sps=sps, bias=bias: e_.matmul(sps, lhsT=ident[:], rhs=bias[:], start=False, stop=True),
                              r=["ident", "bias"], w=[("sps", sb_)])
                        P.add("act", lambda e_, sps=sps, sb_=sb_: e_.activation(out=pt[sb_][:], in_=sps, func=AF.Exp, scale=0.125),
                              r=[], w=[("sps", sb_), ("pt", sb_)])
                        flush()

                        def pv(kt=kt, sb_=sb_, ops_=ops_, dps_=dps_, i=i, okey=okey, nk=len(kts)):
                            P.add("pe", lambda e_: e_.matmul(ops_, lhsT=vbf[:, kt, :], rhs=pt[sb_][:], start=(i == 0), stop=(i == nk - 1)),
                                  r=[("pt", sb_), "v"], w=[okey])
                            P.add("pe", lambda e_: e_.matmul(dps_, lhsT=ones64[:], rhs=pt[sb_][:], start=(i == 0), stop=False),
                                  r=[("pt", sb_), "ones64"], w=[okey])
                        pend.append(pv)

                    def fin(e=e, dps_=dps_, ops_=ops_, oset=oset, os_=os_, qi=qi, okey=okey, oskey=oskey, qg=qg):
                        P.add("pe", lambda e_: e_.matmul(dps_, lhsT=ones1[:], rhs=esink[:, e, :], start=False, stop=True),
                              r=["ones1", "esink"], w=[okey])
                        P.add("dve", lambda e_: e_.reciprocal(out=rden[oset][:], in_=dps_), r=[], w=[okey, ("rden", oset)])
                        P.add("dve", lambda e_: e_.tensor_tensor(
                            out=os_[:, :, qi * 128:(qi + 1) * 128], in0=ops_.rearrange("d (p q) -> d p q", p=4),
                            in1=rden[oset][:].rearrange("d (p q) -> d p q", p=4), op=ALU.mult),
                            r=[("rden", oset)], w=[okey, oskey])
                        if qi == 3:
                            P.dma("sp", outv[e, :, :, qg * 512:(qg + 1) * 512], os_[:], r=[oskey], w=[("out", qg, e)])
                    pend.append(fin)
        flush()
        P.finish([("out", qg, e) for qg in range(SEQ // 512) for e in range(2)])
        P.emit()
    return nc


def run_swaB(projT, q_norm, k_norm, sinks, trace=False):
    S = SEQ
    nc = get_prog("swaB", build_swaB)
    cosf, sinf = rope_tables(64, np.arange(S), 2)
    Rm = rope_matrix(64, 2)
    ident = np.eye(128, dtype=np.float32)
    bdiag, bprev = mask_biases()
    onesblk = np.kron(np.eye(2, dtype=np.float32), np.ones((64, 64), np.float32))
    gq = np.tile(q_norm.astype(np.float32), 2).reshape(128, 1)
    gk = np.tile(k_norm.astype(np.float32), 2).reshape(128, 1)
    in_maps = []
    for c in range(NCORES):
        b, g = c // 4, c % 4
        t0 = b * S
        q = np.ascontiguousarray(projT[g * 512:(g + 1) * 512, t0:t0 + S].reshape(4, 128, S))
        k = projT[2048 + g * 64:2048 + (g + 1) * 64, t0:t0 + S]
        k2 = np.ascontiguousarray(np.concatenate([k, k], axis=0))
        v = projT[2304 + g * 64:2304 + (g + 1) * 64, t0:t0 + S].T
        v = np.ascontiguousarray(v.reshape(32, 128, 64).transpose(1, 0, 2))
        sk = sinks[g * 8:(g + 1) * 8].astype(np.float32).reshape(4, 2)
        es = np.ascontiguousarray(np.repeat(sk.T[:, :, None], 128, axis=2).reshape(1, 2, 512))
        in_maps.append(dict(q=q, k2=k2, v=v, gq=gq, gk=gk, esink=es, cosf=cosf, sinf=sinf, Rm=Rm, ident=ident,
                            bdiag=bdiag, bprev=bprev, onesblk=onesblk))
    res = run_bass_kernel_spmd(nc, in_maps, core_ids=list(range(NCORES)), trace=trace)
    if trace:
        print("swaB exec_time_ns", res.exec_time_ns)
    oT = np.zeros((D_MODEL, BATCH * S), np.float32)
    for c in range(NCORES):
        b, g = c // 4, c % 4
        oT[g * 512:(g + 1) * 512, b * S:(b + 1) * S] = res.results[c]["oT"]
    return oT


def gla_consts():
    j = np.arange(128)[:, None]
    i = np.arange(128)[None, :]
    same = (j // 64) == (i // 64)
    T2 = np.where(same & (j <= i), -1.0 / 16.0, 0.0).astype(np.float32)
    U2 = np.where(same & (j > i), -1.0 / 16.0, 0.0).astype(np.float32)
    M2 = np.where(same & (j <= i), 1.0, 0.0).astype(np.float32)
    return T2, U2, M2


def build_glaB(dbg=False):
    S = SEQ
    NT = S // 128
    nc = bass.Bass("TRN2", target_bir_lowering=False)
    glr_d = nc.dram_tensor("glrT", [16, S], F32, kind="ExternalInput").ap()
    q_d = nc.dram_tensor("qT", [128, 2, S], F32, kind="ExternalInput").ap()
    k_d = nc.dram_tensor("kT", [128, 2, S], F32, kind="ExternalInput").ap()
    ktm_d = nc.dram_tensor("ktm", [NT, 128, 256], F32, kind="ExternalInput").ap()
    v_d = nc.dram_tensor("vtm", [NT, 128, 512], F32, kind="ExternalInput").ap()
    r_d = nc.dram_tensor("rT", [128, 4, S], F32, kind="ExternalInput").ap()
    wg_d = nc.dram_tensor("wg", [16, 256], F32, kind="ExternalInput").ap()
    bg_d = nc.dram_tensor("bg", [1, 256], F32, kind="ExternalInput").ap()
    gn_d = nc.dram_tensor("gn", [128, 4], F32, kind="ExternalInput").ap()
    T2_d = nc.dram_tensor("T2", [128, 128], F32, kind="ExternalInput").ap()
    U2_d = nc.dram_tensor("U2", [128, 128], F32, kind="ExternalInput").ap()
    M2_d = nc.dram_tensor("M2", [128, 128], F32, kind="ExternalInput").ap()
    out_d = nc.dram_tensor("oT", [512, S], F32, kind="ExternalOutput").ap()
    with ExitStack() as stack:
        P = Prog(nc, stack)
        qp = P.sb([128, 2, S], BF16, "qp")
        kp = P.sb([128, 2, S], BF16, "kp")
        kpp = P.sb([128, NT, 256], BF16, "kpp")
        vbf = P.sb([128, NT, 512], BF16, "vbf")
        att = P.sb([128, NT, 128], BF16, "att")
        explast = P.sb([128, 2, 2 * NT], F32, "explast")
        glr = P.sb([16, S], F32, "glr")
        wg = P.sb([16, 256], F32, "wg")
        bg = P.sb([1, 256], F32, "bg")
        gn = P.sb([128, 4], F32, "gn")
        T2 = P.sb([128, 128], F32, "T2")
        U2 = P.sb([128, 128], F32, "U2")
        M2 = P.sb([128, 128], F32, "M2")
        ones1 = P.sb([1, 128], F32, "ones1")
        onesb = P.sb([128, 128], BF16, "onesb")
        eps_col = P.sb([128, 1], F32, "eps_col")
        qt_ = [P.sb([128, 2, 128], F32, f"qt{i}") for i in range(2)]
        kt_ = [P.sb([128, 2, 128], F32, f"kt{i}") for i in range(2)]
        ktm = [P.sb([128, 256], F32, f"ktm{i}") for i in range(2)]
        e1 = [P.sb([128, 256], F32, f"e1{i}") for i in range(2)]
        la = [P.sb([128, 256], F32, f"la{i}") for i in range(2)]
        ET = [P.sb([128, 2, 128], F32, f"ET{i}") for i in range(2)]
        EinvT = [P.sb([128, 2, 128], F32, f"EinvT{i}") for i in range(2)]
        Elmc = [P.sb([128, 256], F32, f"Elmc{i}") for i in range(2)]
        Sst = P.sb([128, 2, 512], F32, "Sst")
        Sbf = [P.sb([128, 2, 512], BF16, f"Sbf{i}") for i in range(2)]
        ot = [P.sb([128, 4, 128], F32, f"ot{i}") for i in range(2)]
        osq = [P.sb([128, 4, 128], BF16, f"osq{i}") for i in range(2)]
        rs = [P.sb([128, 128], F32, f"rs{i}") for i in range(2)]
        rt = [P.sb([128, 4, 128], F32, f"rt{i}") for i in range(2)]
        ost = [P.sb([128, 4, 512], F32, f"ost{i}") for i in range(2)]
        ps = [P.ps([128, 512], F32, f"psb{i}") for i in range(8)]

        P.add("pool", lambda e: e.memset(eps_col[:], float(NORM_EPS)), w=["epsc"])
        P.add("pool", lambda e: e.memset(ones1[:], 1.0), w=["ones1"])
        P.add("pool", lambda e: e.memset(onesb[:], 1.0), w=["onesb"])
        P.add("pool", lambda e: e.memset(Sst[:], 0.0), w=["S"])
        for t_, d_, k_ in ((glr, glr_d, "glr"), (wg, wg_d, "wg"), (bg, bg_d, "bg"), (gn, gn_d, "gn"), (T2, T2_d, "T2"),
                           (U2, U2_d, "U2"), (M2, M2_d, "M2")):
            P.dma("sp", t_[:], d_, w=[k_])

        def pass1(t):
            b = t % 2
            tok = slice(t * 128, (t + 1) * 128)
            P.dma("sp", qt_[b][:], q_d[:, :, tok], w=[("qt", b)])
            P.dma("sp", kt_[b][:], k_d[:, :, tok], w=[("kt", b)])
            P.dma("sp", ktm[b][:], ktm_d[t], w=[("ktm", b)])
            P.dma("pool", vbf[:, t, :], v_d[t], w=[("v", t)])
            pA = ps[2 * b]
            pB = ps[2 * b + 1]
            P.add("pe", lambda e: e.matmul(pA[:, 0:256], lhsT=glr[:, tok], rhs=wg[:], start=True, stop=False),
                  r=["glr", "wg"], w=[("pA", b)])
            P.add("pe", lambda e: e.matmul(pA[:, 0:256], lhsT=ones1[:], rhs=bg[:], start=False, stop=True),
                  r=["ones1", "bg"], w=[("pA", b)])
            P.add("act", lambda e: e.activation(out=e1[b][:], in_=pA[:, 0:256], func=AF.Exp, scale=-1.0), r=[("pA", b)], w=[("e1", b)])
            P.add("act", lambda e: e.activation(out=la[b][:], in_=e1[b][:], func=AF.Ln, bias=1.0), r=[("e1", b)], w=[("la", b)])
            for dc in range(2):
                P.add("pe", lambda e, dc=dc: e.matmul(pA[:, 256 + dc * 128:256 + (dc + 1) * 128], lhsT=la[b][:, dc * 128:(dc + 1) * 128],
                                                      rhs=T2[:], start=True, stop=True), r=[("la", b), "T2"], w=[("pA2", b)])
            P.add("pe", lambda e: e.matmul(pB[:, 0:256], lhsT=U2[:], rhs=la[b][:], start=True, stop=True), r=[("la", b), "U2"], w=[("pB", b)])
            cumT = pA[:, 256:512].rearrange("p (c i) -> p c i", c=2)
            P.add("act", lambda e: e.activation(out=ET[b][:], in_=cumT, func=AF.Exp), r=[("pA2", b)], w=[("ET", b)])
            P.add("act", lambda e: e.activation(out=EinvT[b][:], in_=cumT, func=AF.Exp, scale=-1.0), r=[("pA2", b)], w=[("EinvT", b)])
            P.add("act", lambda e: e.activation(out=Elmc[b][:], in_=pB[:, 0:256], func=AF.Exp), w=[("pB", b), ("Elmc", b)])
            P.add("dve", lambda e: e.scalar_tensor_tensor(out=qp[:, :, tok], in0=qt_[b][:], scalar=float(256 ** -0.5), in1=ET[b][:],
                                                          op0=ALU.mult, op1=ALU.mult), r=[("qt", b), ("ET", b)], w=[("qp", t)])
            P.add("dve", lambda e: e.tensor_tensor(out=kp[:, :, tok], in0=kt_[b][:], in1=EinvT[b][:], op=ALU.mult),
                  r=[("kt", b), ("EinvT", b)], w=[("kp", t)])
            P.add("pool", lambda e: e.tensor_tensor(out=kpp[:, t, :], in0=ktm[b][:], in1=Elmc[b][:], op=ALU.mult),
                  r=[("ktm", b), ("Elmc", b)], w=[("kpp", t)])
            P.add("pool", lambda e: e.tensor_copy(out=explast[:, :, 2 * t:2 * t + 2], in_=ET[b][:, :, 63:128:64]),
                  r=[("ET", b)], w=[("explast", t)])
            for dc in range(2):
                P.add("pe", lambda e, dc=dc: e.matmul(pB[:, 256:384], lhsT=kp[:, dc, tok], rhs=qp[:, dc, tok], start=(dc == 0), stop=(dc == 1)),
                      r=[("kp", t), ("qp", t)], w=[("pB", b)])
            P.add("dve", lambda e: e.tensor_tensor(out=att[:, t, :], in0=pB[:, 256:384], in1=M2[:], op=ALU.mult),
                  r=["M2"], w=[("pB", b), ("att", t)])

        sidx = [0]

        def pass2(t):
            b = t % 2
            tok0 = t * 128
            pO = ps[6]
            pN = ps[7]
            P.dma("sp", rt[b][:], r_d[:, :, tok0:tok0 + 128], w=[("rt", b)])
            for dvc in range(4):
                P.add("pe", lambda e, dvc=dvc: e.matmul(pO[:, dvc * 128:(dvc + 1) * 128], lhsT=vbf[:, t, dvc * 128:(dvc + 1) * 128],
                                                        rhs=att[:, t, :], start=(dvc == 0), stop=False),
                      r=[("v", t), ("att", t)], w=["pO"])
            for c in range(2):
                ch = 2 * t + c
                cs = slice(tok0 + 64 * c, tok0 + 64 * c + 64)
                if ch > 0:
                    sb_ = Sbf[sidx[0] % 2]
                    sk = ("Sbf", sidx[0] % 2)
                    for dvc in range(4):
                        for dc in range(2):
                            P.add("pe", lambda e, dvc=dvc, dc=dc, sb_=sb_, c=c, cs=cs: e.matmul(
                                pO[:, dvc * 128 + 64 * c:dvc * 128 + 64 * c + 64], lhsT=sb_[:, dc, dvc * 128:(dvc + 1) * 128],
                                rhs=qp[:, dc, cs], start=False, stop=(dc == 1 and c == 1)),
                                r=[sk, ("qp", t)], w=["pO"])
                for dc in range(2):
                    pk = ps[4 + dc]
                    P.add("pe", lambda e, dc=dc, pk=pk, c=c: e.matmul(pk[:], lhsT=kpp[64 * c:64 * c + 64, t, dc * 128:(dc + 1) * 128],
                                                                 rhs=vbf[64 * c:64 * c + 64, t, :], start=True, stop=True),
                          r=[("kpp", t), ("v", t)], w=[("pk", dc)])
                sidx[0] += 1
                sb_ = Sbf[sidx[0] % 2]
                sk = ("Sbf", sidx[0] % 2)
                for dc in range(2):
                    pk = ps[4 + dc]
                    P.add("dve", lambda e, dc=dc, pk=pk, ch=ch: e.scalar_tensor_tensor(
                        out=Sst[:, dc, :], in0=Sst[:, dc, :], scalar=explast[:, dc, ch:ch + 1], in1=pk[:], op0=ALU.mult, op1=ALU.add),
                        r=[("pk", dc), ("explast", t), "S"], w=["S"])
                P.add("act", lambda e, sb_=sb_: e.copy(out=sb_[:], in_=Sst[:]), r=["S"], w=[sk])
                if dbg and t == 0 and c == 0:
                    P.dump("d_S0", Sst[:], ["S"])
                    P.dump("d_Sbf0", sb_[:], [sk])
            P.add("act", lambda e: e.copy(out=ot[b][:], in_=pO[:].rearrange("p (c i) -> p c i", c=4)), r=["pO"], w=[("ot", b)])
            if dbg and t == 0:
                P.dump("d_ot", ot[0][:], [("ot", 0)])
                P.dump("d_S", Sst[:], ["S"])
            P.add("act", lambda e: e.activation(out=osq[b][:], in_=ot[b][:], func=AF.Square), r=[("ot", b)], w=[("osq", b)])
            for dvc in range(4):
                P.add("pe", lambda e, dvc=dvc: e.matmul(pN[:, 0:128], lhsT=onesb[:], rhs=osq[b][:, dvc, :], start=(dvc == 0), stop=(dvc == 3)),
                      r=[("osq", b), "onesb"], w=["pN"])
            P.add("act", lambda e: e.activation(out=rs[b][:], in_=pN[:, 0:128], func=AF.Sqrt, bias=eps_col[:], scale=1.0 / 512),
                  r=["pN", "epsc"], w=[("rs", b)])
            P.add("dve", lambda e: e.reciprocal(out=rs[b][:], in_=rs[b][:]), r=[("rs", b)], w=[("rs", b)])
            P.add("act", lambda e: e.activation(out=rt[b][:], in_=rt[b][:], func=AF.Silu), r=[("rt", b)], w=[("rt", b)])
            o4 = ost[(t // 4) % 2]
            for dvc in range(4):
                P.add("dve", lambda e, dvc=dvc: e.scalar_tensor_tensor(out=ot[b][:, dvc, :], in0=ot[b][:, dvc, :], scalar=gn[:, dvc:dvc + 1],
                                                                       in1=rs[b][:], op0=ALU.mult, op1=ALU.mult),
                      r=[("ot", b), ("rs", b), "gn"], w=[("ot", b)])
            P.add("pool", lambda e: e.tensor_tensor(out=o4[:, :, (t % 4) * 128:(t % 4 + 1) * 128], in0=ot[b][:], in1=rt[b][:], op=ALU.mult),
                  r=[("ot", b), ("rt", b)], w=[("ost", (t // 4) % 2)])
            if t % 4 == 3:
                g4 = t // 4
                P.dma("sp", out_d.rearrange("(c p) t -> p c t", p=128)[:, :, g4 * 512:(g4 + 1) * 512], o4[:],
                      r=[("ost", g4 % 2)], w=[("out", g4)])

        pass1(0)
        if dbg:
            P.dump("d_la", la[0][:], [("la", 0)])
            P.dump("d_ET", ET[0][:], [("ET", 0)])
            P.dump("d_Elmc", Elmc[0][:], [("Elmc", 0)])
            P.dump("d_att", att[:, 0, :], [("att", 0)])
            P.dump("d_qp", qp[:, :, 0:128], [("qp", 0)])
            P.dump("d_kp", kp[:, :, 0:128], [("kp", 0)])
            P.dump("d_kpp", kpp[:, 0, :], [("kpp", 0)])
            P.dump("d_explast", explast[:, :, 0:2], [("explast", 0)])
        pass1(1)
        for t in range(NT):
            if t + 2 < NT:
                pass1(t + 2)
            pass2(t)
        P.finish([("out", g4) for g4 in range(NT // 4)])
        P.emit()
    return nc


def run_glaB(projT, w_gate_up, b_gate, o_norm, trace=False):
    S = SEQ
    nc = get_prog("glaB", build_glaB)
    T2, U2, M2 = gla_consts()
    gn = np.ascontiguousarray(o_norm.astype(np.float32).reshape(4, 128).T)
    in_maps = []
    for c in range(NCORES):
        b, hd = c // 4, c % 4
        ts = slice(b * S, (b + 1) * S)
        qT = to_fm(projT[hd * 256:(hd + 1) * 256, ts])
        kT_ = projT[1024 + hd * 256:1024 + (hd + 1) * 256, ts]
        kT = to_fm(kT_)
        ktm = np.ascontiguousarray(kT_.T.reshape(S // 128, 128, 256))
        vtm = np.ascontiguousarray(projT[2048 + hd * 512:2048 + (hd + 1) * 512, ts].T.reshape(S // 128, 128, 512))
        glrT = np.ascontiguousarray(projT[4096:4112, ts])
        rT = to_fm(projT[4112 + hd * 512:4112 + (hd + 1) * 512, ts])
        wg = np.ascontiguousarray(w_gate_up[:, hd * 256:(hd + 1) * 256])
        bg = np.ascontiguousarray(b_gate[hd * 256:(hd + 1) * 256].reshape(1, 256))
        in_maps.append(dict(glrT=glrT, qT=qT, kT=kT, ktm=ktm, vtm=vtm, rT=rT, wg=wg, bg=bg, gn=gn, T2=T2, U2=U2, M2=M2))
    res = run_bass_kernel_spmd(nc, in_maps, core_ids=list(range(NCORES)), trace=trace)
    if trace:
        print("glaB exec_time_ns", res.exec_time_ns)
    oT = np.zeros((D_MODEL, BATCH * S), np.float32)
    for c in range(NCORES):
        b, hd = c // 4, c % 4
        oT[hd * 512:(hd + 1) * 512, b * S:(b + 1) * S] = res.results[c]["oT"]
    return oT


NSA_NT = SEQ // 128
NSA_NCMP = (SEQ - 32) // 16 + 1


def nsa_consts():
    S = SEQ
    NT = NSA_NT
    cm = np.zeros((128, 48, 128), np.float32)
    ql = np.arange(128)[None, :]
    cl = np.arange(128)[:, None]
    for qt in range(NT):
        for ct in range(2):
            if ct == 1 and qt < 16:
                continue
            idx = qt if ct == 0 else 32 + qt - 16
            c = cl + 128 * ct
            vis = (16 * c + 31 <= 128 * qt + ql) & (c < NSA_NCMP)
            cm[:, idx, :] = np.where(vis, 0.0, NEGB)
    bonus = np.zeros((128, NT, 64), np.float32)
    j = np.arange(64)[None, :]
    for qt in range(NT):
        pos = 128 * qt + np.arange(128)[:, None]
        bq = pos // 64
        forced = (j == 0) | (j == bq) | (j == bq - 1)
        bonus[:, qt, :] = np.where(j <= bq, np.where(forced, 1e4, 0.0), -1e30)
    E = np.zeros((64, NT, 128), np.float32)
    for kt in range(NT):
        E[2 * kt, kt, :64] = 1.0
        E[2 * kt + 1, kt, 64:] = 1.0
    c0 = np.arange(256) * 16
    s0 = np.arange(64) * 64
    ov = np.minimum(c0[:, None] + 32, s0[None, :] + 64) - np.maximum(c0[:, None], s0[None, :])
    ov = (np.clip(ov, 0, None) / 32.0).astype(np.float32)
    ov[NSA_NCMP:] = 0.0
    ov = np.ascontiguousarray(ov.reshape(2, 128, 64).transpose(1, 0, 2))
    return cm, bonus, E, ov


def build_nsaB(dbg=False):
    S = SEQ
    NT = NSA_NT
    SCALE = float(128 ** -0.5)
    nc = bass.Bass("TRN2", target_bir_lowering=False)

    def din(name, shape):
        return nc.dram_tensor(name, list(shape), F32, kind="ExternalInput").ap()

    q_d = din("q", [4, 128, S])
    kc_d = din("kc", [128, S])
    vc_d = din("vc", [128, S])
    ks_d = din("ks", [128, S])
    kw_d = din("kw", [128, S])
    vs_d = din("vs", [128, NT, 128])
    vw_d = din("vw", [128, NT, 128])
    gate_d = din("gate", [12, S])
    gq_d = din("gq", [128, 1])
    gk_d = din("gk", [128, 3])
    pe_d = din("peT", [2, 128, 32])
    w1_d = din("w1", [2, 128, 32, 128])
    w2_d = din("w2", [2, 128, 128])
    cos_d = din("cosf", [128, S])
    sin_d = din("sinf", [128, S])
    cosc_d = din("cosc", [128, 256])
    sinc_d = din("sinc", [128, 256])
    R_d = din("Rm", [128, 128])
    id_d = din("ident", [128, 128])
    bd_d = din("bdiag", [128, 512])
    bp_d = din("bprev", [128, 512])
    cm_d = din("cmask", [128, 48, 128])
    bon_d = din("bonus", [128, NT, 64])
    E_d = din("Emat", [64, NT, 128])
    ov_d = din("ov", [128, 2, 64])
    out_d = nc.dram_tensor("oT", [512, S], F32, kind="ExternalOutput").ap()
    with ExitStack() as stack:
        P = Prog(nc, stack)
        qbf = P.sb([128, 4, S], BF16, "qbf")
        ksbf = P.sb([128, S], BF16, "ksbf")
        kwbf = P.sb([128, S], BF16, "kwbf")
        vsb = P.sb([128, NT, 128], BF16, "vsb")
        vwb = P.sb([128, NT, 128], BF16, "vwb")
        kcm = P.sb([128, 256], BF16, "kcm")
        vcm = P.sb([128, 2, 128], BF16, "vcm")
        R_sb = P.sb([128, 128], F32, "R_sb")
        identf = P.sb([128, 128], F32, "identf")
        ident = P.sb([128, 128], BF16, "identb")
        bdiag = P.sb([128, 512], BF16, "bdiag")
        bprev = P.sb([128, 512], BF16, "bprev")
        cmask = P.sb([128, 48, 128], BF16, "cmask")
        bonus = P.sb([128, NT, 64], F32, "bonus")
        Emat = P.sb([64, NT, 128], BF16, "Emat")
        ov = P.sb([128, 2, 64], BF16, "ov")
        onesb = P.sb([128, 128], BF16, "onesb")
        gq = P.sb([128, 1], F32, "gq")
        gk = P.sb([128, 3], F32, "gk")
        eps_col = P.sb([128, 1], F32, "eps_col")
        region = P.sb([128, 12288], F32, "region")
        psbig = [P.ps([128, 2048], F32, f"psbig{i}") for i in range(2)]

        def bank(i):
            return psbig[i // 4][:, (i % 4) * 512:(i % 4 + 1) * 512]

        roff = [0]

        def rsb(shape, dtype):
            exact = int(np.prod(shape[1:])) * (2 if dtype == BF16 else 4)
            assert exact % 4 == 0
            nbytes = (exact + 31) // 32 * 32
            a = roff[0] // 4
            v = region[0:shape[0], a:a + exact // 4]
            roff[0] += nbytes
            assert roff[0] <= 12288 * 4, roff[0]
            if dtype == BF16:
                v = v.bitcast(BF16)
            if len(shape) == 3:
                v = v.rearrange("p (a b) -> p a b", a=shape[1])
            return v

        P.add("pool", lambda e: e.memset(eps_col[:], float(NORM_EPS)), w=["epsc"])
        P.add("pool", lambda e: e.memset(onesb[:], 1.0), w=["onesblk"])
        for t_, d_, k_ in ((R_sb, R_d, "Rm"), (identf, id_d, "identf"), (gq, gq_d, "gains"), (gk, gk_d, "gains"), (bonus, bon_d, "bonus")):
            P.dma("sp", t_[:], d_, w=[k_])
        for t_, d_, k_ in ((ident, id_d, "ident"), (bdiag, bd_d, "bias"), (bprev, bp_d, "bias"), (cmask, cm_d, "cmask"),
                           (Emat, E_d, "Emat"), (ov, ov_d, "ov"), (vsb, vs_d, "vs"), (vwb, vw_d, "vw")):
            P.dma("pool", t_[:], d_, w=[k_])

        CH = 1024
        work = [rsb([128, CH], F32) for _ in range(2)]
        tcos = [rsb([128, CH], F32) for _ in range(2)]
        tsin = [rsb([128, CH], F32) for _ in range(2)]
        rr = [rsb([128, CH], F32) for _ in range(2)]
        tmpf = [rsb([128, CH], F32) for _ in range(2)]
        sq = [rsb([128, CH], BF16) for _ in range(2)]
        jobs = [(q_d[h], gq[:, 0:1], lambda c0, h=h: qbf[:, h, c0:c0 + CH]) for h in range(4)]
        jobs.append((ks_d, gk[:, 1:2], lambda c0: ksbf[:, c0:c0 + CH]))
        jobs.append((kw_d, gk[:, 2:3], lambda c0: kwbf[:, c0:c0 + CH]))
        n1 = 0
        for (src_d, gcol, dstf) in jobs:
            for ci in range(S // CH):
                w_ = n1 % 2
                n1 += 1
                c0 = ci * CH
                P.dma("sp", work[w_], src_d[:, c0:c0 + CH], w=[f"w{w_}"])
                P.dma("sp", tcos[w_], cos_d[:, c0:c0 + CH], w=[f"tabs{w_}"])
                P.dma("sp", tsin[w_], sin_d[:, c0:c0 + CH], w=[f"tabs{w_}"])
                qk_norm_rope(P, work[w_], dstf(c0), gcol, onesb, R_sb, tcos[w_], tsin[w_], sq[w_], rr[w_], tmpf[w_], [psbig[w_]], eps_col,
                             1.0 / 128, f"w{w_}", CH, sfx=str(w_))
        P.barrier()

        roff[0] = 0
        kcbf = rsb([128, S], BF16)
        vcbf = rsb([128, S], BF16)
        w1b = [rsb([128, 32, 128], BF16) for _ in range(2)]
        w2b = [rsb([128, 128], BF16) for _ in range(2)]
        peb = [rsb([128, 32], BF16) for _ in range(2)]
        ccol = [rsb([128, 1], F32) for _ in range(2)]
        xg = rsb([128, 256], F32)
        x2 = rsb([128, 256], F32)
        th = rsb([128, 256], F32)
        gel = rsb([128, 256], BF16)
        kcmf = rsb([128, 256], F32)
        tcc = rsb([128, 256], F32)
        tsc = rsb([128, 256], F32)
        rr2 = rsb([128, 256], F32)
        tmp2 = rsb([128, 256], F32)
        sq2 = rsb([128, 256], BF16)
        P.dma("pool", kcbf, kc_d, w=["kcbf"])
        P.dma("pool", vcbf, vc_d, w=["vcbf"])
        P.dma("sp", tcc, cosc_d, w=["tabs"])
        P.dma("sp", tsc, sinc_d, w=["tabs"])
        for i in range(2):
            P.dma("pool", w1b[i], w1_d[i], w=[("w1", i)])
            P.dma("pool", w2b[i], w2_d[i], w=[("w2", i)])
            P.dma("pool", peb[i], pe_d[i], w=[("pe", i)])
        for i, srcbf, skey in ((0, kcbf, "kcbf"), (1, vcbf, "vcbf")):
            pc = bank(0)
            pv = bank(1)
            for l in range(32):
                P.add("pe", lambda e, i=i, l=l, pc=pc: e.matmul(pc[:, 0:1], lhsT=w1b[i][:, l, :], rhs=peb[i][:, l:l + 1],
                                                             start=(l == 0), stop=(l == 31)),
                      r=[("w1", i), ("pe", i)], w=["pc"])
            P.add("act", lambda e, i=i, pc=pc: e.copy(out=ccol[i], in_=pc[:, 0:1]), r=[], w=["pc", ("ccol", i)])
            for l in range(32):
                P.add("pe", lambda e, i=i, l=l, pv=pv, srcbf=srcbf: e.matmul(
                    pv[:, 0:NSA_NCMP], lhsT=w1b[i][:, l, :], rhs=srcbf[:, l:l + 16 * (NSA_NCMP - 1) + 1:16],
                    start=(l == 0), stop=(l == 31)), r=[("w1", i), skey], w=["pv"])
            P.add("pool", lambda e: e.memset(xg, 0.0), w=["xg"])
            P.add("act", lambda e, i=i, pv=pv: e.activation(out=xg[:, 0:NSA_NCMP], in_=pv[:, 0:NSA_NCMP], func=AF.Identity, bias=ccol[i]),
                  r=[("ccol", i)], w=["pv", "xg"])
            P.add("pool", lambda e: e.tensor_tensor(out=x2, in0=xg, in1=xg, op=ALU.mult), r=["xg"], w=["x2"])
            P.add("pool", lambda e: e.tensor_scalar(out=x2, in0=x2, scalar1=0.044715, scalar2=1.0, op0=ALU.mult, op1=ALU.add),
                  r=["x2"], w=["x2"])
            P.add("pool", lambda e: e.tensor_tensor(out=x2, in0=x2, in1=xg, op=ALU.mult), r=["x2", "xg"], w=["x2"])
            P.add("act", lambda e: e.activation(out=th, in_=x2, func=AF.Tanh, scale=float(np.sqrt(2.0 / np.pi))), r=["x2"], w=["th"])
            P.add("pool", lambda e: e.tensor_scalar(out=th, in0=th, scalar1=1.0, scalar2=0.5, op0=ALU.add, op1=ALU.mult), r=["th"], w=["th"])
            P.add("pool", lambda e: e.tensor_tensor(out=gel, in0=th, in1=xg, op=ALU.mult), r=["th", "xg"], w=["gel"])
            if i == 0:
                pk2 = bank(2)
                P.add("pe", lambda e, pk2=pk2: e.matmul(pk2[:, 0:256], lhsT=w2b[0], rhs=gel, start=True, stop=True),
                      r=[("w2", 0), "gel"], w=["pk2"])
                P.add("act", lambda e, pk2=pk2: e.copy(out=kcmf, in_=pk2[:, 0:256]), r=[], w=["pk2", "kcmf"])
                qk_norm_rope(P, kcmf, kcm[:], gk[:, 0:1], onesb, R_sb, tcc, tsc, sq2, rr2, tmp2, [psbig[1]], eps_col, 1.0 / 128, "kcmf", 256)
            else:
                pk2 = bank(3)
                for ct in range(2):
                    P.add("pe", lambda e, ct=ct, pk2=pk2: e.matmul(pk2[:, ct * 128:(ct + 1) * 128], lhsT=gel[:, ct * 128:(ct + 1) * 128],
                                                                   rhs=w2b[1], start=(ct == 0), stop=True),
                          r=[("w2", 1), "gel"], w=["pk3"])
                P.add("act", lambda e, pk2=pk2: e.copy(out=vcm[:], in_=pk2[:, 0:256].rearrange("p (a b) -> p a b", a=2)),
                      r=[], w=["pk3", "vcm"])
        if dbg:
            P.dump("d_kcm", kcm[:], ["kcmf_bf"])
            P.dump("d_vcm", vcm[:], ["vcm"])
            P.dump("d_qbf", qbf[:, :, 0:256], ["w0_bf", "w1_bf"])
        P.barrier()

        roff[0] = 0
        pt = [rsb([128, 512], BF16) for _ in range(4)]
        G = [rsb([128, 12, 128], F32) for _ in range(2)]
        rden = [rsb([128, 512], F32) for _ in range(2)]
        fac = [rsb([128, 512], F32) for _ in range(2)]
        oft = [rsb([128, 512], F32) for _ in range(2)]
        ostage = [rsb([128, 4, 512], F32) for _ in range(2)]
        tmpU = rsb([64, 512], F32)
        impT = rsb([64, 128], F32)
        score = rsb([128, 64], F32)
        wk = rsb([128, 64], F32)
        m1 = rsb([128, 8], F32)
        m2 = rsb([128, 8], F32)
        negb = rsb([128, 64], F32)
        selT = [rsb([64, 128], BF16) for _ in range(2)]
        tiny = rsb([128, 1], F32)
        P.add("pool", lambda e: e.memset(tiny, 1e-30), w=["tiny"])
        cnt = dict(s=0, pt=0, o=0)
        qall = ["w0_bf", "w1_bf"]

        pend = []

        def flush(depth=0):
            while len(pend) > depth:
                pend.pop(0)()

        def branch(qt, kts, kfn, vfn, maskfn, extra=None, extra_key=None):
            oset = cnt["o"] % 2
            cnt["o"] += 1
            ob, db = 4 + 2 * oset, 5 + 2 * oset
            okey = ("oset", oset)
            nk = len(kts)
            for i, kt in enumerate(kts):
                sl = cnt["s"] % 3
                cnt["s"] += 1
                sps = bank(sl)
                skey = ("sps", sl)
                p_ = cnt["pt"] % 4
                cnt["pt"] += 1
                masks = maskfn(kt)
                kl, kr = kfn(kt)
                P.add("pe", lambda e, sps=sps, kl=kl, qt=qt, masks=masks: e.matmul(
                    sps, lhsT=kl, rhs=qbf[:, :, qt * 128:(qt + 1) * 128], start=True, stop=(len(masks) == 0)),
                    r=qall + kr, w=[skey])
                for mi, (ml, mr, mk) in enumerate(masks):
                    P.add("pe", lambda e, sps=sps, ml=ml, mr=mr, mi=mi, masks=masks: e.matmul(
                        sps, lhsT=ml, rhs=mr, start=False, stop=(mi == len(masks) - 1)), r=mk, w=[skey])
                P.add("act", lambda e, sps=sps, p_=p_: e.activation(out=pt[p_], in_=sps, func=AF.Exp, scale=SCALE),
                      r=[], w=[skey, ("pt", p_)])
                flush(1)
                vl, vr = vfn(kt)

                def pv(ob=ob, db=db, vl=vl, vr=vr, p_=p_, i=i, nk=nk, kt=kt, okey=okey):
                    P.add("pe", lambda e: e.matmul(bank(ob), lhsT=vl, rhs=pt[p_], start=(i == 0), stop=(i == nk - 1)),
                          r=[("pt", p_)] + vr, w=[okey])
                    P.add("pe", lambda e: e.matmul(bank(db), lhsT=onesb[:], rhs=pt[p_], start=(i == 0), stop=(i == nk - 1)),
                          r=[("pt", p_), "onesblk"], w=[okey])
                    if extra is not None:
                        extra(kt, i, nk, p_)
                pend.append(pv)
            return ob, db, okey, oset

        def combine(qt, br, ob, db, okey, oset, add_tiny):
            st = ostage[(qt // 4) % 2]
            stkey = ("ostage", (qt // 4) % 2)
            gb = qt % 2
            dst = st[:, :, (qt % 4) * 128:(qt % 4 + 1) * 128]
            if add_tiny:
                P.add("dve", lambda e, db=db, oset=oset: e.tensor_scalar(out=rden[oset], in0=bank(db), scalar1=tiny[:, 0:1], scalar2=None, op0=ALU.add),
                      r=["tiny"], w=[okey, ("rden", oset)])
                P.add("dve", lambda e, oset=oset: e.reciprocal(out=rden[oset], in_=rden[oset]), r=[], w=[("rden", oset)])
            else:
                P.add("dve", lambda e, db=db, oset=oset: e.reciprocal(out=rden[oset], in_=bank(db)), r=[], w=[okey, ("rden", oset)])
            P.add("pool", lambda e, oset=oset, gb=gb, br=br: e.tensor_tensor(
                out=fac[oset].rearrange("p (h q) -> p h q", h=4), in0=rden[oset].rearrange("p (h q) -> p h q", h=4),
                in1=G[gb].rearrange("p (h b) q -> p h b q", b=3)[:, :, br, :], op=ALU.mult),
                r=[("rden", oset), ("G", gb)], w=[("fac", oset)])
            if br == 0:
                P.add("dve", lambda e, ob=ob, oset=oset, dst=dst: e.tensor_tensor(
                    out=dst, in0=bank(ob).rearrange("p (h q) -> p h q", h=4), in1=fac[oset].rearrange("p (h q) -> p h q", h=4), op=ALU.mult),
                    r=[("fac", oset)], w=[okey, stkey])
            else:
                P.add("dve", lambda e, ob=ob, oset=oset: e.tensor_tensor(out=oft[oset], in0=bank(ob), in1=fac[oset], op=ALU.mult),
                      r=[("fac", oset)], w=[okey, ("oft", oset)])
                P.add("pool", lambda e, oset=oset, dst=dst: e.tensor_tensor(
                    out=dst, in0=dst, in1=oft[oset].rearrange("p (h q) -> p h q", h=4), op=ALU.add),
                    r=[("oft", oset)], w=[stkey])

        for qt in range(NT):
            gb = qt % 2
            tok = slice(qt * 128, (qt + 1) * 128)
            P.dma("sp", G[gb], gate_d[:, tok].partition_broadcast(128), w=[("G", gb)])
            P.add("act", lambda e, gb=gb: e.activation(out=G[gb], in_=G[gb], func=AF.Sigmoid), r=[], w=[("G", gb)])
            cts = [0] if qt < 16 else [0, 1]
            ub = bank(3)

            def cmask_fn(ct, qt=qt):
                idx = qt if ct == 0 else 32 + qt - 16
                return [(ident[:], cmask[:, idx, :].unsqueeze(1).to_broadcast([128, 4, 128]), ["ident", "cmask"])]

            def uextra(ct, i, nk, p_, ub=ub):
                P.add("pe", lambda e, ct=ct, i=i, nk=nk, p_=p_: e.matmul(ub[0:64, :], lhsT=ov[:, ct, :], rhs=pt[p_], start=(i == 0), stop=(i == nk - 1)),
                      r=[("pt", p_), "ov"], w=["ub"])

            ob, db, okey, oset = branch(qt, cts, lambda ct: (kcm[:, ct * 128:(ct + 1) * 128], ["kcmf_bf"]),
                                        lambda ct: (vcm[:, ct, :], ["vcm"]), cmask_fn, extra=uextra)
            flush()
            combine(qt, 0, ob, db, okey, oset, True)
            P.add("dve", lambda e, ub=ub, oset=oset: e.tensor_tensor(out=tmpU, in0=ub[0:64, :], in1=rden[oset][0:64, :], op=ALU.mult),
                  r=[("rden", oset)], w=["ub", "tmpU"])
            P.add("dve", lambda e: e.tensor_reduce(out=impT, in_=tmpU.rearrange("p (h q) -> p q h", h=4), axis=AX.X, op=ALU.add),
                  r=["tmpU"], w=["impT"])
            tb = bank(2)
            P.add("pe", lambda e, tb=tb: e.transpose(tb[:, 0:64], impT, identf[0:64, 0:64]), r=["impT", "identf"], w=[("sps", 2)])
            P.add("dve", lambda e, tb=tb, qt=qt: e.tensor_tensor(out=score, in0=tb[:, 0:64], in1=bonus[:, qt, :], op=ALU.add),
                  r=["bonus"], w=[("sps", 2), "score"])
            P.add("dve", lambda e: e.max(out=m1, in_=score), r=["score"], w=["m1"])
            P.add("dve", lambda e: e.match_replace(out=wk, in_to_replace=m1, in_values=score, imm_value=-3.0e38), r=["score", "m1"], w=["wk"])
            P.add("dve", lambda e: e.max(out=m2, in_=wk), r=["wk"], w=["m2"])
            P.add("dve", lambda e: e.tensor_scalar(out=negb, in0=score, scalar1=m2[:, 7:8], scalar2=None, op0=ALU.is_ge), r=["score", "m2"], w=["negb"])
            P.add("dve", lambda e: e.tensor_scalar(out=negb, in0=negb, scalar1=1.0, scalar2=-NEGB, op0=ALU.subtract, op1=ALU.mult),
                  r=["negb"], w=["negb"])
            P.add("pe", lambda e, tb=tb: e.transpose(tb[0:64, 128:256], negb, identf[:]), r=["negb", "identf"], w=[("sps", 2)])
            sT = selT[qt % 2]
            P.add("act", lambda e, tb=tb, sT=sT: e.copy(out=sT, in_=tb[0:64, 128:256]), r=[], w=[("sps", 2), ("selT", qt % 2)])
            if dbg and qt in (3, 20):
                P.dump(f"d_score{qt}", score, ["score"])
                P.dump(f"d_negb{qt}", negb, ["negb"])
                P.dump(f"d_selT{qt}", sT, [("selT", qt % 2)])

            def win_mask(kt, qt=qt):
                if kt == qt:
                    return [(ident[:], bdiag[:], ["ident", "bias"])]
                if kt == qt - 4:
                    return [(ident[:], bprev[:], ["ident", "bias"])]
                return []

            ob, db, okey, oset = branch(qt, [kt for kt in range(qt - 4, qt + 1) if kt >= 0],
                                        lambda kt: (kwbf[:, kt * 128:(kt + 1) * 128], ["w0_bf", "w1_bf"]),
                                        lambda kt: (vwb[:, kt, :], ["vw"]), win_mask)
            pend.append(lambda qt=qt, ob=ob, db=db, okey=okey, oset=oset: combine(qt, 2, ob, db, okey, oset, False))
            def sel_mask(kt, qt=qt, sT=sT):
                ms = [(Emat[:, kt, :], sT.unsqueeze(1).to_broadcast([64, 4, 128]), ["Emat", ("selT", qt % 2)])]
                if kt == qt:
                    ms.append((ident[:], bdiag[:], ["ident", "bias"]))
                return ms

            ob, db, okey, oset = branch(qt, list(range(qt + 1)), lambda kt: (ksbf[:, kt * 128:(kt + 1) * 128], ["w0_bf", "w1_bf"]),
                                        lambda kt: (vsb[:, kt, :], ["vs"]), sel_mask)

            def fin(qt=qt, ob=ob, db=db, okey=okey, oset=oset):
                combine(qt, 1, ob, db, okey, oset, False)
                if qt % 4 == 3:
                    g4 = qt // 4
                    P.dma("sp", out_d.rearrange("(h d) t -> d h t", h=4)[:, :, g4 * 512:(g4 + 1) * 512], ostage[g4 % 2],
                          r=[("ostage", g4 % 2)], w=[("out", g4)])
            pend.append(fin)
        flush()
        P.finish([("out", g4) for g4 in range(NT // 4)])
        P.emit()
    return nc


def run_nsaB(projT, q_norm, k_norm, cmp_pe, cmp_w1, cmp_w2, trace=False, dbg=False, cores=None):
    S = SEQ
    NT = NSA_NT
    nc = get_prog(("nsaB", dbg), lambda: build_nsaB(dbg))
    cosf, sinf = rope_tables(128, np.arange(S), 1)
    cend = np.arange(256) * 16 + 31
    cosc, sinc = rope_tables(128, cend, 1)
    Rm = rope_matrix(128, 1)
    ident = np.eye(128, dtype=np.float32)
    bdiag, bprev = mask_biases()
    cm, bonus, E, ov = nsa_consts()
    gq = q_norm.astype(np.float32).reshape(128, 1)
    gk = np.ascontiguousarray(k_norm.astype(np.float32).T)
    peT = np.ascontiguousarray(cmp_pe.transpose(0, 2, 1))
    w1 = np.ascontiguousarray(cmp_w1.reshape(2, 32, 128, 128).transpose(0, 2, 1, 3))
    w2 = np.ascontiguousarray(cmp_w2)
    in_maps = []
    clist = list(range(NCORES)) if cores is None else cores
    for c in clist:
        b, g = c // 4, c % 4
        ts = slice(b * S, (b + 1) * S)

        def rows(base):
            return np.ascontiguousarray(projT[base + g * 128:base + (g + 1) * 128, ts])

        def tm(base):
            return np.ascontiguousarray(projT[base + g * 128:base + (g + 1) * 128, ts].T.reshape(NT, 128, 128).transpose(1, 0, 2))

        q = np.ascontiguousarray(projT[g * 512:(g + 1) * 512, ts].reshape(4, 128, S))
        gate = np.ascontiguousarray(projT[5120 + g * 12:5120 + (g + 1) * 12, ts])
        in_maps.append(dict(q=q, kc=rows(2048), vc=rows(2560), ks=rows(3072), vs=tm(3584), kw=rows(4096), vw=tm(4608), gate=gate,
                            gq=gq, gk=gk, peT=peT, w1=w1, w2=w2, cosf=cosf, sinf=sinf, cosc=cosc, sinc=sinc, Rm=Rm, ident=ident,
                            bdiag=bdiag, bprev=bprev, cmask=cm, bonus=bonus, Emat=E, ov=ov))
    res = run_bass_kernel_spmd(nc, in_maps, core_ids=list(range(len(clist))), trace=trace)
    if trace:
        print("nsaB exec_time_ns", res.exec_time_ns)
    if dbg:
        return res.results
    oT = np.zeros((D_MODEL, BATCH * S), np.float32)
    for i, c in enumerate(clist):
        b, g = c // 4, c % 4
        oT[g * 512:(g + 1) * 512, b * S:(b + 1) * S] = res.results[i]["oT"]
    return oT


def kernel(x, norm_mix, norm_mlp, mlp_w_up, mlp_w_down,
           nsa_w_in, nsa_w_out, nsa_q_norm, nsa_k_norm, nsa_cmp_pe, nsa_cmp_w1, nsa_cmp_w2,
           gla_w_in, gla_w_gate_up, gla_b_gate, gla_o_norm, gla_w_out,
           swa_w_in, swa_w_out, swa_q_norm, swa_k_norm, swa_sinks):
    f = lambda a: np.asarray(a, dtype=np.float32)
    x = f(x)
    xT = np.ascontiguousarray(x.reshape(BATCH * SEQ, D_MODEL).T)
    idx = {0: 0, 1: 0, 2: 0}
    w_ins = []
    for i in range(DEPTH):
        kind = i % 3
        w_ins.append(f((nsa_w_in, gla_w_in, swa_w_in)[kind][idx[kind]]))
        idx[kind] += 1
    idx = {0: 0, 1: 0, 2: 0}
    projT = run_phaseA(xT, f(norm_mix[0]), w_ins[0])
    for i in range(DEPTH):
        kind = i % 3
        j = idx[kind]
        idx[kind] += 1
        if kind == 0:
            oT = run_nsaB(projT, f(nsa_q_norm[j]), f(nsa_k_norm[j]), f(nsa_cmp_pe[j]), f(nsa_cmp_w1[j]), f(nsa_cmp_w2[j]))
            w_out = f(nsa_w_out[j])
        elif kind == 1:
            oT = run_glaB(projT, f(gla_w_gate_up[j]), f(gla_b_gate[j]), f(gla_o_norm[j]))
            w_out = f(gla_w_out[j])
        else:
            oT = run_swaB(projT, f(swa_q_norm[j]), f(swa_k_norm[j]), f(swa_sinks[j]))
            w_out = f(swa_w_out[j])
        if i + 1 < DEPTH:
            xT, projT = run_phaseC(xT, oT, w_out, f(norm_mlp[i]), f(mlp_w_up[i]), f(mlp_w_down[i]),
                                   gain2=f(norm_mix[i + 1]), w_in2=w_ins[i + 1])
        else:
            xT = run_phaseC(xT, oT, w_out, f(norm_mlp[i]), f(mlp_w_up[i]), f(mlp_w_down[i]))
    return np.ascontiguousarray(xT.T).reshape(BATCH, SEQ, D_MODEL).astype(np.float32)
```
